# Optimizing a Trainium2 kernel written in Bass

```python
import jax, jax.numpy as jnp
from jax import lax
import numpy as np

D_MODEL = 1024
BATCH = 16
SEQ = 2048
DEPTH = 4
DEC_BATCH = 128
DEC_SEQ = 1
PAST_LEN = 8192
PAGE_SIZE = 128

N_A_LAYERS = DEPTH // 2
N_B_LAYERS = DEPTH - N_A_LAYERS
EXPAND = 2
D_INNER = EXPAND * D_MODEL
A_HEAD = 64
A_HEADS = D_INNER // A_HEAD
DECAY_LORA = 64
ICLR_LORA = 64
VRES_LORA = 32
GN_EPS = A_HEAD * 1e-5
B_HEADS = 16
QK_NOPE = 128
QK_ROPE = 64
V_HEAD = D_INNER // B_HEADS
KV_LORA = 256
Q_LORA = 384
ROPE_THETA = 10000.0
Q_BLOCK = 128
NORM_EPS = 1e-6
ATTN_SCALE = (QK_NOPE + QK_ROPE) ** -0.5

kernel_name = 'yoco_rwkv7_mla_decoder_step'

F32 = jnp.float32


def rmsnorm(x, w):
    x32 = x.astype(F32)
    y = x32 * lax.rsqrt(jnp.mean(x32 * x32, axis=-1, keepdims=True) + NORM_EPS)
    return (y * w.astype(F32)).astype(x.dtype)


def rope(x, pos):
    half = x.shape[-1] // 2
    inv_freq = ROPE_THETA ** (-jnp.arange(half, dtype=F32) / half)
    ang = pos.astype(F32)[:, None] * inv_freq[None, :]
    shape = (1, pos.shape[0]) + (1,) * (x.ndim - 3) + (half,)
    cos = jnp.cos(ang).reshape(shape)
    sin = jnp.sin(ang).reshape(shape)
    x32 = x.astype(F32)
    x1, x2 = x32[..., :half], x32[..., half:]
    return jnp.concatenate([x1 * cos - x2 * sin, x2 * cos + x1 * sin], axis=-1).astype(x.dtype)


def _wkv_step(S, inp):
    r, w, k, v, a, b = inp
    sa = jnp.einsum('bhvk,bhk->bhv', S, a)
    S = S * w[:, :, None, :] + sa[..., None] * b[:, :, None, :] + v[..., None] * k[:, :, None, :]
    return S, jnp.einsum('bhvk,bhk->bhv', S, r)


def rwkv7_mix(h, h_prev, S0, v_first, vres, mix, w_in, w0, w1, w2, a0, a1, a2,
              k_k, k_a, r_k, lnx_w, lnx_b, w_out):
    B, T, _ = h.shape
    shifted = jnp.concatenate([h_prev[:, None, :].astype(h.dtype), h[:, :-1, :]], axis=1)
    dx = shifted - h
    mixed = h[None] + dx[None] * mix[:4, None, None, :]
    r, k, v, z = jnp.einsum('pbtd,pde->pbte', mixed, w_in)
    xw = h + dx * mix[4]
    xa = h + dx * mix[5]
    w_log = -jax.nn.softplus(-(w0 + jnp.tanh(xw @ w1) @ w2)) - 0.5
    decay = jnp.exp(-jnp.exp(w_log.astype(F32)))
    a = jax.nn.sigmoid(a0 + (xa @ a1) @ a2)
    if vres is not None:
        v0, v1, v2 = vres
        v = v + (v_first - v) * jax.nn.sigmoid(v0 + (mixed[2] @ v1) @ v2)
    heads = lambda t: t.reshape(B, T, A_HEADS, A_HEAD)
    kk = heads(k * k_k).astype(F32)
    kk = kk / jnp.maximum(jnp.sqrt(jnp.sum(kk * kk, axis=-1, keepdims=True)), 1e-12)
    k = k * (1 + (a - 1) * k_a)
    rh, kh, vh, ah = heads(r), heads(k), heads(v), heads(a)
    seq = lambda t: jnp.moveaxis(t.astype(F32), 1, 0)
    xs = (seq(rh), seq(heads(decay)), seq(kh), seq(vh), seq(-kk), seq(kk * ah.astype(F32)))
    S_fin, y = lax.scan(_wkv_step, S0.astype(F32), xs)
    y = jnp.moveaxis(y, 0, 1)
    mu = jnp.mean(y, axis=-1, keepdims=True)
    var = jnp.mean(jnp.square(y - mu), axis=-1, keepdims=True)
    yn = ((y - mu) * lax.rsqrt(var + GN_EPS)).reshape(B, T, D_INNER)
    yn = (yn * lnx_w.astype(F32) + lnx_b.astype(F32)).astype(h.dtype)
    bonus = (jnp.sum(rh * kh * r_k, axis=-1, keepdims=True) * vh).reshape(B, T, D_INNER)
    out = ((yn + bonus) * jax.nn.silu(z)) @ w_out
    return out, v, S_fin.astype(S0.dtype), h[:, -1, :]


def mla_shared_kv(x, pos, kv_in_norm_w, w_dkv, kv_norm_w):
    kv = rmsnorm(x, kv_in_norm_w) @ w_dkv
    ckv = rmsnorm(kv[..., :KV_LORA], kv_norm_w)
    kpe = rope(kv[..., KV_LORA:], pos)
    return ckv, kpe


def mla_query(h, pos, w_in, q_norm_w, w_uq):
    B, T, _ = h.shape
    proj = h @ w_in
    cq = rmsnorm(proj[..., :Q_LORA], q_norm_w)
    z = proj[..., Q_LORA:]
    q = (cq @ w_uq).reshape(B, T, B_HEADS, QK_NOPE + QK_ROPE)
    return q[..., :QK_NOPE], rope(q[..., QK_NOPE:], pos), z


def mla_prompt_attn(q_nope, q_pe, k_nope, kpe, v):
    B, S = q_nope.shape[:2]
    nblk = S // Q_BLOCK
    qn = jnp.moveaxis(q_nope.reshape(B, nblk, Q_BLOCK, B_HEADS, QK_NOPE), 1, 0)
    qp = jnp.moveaxis(q_pe.reshape(B, nblk, Q_BLOCK, B_HEADS, QK_ROPE), 1, 0)
    kpos = jnp.arange(S)

    def block(args):
        i, qn_b, qp_b = args
        s = (jnp.einsum('bqhd,bshd->bhqs', qn_b, k_nope)
             + jnp.einsum('bqhr,bsr->bhqs', qp_b, kpe)).astype(F32) * ATTN_SCALE
        qpos = i * Q_BLOCK + jnp.arange(Q_BLOCK)
        s = jnp.where(kpos[None, :] <= qpos[:, None], s, -jnp.inf)
        p = jax.nn.softmax(s, axis=-1).astype(v.dtype)
        return jnp.einsum('bhqs,bshv->bqhv', p, v)

    o = lax.map(block, (jnp.arange(nblk), qn, qp))
    return jnp.moveaxis(o, 0, 1).reshape(B, S, B_HEADS * V_HEAD)


def mla_sample_attn(q_nope, q_pe, ckv_past, kpe_past, ckv_new, kpe_new, w_uk, w_uv):
    B, T = q_nope.shape[:2]
    P = ckv_past.shape[1]
    q_lat = jnp.einsum('bthn,chn->bthc', q_nope, w_uk)
    s_past = (jnp.einsum('bthc,bsc->bhts', q_lat, ckv_past)
              + jnp.einsum('bthr,bsr->bhts', q_pe, kpe_past)).astype(F32) * ATTN_SCALE
    s_new = (jnp.einsum('bthc,bsc->bhts', q_lat, ckv_new)
             + jnp.einsum('bthr,bsr->bhts', q_pe, kpe_new)).astype(F32) * ATTN_SCALE
    causal = jnp.arange(T)[None, :] <= jnp.arange(T)[:, None]
    s_new = jnp.where(causal, s_new, -jnp.inf)
    p = jax.nn.softmax(jnp.concatenate([s_past, s_new], axis=-1), axis=-1).astype(ckv_past.dtype)
    o_lat = (jnp.einsum('bhts,bsc->bthc', p[..., :P], ckv_past)
             + jnp.einsum('bhts,bsc->bthc', p[..., P:], ckv_new))
    o = jnp.einsum('bthc,chv->bthv', o_lat, w_uv)
    return o.reshape(B, T, B_HEADS * V_HEAD)


def setup_inputs(seed: int = 0) -> dict:
    key = jax.random.key(seed)
    ks = iter(jax.random.split(key, 48))
    nrm = lambda shape, scale: jax.random.normal(next(ks), shape, F32) * scale
    n_pages = PAST_LEN // PAGE_SIZE
    n_used = DEC_BATCH * n_pages
    n_pool = n_used + (n_used + 3) // 4
    page_table = jax.random.permutation(next(ks), n_pool)[:n_used].astype(jnp.int32).reshape(DEC_BATCH, n_pages)
    NA, NB = N_A_LAYERS, N_B_LAYERS
    return {
        'x_prompt': nrm((BATCH, SEQ, D_MODEL), 1.0),
        'x_sample': nrm((DEC_BATCH, DEC_SEQ, D_MODEL), 1.0),
        'state_wkv': nrm((NA, DEC_BATCH, A_HEADS, A_HEAD, A_HEAD), 0.5),
        'state_shift': nrm((NA, DEC_BATCH, D_MODEL), 1.0),
        'cache_ckv': nrm((n_pool, PAGE_SIZE, KV_LORA), 1.0),
        'cache_kpe': nrm((n_pool, PAGE_SIZE, QK_ROPE), 1.0),
        'page_table': page_table,
        'a_norm_w': 1.0 + nrm((NA, D_MODEL), 0.02),
        'a_mix': jax.random.uniform(next(ks), (NA, 6, D_MODEL), F32),
        'a_w_in': nrm((NA, 4, D_MODEL, D_INNER), D_MODEL ** -0.5),
        'a_w0': jax.random.uniform(next(ks), (NA, D_INNER), F32, -6.0, -1.0),
        'a_w1': nrm((NA, D_MODEL, DECAY_LORA), D_MODEL ** -0.5),
        'a_w2': nrm((NA, DECAY_LORA, D_INNER), 0.1 * DECAY_LORA ** -0.5),
        'a_a0': nrm((NA, D_INNER), 0.1),
        'a_a1': nrm((NA, D_MODEL, ICLR_LORA), D_MODEL ** -0.5),
        'a_a2': nrm((NA, ICLR_LORA, D_INNER), 0.5 * ICLR_LORA ** -0.5),
        'a_v0': nrm((NA - 1, D_INNER), 0.1),
        'a_v1': nrm((NA - 1, D_MODEL, VRES_LORA), D_MODEL ** -0.5),
        'a_v2': nrm((NA - 1, VRES_LORA, D_INNER), 0.5 * VRES_LORA ** -0.5),
        'a_k_k': 0.85 + nrm((NA, D_INNER), 0.02),
        'a_k_a': 1.0 + nrm((NA, D_INNER), 0.02),
        'a_r_k': nrm((NA, A_HEADS, A_HEAD), 0.1),
        'a_lnx_w': 1.0 + nrm((NA, D_INNER), 0.02),
        'a_lnx_b': nrm((NA, D_INNER), 0.01),
        'a_w_out': nrm((NA, D_INNER, D_MODEL), D_INNER ** -0.5),
        'kv_in_norm_w': 1.0 + nrm((D_MODEL,), 0.02),
        'w_dkv': nrm((D_MODEL, KV_LORA + QK_ROPE), D_MODEL ** -0.5),
        'kv_norm_w': 1.0 + nrm((KV_LORA,), 0.02),
        'w_uk': nrm((KV_LORA, B_HEADS, QK_NOPE), KV_LORA ** -0.5),
        'w_uv': nrm((KV_LORA, B_HEADS, V_HEAD), KV_LORA ** -0.5),
        'b_norm_w': 1.0 + nrm((NB, D_MODEL), 0.02),
        'b_w_in': nrm((NB, D_MODEL, Q_LORA + D_INNER), D_MODEL ** -0.5),
        'b_q_norm_w': 1.0 + nrm((NB, Q_LORA), 0.02),
        'b_w_uq': nrm((NB, Q_LORA, B_HEADS * (QK_NOPE + QK_ROPE)), Q_LORA ** -0.5),
        'b_w_out': nrm((NB, D_INNER, D_MODEL), D_INNER ** -0.5),
        'final_norm_w': 1.0 + nrm((D_MODEL,), 0.02),
    }


def reference(x_prompt, x_sample, state_wkv, state_shift, cache_ckv, cache_kpe, page_table,
              a_norm_w, a_mix, a_w_in, a_w0, a_w1, a_w2, a_a0, a_a1, a_a2, a_v0, a_v1, a_v2,
              a_k_k, a_k_a, a_r_k, a_lnx_w, a_lnx_b, a_w_out,
              kv_in_norm_w, w_dkv, kv_norm_w, w_uk, w_uv,
              b_norm_w, b_w_in, b_q_norm_w, b_w_uq, b_w_out, final_norm_w):
    Bp, Sp, _ = x_prompt.shape
    Bs, Ts, _ = x_sample.shape
    past = page_table.shape[1] * cache_ckv.shape[1]
    pos_p = jnp.arange(Sp, dtype=jnp.int32)
    pos_s = past + jnp.arange(Ts, dtype=jnp.int32)
    xp, xs = x_prompt, x_sample
    shift0 = jnp.zeros((Bp, D_MODEL), x_prompt.dtype)
    wkv0 = jnp.zeros((Bp, A_HEADS, A_HEAD, A_HEAD), state_wkv.dtype)
    vf_p = vf_s = None
    wkv_p, sh_p, wkv_s, sh_s = [], [], [], []
    for l in range(N_A_LAYERS):
        vres = None if l == 0 else (a_v0[l - 1], a_v1[l - 1], a_v2[l - 1])
        prm = (a_mix[l], a_w_in[l], a_w0[l], a_w1[l], a_w2[l], a_a0[l], a_a1[l], a_a2[l],
               a_k_k[l], a_k_a[l], a_r_k[l], a_lnx_w[l], a_lnx_b[l], a_w_out[l])
        o, v, S, last = rwkv7_mix(rmsnorm(xp, a_norm_w[l]), shift0, wkv0, vf_p, vres, *prm)
        xp = xp + o
        wkv_p.append(S)
        sh_p.append(last)
        if l == 0:
            vf_p = v
        o, v, S, last = rwkv7_mix(rmsnorm(xs, a_norm_w[l]), state_shift[l], state_wkv[l], vf_s, vres, *prm)
        xs = xs + o
        wkv_s.append(S)
        sh_s.append(last)
        if l == 0:
            vf_s = v
    ckv_p, kpe_p = mla_shared_kv(xp, pos_p, kv_in_norm_w, w_dkv, kv_norm_w)
    ckv_s, kpe_s = mla_shared_kv(xs, pos_s, kv_in_norm_w, w_dkv, kv_norm_w)
    k_nope_p = jnp.einsum('bsc,chn->bshn', ckv_p, w_uk)
    v_p = jnp.einsum('bsc,chv->bshv', ckv_p, w_uv)
    ckv_past = cache_ckv[page_table].reshape(Bs, past, KV_LORA)
    kpe_past = cache_kpe[page_table].reshape(Bs, past, QK_ROPE)
    for l in range(N_B_LAYERS):
        qn, qr, z = mla_query(rmsnorm(xp, b_norm_w[l]), pos_p, b_w_in[l], b_q_norm_w[l], b_w_uq[l])
        o = mla_prompt_attn(qn, qr, k_nope_p, kpe_p, v_p)
        xp = xp + (o * jax.nn.silu(z)) @ b_w_out[l]
        qn, qr, z = mla_query(rmsnorm(xs, b_norm_w[l]), pos_s, b_w_in[l], b_q_norm_w[l], b_w_uq[l])
        o = mla_sample_attn(qn, qr, ckv_past, kpe_past, ckv_s, kpe_s, w_uk, w_uv)
        xs = xs + (o * jax.nn.silu(z)) @ b_w_out[l]
    y_prompt = rmsnorm(xp, final_norm_w)
    y_sample = rmsnorm(xs, final_norm_w)
    return (y_prompt, y_sample, jnp.stack(wkv_p), jnp.stack(sh_p), ckv_p, kpe_p,
            jnp.stack(wkv_s), jnp.stack(sh_s), ckv_s, kpe_s)
```

```python
import contextlib
import numpy as np
import concourse.bass as bass
import concourse.mybir as mybir
from concourse.bass_utils import run_bass_kernel_spmd

F32 = mybir.dt.float32
BF16 = mybir.dt.bfloat16
I32 = mybir.dt.int32
ALU = mybir.AluOpType
AF = mybir.ActivationFunctionType
AX = mybir.AxisListType

D = 1024
KC = 8
DI = 2048
FC = 16
NA = 2
NB = 2
C = 64
TT = 256
NCH = TT // C
KV_LORA = 256
QK_ROPE = 64
QK_NOPE = 128
Q_LORA = 384
BH = 16
NORM_EPS = 1e-6
GN_EPS = 64 * 1e-5
C0 = float(np.exp(-0.5))
ATTN_SCALE = float((QK_NOPE + QK_ROPE) ** -0.5)
PAGE = 128

COMPUTE = ("pe", "dve", "act", "pool")
NHW = 32
NSW = 16
NDMASEM = NHW + NSW


class T:
    def __init__(self, ap, key):
        self.ap = ap
        self.key = key

    def __getitem__(self, idx):
        return T(self.ap[idx], self.key)

    def bc(self, shape):
        return T(self.ap.broadcast_to(list(shape)), self.key)

    def re(self, pat, **kw):
        return T(self.ap.rearrange(pat, **kw), self.key)

    def k(self, key):
        return T(self.ap, key)

    def bf(self):
        return T(self.ap.bitcast(BF16), self.key)


def _keys(*ts):
    out = []
    for t in ts:
        if isinstance(t, T):
            out.append(t.key)
    return out


class Prog:
    def __init__(self, nc):
        self.nc = nc
        self.ops = []
        self.stack = contextlib.ExitStack()
        self.sbuf = self.stack.enter_context(nc.sbuf_tensor("sbpool", [128, 50 * 1024], F32))
        self.psum = self.stack.enter_context(nc.psum_tensor("pspool", [128, 8, 512], F32))
        self.off = 0
        self.names = set()

    def alloc(self, name, shape, dt=F32):
        assert name not in self.names, name
        self.names.add(name)
        n = int(np.prod(shape[1:]))
        words = n if dt in (F32, I32) else (n + 1) // 2
        words = (words + 7) // 8 * 8
        assert self.off + words <= 50 * 1024, ("sbuf overflow", name, self.off, words)
        ap = self.sbuf[0:shape[0], self.off:self.off + words]
        self.off += words
        if dt != F32:
            ap = ap.bitcast(dt)
        ap = ap[:, 0:n]
        if len(shape) == 3:
            ap = ap.rearrange("p (a b) -> p a b", a=shape[1], b=shape[2])
        elif len(shape) == 4:
            ap = ap.rearrange("p (a b c) -> p a b c", a=shape[1], b=shape[2], c=shape[3])
        return T(ap, name)

    def bank(self, i, key=None):
        return T(self.psum[:, i, :], "b%d" % i)

    def op(self, eng, fn, reads=(), writes=(), dma=False):
        self.ops.append(dict(eng=eng, fn=fn, reads=tuple(reads), writes=tuple(writes), dma=dma))

    def mark(self, name):
        self.marks = getattr(self, "marks", [])
        self.marks.append((name, len(self.ops)))

    def barrier(self):
        self.ops.append(dict(eng=None, barrier=True, reads=(), writes=(), dma=False))

    def dma(self, q, out, in_, **kw):
        o = out.ap if isinstance(out, T) else out
        i = in_.ap if isinstance(in_, T) else in_
        self.op(q, lambda e: e.dma_start(out=o, in_=i, **kw), _keys(in_), _keys(out), dma=True)

    def mm(self, out, lhsT, rhs, start=True, stop=True):
        self.op("pe", lambda e: e.matmul(out.ap, lhsT.ap, rhs.ap, start=start, stop=stop),
                _keys(lhsT, rhs), _keys(out))

    def tr(self, out, in_, ident):
        self.op("pe", lambda e: e.transpose(out.ap, in_.ap, ident.ap), _keys(in_, ident), _keys(out))

    def tt(self, eng, out, in0, in1, op):
        self.op(eng, lambda e: e.tensor_tensor(out.ap, in0.ap, in1.ap, op), _keys(in0, in1), _keys(out))

    def ts(self, eng, out, in0, s1, s2=None, op0=ALU.mult, op1=None):
        a1 = s1.ap if isinstance(s1, T) else s1
        a2 = s2.ap if isinstance(s2, T) else s2
        if op1 is None:
            self.op(eng, lambda e: e.tensor_scalar(out.ap, in0.ap, a1, None, op0), _keys(in0, s1), _keys(out))
        else:
            self.op(eng, lambda e: e.tensor_scalar(out.ap, in0.ap, a1, a2, op0, op1),
                    _keys(in0, s1, s2), _keys(out))

    def stt(self, out, in0, scalar, in1, op0, op1):
        sc = scalar.ap if isinstance(scalar, T) else scalar
        self.op("dve", lambda e: e.scalar_tensor_tensor(out.ap, in0.ap, sc, in1.ap, op0, op1),
                _keys(in0, scalar, in1), _keys(out))

    def act(self, out, in_, func, bias=None, scale=None):
        kw = {}
        if bias is not None:
            kw["bias"] = bias.ap if isinstance(bias, T) else bias
        if scale is not None:
            kw["scale"] = scale.ap if isinstance(scale, T) else scale
        self.op("act", lambda e: e.activation(out=out.ap, in_=in_.ap, func=func, **kw),
                _keys(in_, bias, scale), _keys(out))

    def gather(self, out, in_, idx):
        self.op("pool", lambda e: e.indirect_dma_start(out=out.ap, out_offset=None, in_=in_.ap,
                in_offset=bass.IndirectOffsetOnAxis(ap=idx.ap, axis=0)), _keys(in_, idx), _keys(out), dma=True)

    def act_acc(self, out, in_, func, accum):
        self.op("act", lambda e: e.activation(out=out.ap, in_=in_.ap, func=func, accum_out=accum.ap),
                _keys(in_), _keys(out, accum))

    def copy(self, eng, out, in_):
        if eng == "act":
            self.op("act", lambda e: e.activation(out=out.ap, in_=in_.ap, func=AF.Copy), _keys(in_), _keys(out))
        else:
            self.op(eng, lambda e: e.tensor_copy(out.ap, in_.ap), _keys(in_), _keys(out))

    def memset(self, eng, out, val):
        self.op(eng, lambda e: e.memset(out.ap, val), (), _keys(out))

    def red(self, out, in_, op=ALU.add, axis=AX.X):
        self.op("dve", lambda e: e.tensor_reduce(out.ap, in_.ap, axis, op), _keys(in_), _keys(out))

    def recip(self, out, in_):
        self.op("dve", lambda e: e.reciprocal(out.ap, in_.ap), _keys(in_), _keys(out))

    def scan(self, out, d0, d1, init, op0, op1):
        self.op("dve", lambda e: e.tensor_tensor_scan(out.ap, d0.ap, d1.ap, init, op0, op1),
                _keys(d0, d1), _keys(out))

    def emit(self):
        nc = self.nc
        import os as _os
        mo = int(_os.environ.get("MAXOPS", "0"))
        if mo:
            self.ops = self.ops[:mo]
        ops = self.ops
        last_w = {}
        readers = {}
        last_on_eng = {}
        dma_since = []
        for i, o in enumerate(ops):
            if o.get("barrier"):
                o["deps"] = set(last_on_eng.values()) | set(dma_since)
                o["bar_deps"] = set(o["deps"])
                dma_since = []
                last_w.clear()
                readers.clear()
                last_on_eng = {}
                o["is_bar"] = True
                continue
            deps = set()
            E = o["eng"]
            for k in o["reads"]:
                w = last_w.get(k)
                if w is not None:
                    wo = ops[w]
                    if wo["dma"] or wo["eng"] != E or E != "pe":
                        deps.add(w)
                if len(k) == 2 and k[0] == "b" and k[1].isdigit():
                    for (re_, rdma), r in readers.get(k, {}).items():
                        if re_ != E:
                            deps.add(r)
            for k in o["writes"]:
                w = last_w.get(k)
                if w is not None:
                    wo = ops[w]
                    if wo["dma"] or wo["eng"] != E or E != "pe":
                        deps.add(w)
                for (re_, rdma), r in readers.get(k, {}).items():
                    if rdma or re_ != E or E != "pe":
                        deps.add(r)
            for k in o["reads"]:
                readers.setdefault(k, {})[(E, o["dma"])] = i
            for k in o["writes"]:
                last_w[k] = i
                readers[k] = {}
            deps.discard(i)
            o["deps"] = deps
            if o["dma"]:
                dma_since.append(i)
            else:
                last_on_eng[E] = i
        pending = {}
        for i, o in enumerate(ops):
            if o.get("barrier"):
                for e in ("pe", "dve", "act", "pool", "sp"):
                    pending.setdefault(e, set()).update(o["bar_deps"])
                continue
            E = o["eng"]
            if pending.get(E):
                o["deps"] |= pending[E]
                pending[E] = set()
        for o in ops:
            o["flag"] = False
        for o in ops:
            if o.get("barrier"):
                continue
            for d in o["deps"]:
                if not ops[d]["dma"]:
                    ops[d]["flag"] = True
        cnt = {e: 0 for e in COMPUTE}
        ndma = 0
        nq = {"hw": 0, "sw": 0}
        for o in ops:
            if o.get("barrier"):
                continue
            if o["dma"]:
                if o["eng"] == "pool":
                    o["sem"] = NHW + nq["sw"] % NSW
                    o["val"] = 16 * (nq["sw"] // NSW + 1)
                    nq["sw"] += 1
                else:
                    o["sem"] = nq["hw"] % NHW
                    o["val"] = 16 * (nq["hw"] // NHW + 1)
                    nq["hw"] += 1
                ndma += 1
            elif o["flag"]:
                cnt[o["eng"]] += 1
                o["val"] = cnt[o["eng"]]
        sems = {e: self.stack.enter_context(nc.semaphore("s_" + e)) for e in COMPUTE}
        dsems = [self.stack.enter_context(nc.semaphore("s_dma%d" % j)) for j in range(NDMASEM)]
        per_eng = {e: [] for e in ("pe", "dve", "act", "pool", "sp")}
        for i, o in enumerate(ops):
            if o.get("barrier"):
                continue
            per_eng[o["eng"]].append(i)
        final_dma = {}
        for o in ops:
            if o.get("barrier"):
                continue
            if o["dma"]:
                final_dma[o["sem"]] = max(final_dma.get(o["sem"], 0), o["val"])
        n_wait = [0]

        def emit_engine(ename, e):
            waited = {}
            for i in per_eng[ename]:
                o = ops[i]
                need = {}
                for d in o["deps"]:
                    do = ops[d]
                    key = ("d", do["sem"]) if do["dma"] else ("c", do["eng"])
                    need[key] = max(need.get(key, 0), do["val"])
                if o["dma"] and o["val"] > 16:
                    key = ("d", o["sem"])
                    need[key] = max(need.get(key, 0), o["val"] - 16)
                for key, v in need.items():
                    if waited.get(key, 0) >= v:
                        continue
                    waited[key] = v
                    s = dsems[key[1]] if key[0] == "d" else sems[key[1]]
                    e.wait_ge(s, v)
                    n_wait[0] += 1
                ins = o["fn"](e)
                if o["dma"]:
                    ins.then_inc(dsems[o["sem"]], 16)
                elif o["flag"]:
                    ins.then_inc(sems[ename], 1)
            if ename == "sp":
                for sidx, v in final_dma.items():
                    e.wait_ge(dsems[sidx], v)

        with nc.Block() as block:
            @block.sync
            def _(e):
                emit_engine("sp", e)

            @block.tensor
            def _(e):
                emit_engine("pe", e)

            @block.vector
            def _(e):
                emit_engine("dve", e)

            @block.scalar
            def _(e):
                emit_engine("act", e)

            @block.gpsimd
            def _(e):
                emit_engine("pool", e)
        self.stats = dict(n_ops=len(ops), n_wait=n_wait[0], cnt=cnt, ndma=ndma,
                          per_eng={k: len(v) for k, v in per_eng.items()})
        self.stack.close()


PA_NW = 0
PA_MIX = 8
PA_W0 = 56
PA_A0 = 72
PA_V0 = 88
PA_KK = 104
PA_KA = 120
PA_RK = 136
PA_LW = 152
PA_LB = 168
NPAR_A = 184


class Cfg:
    def __init__(self, nseq=2, seq=2048, ns=16, npages=64, npool=10240, phases=("rwkv", "mla")):
        self.nseq, self.seq, self.ns, self.npages, self.npool = nseq, seq, ns, npages, npool
        self.ptiles = nseq * seq // TT
        self.tps = seq // TT
        self.stiles = ns // NCH
        self.ntiles = self.ptiles + self.stiles
        self.nstate = nseq + ns
        self.phases = phases


PB_KVN = 0
PB_BN = 8
PB_FN = 24
PB_QN = 32
NPAR_B = 40
QW = BH * (QK_NOPE + QK_ROPE)
ZW = Q_LORA + DI


def mla_phase(P, nc, cfg, env):
    din, dout, dscr = env["din"], env["dout"], env["dscr"]
    xa, xb = env["xa"], env["xb"]
    ident_f, ident_b, ones_b = env["ident_f"], env["ident_b"], env["ones_b"]
    NS, SEQ, NSEQ = cfg.ns, cfg.seq, cfg.nseq
    NPG = cfg.npages
    par_b_d = din("par_b", [128, NPAR_B])
    kvnw_d = din("kvnw_bc", [128, KV_LORA])
    rope_tm_d = din("rope_tm", [SEQ, 64])
    rope_fm_d = din("rope_fm", [2, 64, SEQ])
    rope_s_tm_d = din("rope_s_tm", [16, 64])
    rope_s_fm_d = din("rope_s_fm", [64, 2])
    cm_d = din("cmask", [128, 2 * TT])
    iota_d = din("iota_p", [128, 1])
    wdkv_d = din("wdkv_f", [128, KC * 320])
    wuk_d = din("wuk_f", [128, 2 * BH * 128])
    wukT_d = din("wukT_f", [128, BH * KV_LORA])
    wuv_d = din("wuv_f", [128, 2 * BH * 128])
    win_d = din("bwin_f", [NB, 128, KC * ZW])
    wuq_d = din("bwuq_f", [NB, 128, 3 * QW])
    wout_d = din("bwout_f", [NB, 128, BH * D])
    pt_d = din("page_table", [NS, NPG], I32)
    cckv_d = din("cache_ckv", [cfg.npool * PAGE, KV_LORA])
    ckpe_d = din("cache_kpe", [cfg.npool * PAGE, QK_ROPE])
    y_p = dout("y_p", [cfg.ptiles, 128, KC, TT])
    y_s = dout("y_s", [128, KC, NS])
    ckv_p = dout("ckv_p", [NSEQ * SEQ, KV_LORA])
    kpe_p = dout("kpe_p", [NSEQ * SEQ, QK_ROPE])
    ckv_s = dout("ckv_s", [NS, KV_LORA])
    kpe_s = dout("kpe_s", [NS, QK_ROPE])
    xs_scr = dscr("xs_scr", [128, KC, NS])
    cs_scr = dscr("cs_scr", [NS, KV_LORA])

    P.barrier()
    P.off = env["base_off"]
    al = P.alloc
    parb = al("parb", [128, NPAR_B])
    kvnw = al("kvnw", [128, KV_LORA])
    iota_p = al("iota_pf", [128, 1])
    win_b = al("win_b", [128, KC, ZW], BF16)
    wuq_b = al("wuq_b", [128, 3, QW], BF16)
    wuqR = al("wuqR", [128, 3, BH, 64], BF16)
    wout_b = al("wout2_b", [128, BH, D], BF16)
    wdkv_b = al("wdkv_b", [128, KC, 320], BF16)
    wuv_b = al("wuv_b", [128, 2, BH, 128], BF16)
    xt = al("m_xt", [128, KC, TT])
    sqb = al("m_sqb", [128, KC, TT], BF16)
    hnb = al("m_hnb", [128, KC, TT], BF16)
    rstd = al("m_rstd", [128, TT])
    cq = al("m_cq", [128, 3, TT])
    cqb = al("m_cqb", [128, 3, TT], BF16)
    zs = al("m_zs", [128, BH, TT], BF16)
    Qn = al("m_Qn", [128, TT], BF16)
    Qr = al("m_Qr", [64, TT], BF16)
    qa = al("m_qa", [64, TT])
    qb = al("m_qb", [64, TT])
    rden = al("m_rden", [128, TT])
    tmpo = al("m_tmpo", [128, TT])
    small = al("m_small", [128, 8])
    P.dma("sp", parb, par_b_d)
    P.dma("sp", kvnw, kvnw_d)
    P.dma("sp", iota_p, iota_d)
    P.dma("pool", wdkv_b.re("p k f -> p (k f)"), wdkv_d, max_dma_last_dim=4096)
    P.dma("pool", wuv_b.re("p c h v -> p (c h v)"), wuv_d, max_dma_last_dim=4096)
    pbc = lambda col: parb[:, col:col + 1]
    mark0 = P.off

    def rmsnorm_fm(x3, n, wcol, out3):
        P.act(sqb[:, :, 0:n], x3, AF.Square)
        ss = P.bank(2)[:, 0:n]
        for kc in range(KC):
            P.mm(ss, ones_b, sqb[:, kc, 0:n], start=kc == 0, stop=kc == KC - 1)
        P.act(rstd[:, 0:n], ss, AF.Sqrt, bias=NORM_EPS, scale=1.0 / D)
        P.recip(rstd[:, 0:n], rstd[:, 0:n])
        for kc in range(KC):
            P.stt(out3[:, kc, :], x3[:, kc, :], pbc(wcol + kc), rstd[:, 0:n], ALU.mult, ALU.mult)

    def q_side(l, n):
        for j in range(3):
            o = P.bank(j % 2)[:, 0:n]
            for kc in range(KC):
                P.mm(o, win_b[:, kc, j * 128:(j + 1) * 128], hnb[:, kc, 0:n], start=kc == 0, stop=kc == KC - 1)
            P.copy("act", cq[:, j, 0:n], o)
        P.act(sqb[:, 0:3, 0:n], cq[:, :, 0:n], AF.Square)
        ss = P.bank(2)[:, 0:n]
        for j in range(3):
            P.mm(ss, ones_b, sqb[:, j, 0:n], start=j == 0, stop=j == 2)
        P.act(rstd[:, 0:n], ss, AF.Sqrt, bias=NORM_EPS, scale=1.0 / Q_LORA)
        P.recip(rstd[:, 0:n], rstd[:, 0:n])
        for j in range(3):
            P.stt(cqb[:, j, 0:n], cq[:, j, 0:n], pbc(PB_QN + l * 3 + j), rstd[:, 0:n], ALU.mult, ALU.mult)
        for fc in range(BH):
            o = P.bank(fc % 2)[:, 0:n]
            for kc in range(KC):
                P.mm(o, win_b[:, kc, Q_LORA + fc * 128:Q_LORA + (fc + 1) * 128], hnb[:, kc, 0:n],
                     start=kc == 0, stop=kc == KC - 1)
            P.act(zs[:, fc, 0:n], o, AF.Silu)

    def q_head(h, n, cos_t, sin_t):
        o = P.bank(0)[:, 0:n]
        for j in range(3):
            P.mm(o, wuq_b[:, j, h * 192:h * 192 + 128], cqb[:, j, 0:n], start=j == 0, stop=j == 2)
        P.ts("dve", Qn[:, 0:n], o, ATTN_SCALE)
        oa = P.bank(1)[0:64, 0:n]
        ob = P.bank(1)[0:64, TT:TT + n]
        for j in range(3):
            P.mm(oa, wuq_b[:, j, h * 192 + 128:h * 192 + 192], cqb[:, j, 0:n], start=j == 0, stop=j == 2)
        for j in range(3):
            P.mm(ob, wuqR[:, j, h, :], cqb[:, j, 0:n], start=j == 0, stop=j == 2)
        if isinstance(cos_t, tuple):
            P.ts("dve", qa[:, 0:n], oa, cos_t[0])
            P.stt(Qr[:, 0:n], ob, sin_t[0], qa[:, 0:n], ALU.mult, ALU.add)
        else:
            P.tt("dve", qa[:, 0:n], oa, cos_t, ALU.mult)
            P.tt("dve", qb[:, 0:n], ob, sin_t, ALU.mult)
            P.tt("pool", Qr[:, 0:n], qa[:, 0:n], qb[:, 0:n], ALU.add)

    def kv_tm(n, ps, rope_t, ckv_t, kpe_t):
        ssq = small[0:n, 0:1]
        rs = small[0:n, 1:2]
        P.act_acc(ckv_t, ps[:, 0:256], AF.Square, ssq)
        P.act(rs, ssq, AF.Sqrt, bias=NORM_EPS, scale=1.0 / KV_LORA)
        P.recip(rs, rs)
        P.stt(ckv_t, ps[:, 0:256], rs, kvnw[0:n, :], ALU.mult, ALU.mult)
        x1, x2 = ps[:, 256:288], ps[:, 288:320]
        cs, sn = rope_t[:, 0:32], rope_t[:, 32:64]
        t1 = tmpo[0:n, 0:32]
        t2 = tmpo[0:n, 32:64]
        P.tt("dve", t1, x1, cs, ALU.mult)
        P.tt("dve", t2, x2, sn, ALU.mult)
        P.tt("dve", kpe_t[:, 0:32], t1, t2, ALU.subtract)
        P.tt("dve", t1, x2, cs, ALU.mult)
        P.tt("dve", t2, x1, sn, ALU.mult)
        P.tt("dve", kpe_t[:, 32:64], t1, t2, ALU.add)

    for l in range(NB):
        src = xb if l == 0 else xa
        skey = "D_xb_%d" if l == 0 else "D_xa_%d"
        for kc in range(KC):
            P.dma("pool", win_b[:, kc, :], win_d[l][:, kc * ZW:(kc + 1) * ZW], max_dma_last_dim=4096)
        for j in range(3):
            P.dma("pool", wuq_b[:, j, :], wuq_d[l][:, j * QW:(j + 1) * QW], max_dma_last_dim=4096)
        for hq in range(4):
            P.dma("pool", wout_b[:, hq * 4:(hq + 1) * 4, :].re("p a f -> p (a f)"),
                  wout_d[l][:, hq * 4 * D:(hq + 1) * 4 * D], max_dma_last_dim=4096)
        wq4 = wuq_b.re("p j (h e) -> p j h e", e=192)
        P.ts("dve", wuqR[:, :, :, 0:32], wq4[:, :, :, 160:192], -1.0)
        P.copy("dve", wuqR[:, :, :, 32:64], wq4[:, :, :, 128:160])
        P.off = mark0
        sfx = "_%d" % l
        wuk_b = al("wuk_b" + sfx, [128, 2, BH, 128], BF16)
        ckvT = al("ckvT" + sfx, [128, 2, SEQ], BF16)
        kpeT = al("kpeT" + sfx, [64, SEQ], BF16)
        KhT = al("KhT" + sfx, [128, SEQ], BF16)
        Vh = al("Vh" + sfx, [128, SEQ // 128, 128], BF16)
        pTb = [al("pT%d" % i + sfx, [128, TT], BF16) for i in range(2)]
        cm = al("cm" + sfx, [128, 2, TT])
        cosf = al("cosf" + sfx, [64, TT])
        sinf = al("sinf" + sfx, [64, TT])
        ropet = al("ropet" + sfx, [128, 64])
        ckv_t = al("ckv_t" + sfx, [128, KV_LORA])
        kpe_t = al("kpe_t" + sfx, [128, 64])
        P.dma("pool", wuk_b.re("p c h n -> p (c h n)"), wuk_d, max_dma_last_dim=4096)
        P.dma("sp", cm.re("p j t -> p (j t)"), cm_d)
        for s_ in range(NSEQ):
            for tis in range(cfg.tps):
                ti = s_ * cfg.tps + tis
                P.dma("sp", xt, xb[ti].k("D_xb_%d" % ti))
                rmsnorm_fm(xt, TT, PB_KVN, hnb)
                for tc in range(TT // 128):
                    pos = tis * TT + tc * 128
                    ps = P.bank(0)[:, 0:320]
                    for kc in range(KC):
                        P.mm(ps, hnb[:, kc, tc * 128:(tc + 1) * 128], wdkv_b[:, kc, :], start=kc == 0, stop=kc == KC - 1)
                    P.dma("sp", ropet, rope_tm_d[pos:pos + 128])
                    kv_tm(128, ps, ropet, ckv_t, kpe_t)
                    if l == 0:
                        r0 = s_ * SEQ + pos
                        P.dma("sp", ckv_p[r0:r0 + 128].k("D_ckvp_%d" % r0), ckv_t)
                        P.dma("sp", kpe_p[r0:r0 + 128].k("D_kpep_%d" % r0), kpe_t)
                    pt_ = P.bank(1)
                    P.tr(pt_[:, 0:128], ckv_t[:, 0:128], ident_f)
                    P.tr(pt_[:, 128:256], ckv_t[:, 128:256], ident_f)
                    P.tr(pt_[0:64, 256:384], kpe_t, ident_f)
                    P.copy("act", ckvT[:, 0, pos:pos + 128], pt_[:, 0:128])
                    P.copy("act", ckvT[:, 1, pos:pos + 128], pt_[:, 128:256])
                    P.copy("act", kpeT[:, pos:pos + 128], pt_[0:64, 256:384])
            for tis in range(cfg.tps):
                ti = s_ * cfg.tps + tis
                P.dma("sp", xt, src[ti].k(skey % ti))
                rmsnorm_fm(xt, TT, PB_BN + l * 8, hnb)
                q_side(l, TT)
                P.dma("sp", cosf, rope_fm_d[0][:, tis * TT:(tis + 1) * TT])
                P.dma("sp", sinf, rope_fm_d[1][:, tis * TT:(tis + 1) * TT])
                P.ts("pool", cosf, cosf, ATTN_SCALE)
                P.ts("pool", sinf, sinf, ATTN_SCALE)
                kend = (tis + 1) * TT
                nsc = kend // 128
                for h in range(BH):
                    q_head(h, TT, cosf, sinf)
                    for kb in range((kend + 511) // 512):
                        w_ = min(512, kend - kb * 512)
                        o = P.bank(7)[:, 0:w_]
                        for cc in range(2):
                            P.mm(o, wuk_b[:, cc, h, :], ckvT[:, cc, kb * 512:kb * 512 + w_], start=cc == 0, stop=cc == 1)
                        P.copy("act", KhT[:, kb * 512:kb * 512 + w_], o)
                    for g0 in range(0, nsc, 4):
                        g1 = min(nsc, g0 + 4)
                        ob_ = P.bank(2)
                        for sc in range(g0, g1):
                            for cc in range(2):
                                P.mm(ob_[:, (sc - g0) * 128:(sc - g0 + 1) * 128], ckvT[:, cc, sc * 128:(sc + 1) * 128],
                                     wuv_b[:, cc, h, :], start=cc == 0, stop=cc == 1)
                        P.copy("dve", Vh[:, g0:g1, :].re("p a v -> p (a v)"), ob_[:, 0:(g1 - g0) * 128])
                    oT = P.bank(5)[:, 0:TT]
                    den = P.bank(6)[:, 0:TT]
                    for sc in range(nsc):
                        sb_ = P.bank(3 + sc % 2)[:, 0:TT]
                        P.mm(sb_, KhT[:, sc * 128:(sc + 1) * 128], Qn, start=True, stop=False)
                        P.mm(sb_, kpeT[:, sc * 128:(sc + 1) * 128], Qr, start=False, stop=True)
                        pT = pTb[sc % 2]
                        P.act(pT, sb_, AF.Exp)
                        dj = sc - (nsc - 2)
                        if dj >= 0:
                            P.tt("pool", pT, pT, cm[:, dj, :], ALU.mult)
                        P.mm(oT, Vh[:, sc, :], pT, start=sc == 0, stop=sc == nsc - 1)
                        P.mm(den, ones_b, pT, start=sc == 0, stop=sc == nsc - 1)
                    P.recip(rden, den)
                    P.tt("dve", tmpo, oT, rden, ALU.mult)
                    P.tt("pool", zs[:, h, :], tmpo, zs[:, h, :], ALU.mult)
                for oc in range(KC):
                    o = P.bank(oc % 2)[:, 0:TT]
                    for h in range(BH):
                        P.mm(o, wout_b[:, h, oc * 128:(oc + 1) * 128], zs[:, h, :], start=h == 0, stop=h == BH - 1)
                    P.tt("dve", xt[:, oc, :], xt[:, oc, :], o, ALU.add)
                if l == NB - 1:
                    rmsnorm_fm_f32(P, xt, TT, PB_FN, pbc, sqb, rstd, ones_b)
                    P.dma("sp", y_p[ti], xt)
                else:
                    P.dma("sp", xa[ti].k("D_xa_%d" % ti), xt)
        P.barrier()
        P.off = mark0
        wukT_b = al("wukT_b" + sfx, [128, BH, KV_LORA], BF16)
        xs = al("xs" + sfx, [128, KC, NS])
        xkv = al("xkv" + sfx, [128, KC, NS])
        ropes = al("ropes" + sfx, [16, 64])
        ropesf = al("ropesf" + sfx, [64, 2])
        ckv_st = al("ckv_st" + sfx, [NS, KV_LORA])
        kpe_st = al("kpe_st" + sfx, [NS, 64])
        ckv_sT = al("ckv_sT" + sfx, [128, 2, NS], BF16)
        kpe_sT = al("kpe_sT" + sfx, [64, NS], BF16)
        cs1 = al("cs1" + sfx, [1, KV_LORA])
        cs1b = al("cs1b" + sfx, [1, KV_LORA + 8], BF16)
        qnb = al("qnb" + sfx, [128, NS], BF16)
        Qr_all = al("Qr_all" + sfx, [64, BH, NS], BF16)
        qlat = al("qlat" + sfx, [128, 2, BH, NS], BF16)
        olatT = al("olatT" + sfx, [128, 2, BH, NS], BF16)
        olat_n = al("olat_n" + sfx, [16, KV_LORA])
        ptb = al("ptb" + sfx, [128, NPG], I32)
        idx = al("idx" + sfx, [128, NPG], I32)
        pg_c = [al("pg_c%d" % i + sfx, [128, 4, KV_LORA]) for i in range(2)]
        pg_k = [al("pg_k%d" % i + sfx, [128, 4, 64]) for i in range(2)]
        cb = [al("cb%d" % i + sfx, [128, 4, KV_LORA + 1], BF16) for i in range(2)]
        kb_ = [al("kb%d" % i + sfx, [128, 4, 64], BF16) for i in range(2)]
        cT = al("cT" + sfx, [128, 4, 2, 128], BF16)
        kT = al("kT" + sfx, [64, 4, 128], BF16)
        pTs = al("pTs" + sfx, [128, 4, BH], BF16)
        pself = al("pself" + sfx, [1, BH], BF16)
        P.dma("pool", wukT_b.re("p h c -> p (h c)"), wukT_d, max_dma_last_dim=4096)
        P.dma("sp", ropes, rope_s_tm_d)
        P.dma("sp", ropesf, rope_s_fm_d)
        P.ts("dve", ropesf, ropesf, ATTN_SCALE)
        for i in range(2):
            P.memset("dve", cb[i][:, :, KV_LORA:KV_LORA + 1], 1.0)
        P.memset("dve", cs1b[:, KV_LORA:KV_LORA + 1], 1.0)
        for st in range(cfg.stiles):
            ti = cfg.ptiles + st
            P.dma("sp", xt, xb[ti].k("D_xb_%d" % ti))
            P.copy("dve", xkv[:, :, st * NCH:(st + 1) * NCH], xt.re("p k (r c) -> p k r c", c=C)[:, :, :, 0])
        if l == 0:
            P.copy("dve", xs, xkv)
        else:
            P.dma("sp", xs, xs_scr)
        rmsnorm_fm(xkv, NS, PB_KVN, hnb[:, :, 0:NS])
        ps = P.bank(0)[0:NS, 0:320]
        for kc in range(KC):
            P.mm(ps, hnb[:, kc, 0:NS], wdkv_b[:, kc, :], start=kc == 0, stop=kc == KC - 1)
        kv_tm(NS, ps, ropes[0:NS], ckv_st, kpe_st)
        if l == 0:
            P.dma("sp", ckv_s, ckv_st)
            P.dma("sp", kpe_s, kpe_st)
        P.dma("sp", cs_scr.k("D_cs_scr%d" % l), ckv_st)
        pt_ = P.bank(1)
        idn = ident_f[0:NS, 0:NS]
        P.tr(pt_[:, 0:NS], ckv_st[:, 0:128], idn)
        P.tr(pt_[:, NS:2 * NS], ckv_st[:, 128:256], idn)
        P.tr(pt_[0:64, 2 * NS:3 * NS], kpe_st, idn)
        P.copy("act", ckv_sT[:, 0, :], pt_[:, 0:NS])
        P.copy("act", ckv_sT[:, 1, :], pt_[:, NS:2 * NS])
        P.copy("act", kpe_sT, pt_[0:64, 2 * NS:3 * NS])
        rmsnorm_fm(xs, NS, PB_BN + l * 8, hnb[:, :, 0:NS])
        q_side(l, NS)
        for h in range(BH):
            q_head(h, NS, (ropesf[:, 0:1],), (ropesf[:, 1:2],))
            P.copy("act", Qr_all[:, h, :], Qr[:, 0:NS])
            for cc in range(2):
                o = P.bank(2)[:, cc * NS:(cc + 1) * NS]
                P.mm(o, wukT_b[:, h, cc * 128:(cc + 1) * 128], Qn[:, 0:NS])
            P.copy("dve", qlat[:, :, h, :], P.bank(2)[:, 0:2 * NS].re("p (c r) -> p c r", c=2))
        for r in range(NS):
            P.dma("sp", ptb, T(pt_d.ap[r].partition_broadcast(128), pt_d.key))
            P.ts("dve", idx, ptb, float(PAGE), iota_p, op0=ALU.mult, op1=ALU.add)
            ol = P.bank(5)[0:BH, 0:KV_LORA + 1]
            ng = NPG // 4
            for g in range(ng):
                b = g % 2
                for j in range(4):
                    pg = g * 4 + j
                    P.gather(pg_c[b][:, j, :], cckv_d, idx[:, pg:pg + 1])
                    P.gather(pg_k[b][:, j, :], ckpe_d, idx[:, pg:pg + 1])
                P.copy("dve", cb[b][:, :, 0:KV_LORA], pg_c[b])
                P.copy("act", kb_[b], pg_k[b])
                t3 = P.bank(3).bf()
                t4 = P.bank(4).bf()
                for j in range(4):
                    for cc in range(2):
                        P.tr(t3[:, (j * 2 + cc) * 128:(j * 2 + cc + 1) * 128], cb[b][:, j, cc * 128:(cc + 1) * 128], ident_b)
                    P.tr(t4[0:64, j * 128:(j + 1) * 128], kb_[b][:, j, :], ident_b)
                P.copy("act", cT.re("p j c t -> p (j c t)"), t3)
                P.copy("dve", kT.re("p j t -> p (j t)"), t4[0:64, 0:512])
                sc_ = P.bank(6)
                for j in range(4):
                    o = sc_[:, j * BH:(j + 1) * BH]
                    P.mm(o, cT[:, j, 0, :], qlat[:, 0, :, r], start=True, stop=False)
                    P.mm(o, cT[:, j, 1, :], qlat[:, 1, :, r], start=False, stop=False)
                    P.mm(o, kT[:, j, :], Qr_all[:, :, r], start=False, stop=True)
                P.act(pTs.re("p j h -> p (j h)"), sc_[:, 0:4 * BH], AF.Exp)
                for j in range(4):
                    P.mm(ol, pTs[:, j, :], cb[b][:, j, :], start=(g == 0 and j == 0), stop=False)
            so = P.bank(6)[0:1, 64:64 + BH]
            P.mm(so, ckv_sT[:, 0, r:r + 1], qlat[:, 0, :, r], start=True, stop=False)
            P.mm(so, ckv_sT[:, 1, r:r + 1], qlat[:, 1, :, r], start=False, stop=False)
            P.mm(so, kpe_sT[:, r:r + 1], Qr_all[:, :, r], start=False, stop=True)
            P.act(pself, so, AF.Exp)
            P.dma("sp", cs1, cs_scr.k("D_cs_scr%d" % l)[r:r + 1, :])
            P.copy("dve", cs1b[:, 0:KV_LORA], cs1)
            P.mm(ol, pself, cs1b[:, 0:KV_LORA + 1], start=False, stop=True)
            rd = small[0:BH, 2:3]
            P.recip(rd, ol[:, KV_LORA:KV_LORA + 1])
            P.ts("dve", olat_n, ol[:, 0:KV_LORA], rd)
            p7 = P.bank(7)
            idh = ident_f[0:BH, 0:BH]
            P.tr(p7[:, 0:BH], olat_n[:, 0:128], idh)
            P.tr(p7[:, BH:2 * BH], olat_n[:, 128:256], idh)
            P.copy("act", olatT[:, :, :, r], p7[:, 0:2 * BH].re("p (c h) -> p c h", c=2))
        for h in range(BH):
            o = P.bank(0)[:, 0:NS]
            for cc in range(2):
                P.mm(o, wuv_b[:, cc, h, :], olatT[:, cc, h, :], start=cc == 0, stop=cc == 1)
            P.tt("dve", zs[:, h, 0:NS], o, zs[:, h, 0:NS], ALU.mult)
        for oc in range(KC):
            o = P.bank(oc % 2)[:, 0:NS]
            for h in range(BH):
                P.mm(o, wout_b[:, h, oc * 128:(oc + 1) * 128], zs[:, h, 0:NS], start=h == 0, stop=h == BH - 1)
            P.tt("dve", xs[:, oc, :], xs[:, oc, :], o, ALU.add)
        if l == NB - 1:
            rmsnorm_fm_f32(P, xs, NS, PB_FN, pbc, sqb, rstd, ones_b)
            P.dma("sp", y_s, xs)
        else:
            P.dma("sp", xs_scr, xs)
        P.barrier()


def rmsnorm_fm_f32(P, x3, n, wcol, pbc, sqb, rstd, ones_b):
    P.act(sqb[:, :, 0:n], x3, AF.Square)
    ss = P.bank(2)[:, 0:n]
    for kc in range(KC):
        P.mm(ss, ones_b, sqb[:, kc, 0:n], start=kc == 0, stop=kc == KC - 1)
    P.act(rstd[:, 0:n], ss, AF.Sqrt, bias=NORM_EPS, scale=1.0 / D)
    P.recip(rstd[:, 0:n], rstd[:, 0:n])
    for kc in range(KC):
        P.stt(x3[:, kc, :], x3[:, kc, :], pbc(wcol + kc), rstd[:, 0:n], ALU.mult, ALU.mult)


def build(cfg):
    import os as _os
    nc = bass.Bass("TRN2", target_bir_lowering=False)
    P = Prog(nc)
    NT = cfg.ntiles

    def din(name, shape, dt=F32):
        return T(nc.dram_tensor(name, list(shape), dt, kind="ExternalInput").ap(), "D_" + name)

    def dout(name, shape, dt=F32):
        return T(nc.dram_tensor(name, list(shape), dt, kind="ExternalOutput").ap(), "D_" + name)

    def dscr(name, shape, dt=F32):
        return T(nc.dram_tensor(name, list(shape), dt, kind="Internal").ap(), "D_" + name)

    x_in = din("x_in", [NT, 128, KC, TT])
    consts_f = din("consts_f", [128, 128 + 3 * TT + 2 * TT])
    par_a = din("par_a", [NA, 128, NPAR_A])
    w_in_f = din("w_in_f", [NA, FC, 128, 4 * KC * 128])
    w1_f = din("w1_f", [NA, 128, KC * 64])
    a1_f = din("a1_f", [NA, 128, KC * 64])
    v1_f = din("v1_f", [128, KC * 32])
    w2_f = din("w2_f", [NA, 64, DI])
    a2_f = din("a2_f", [NA, 64, DI])
    v2_f = din("v2_f", [32, DI])
    wout_f = din("wout_f", [NA, 128, FC * D])
    st_wkv = din("st_wkv", [NA, cfg.stiles, FC, 128, NCH, 64])
    st_shift = din("st_shift", [NA, cfg.stiles, 128, KC, NCH])

    xa = dscr("xa", [NT, 128, KC, TT])
    xb = dscr("xb", [NT, 128, KC, TT])
    vfirst = dscr("vfirst", [NT, FC, 128, TT])
    w_in_b = dscr("w_in_b", [NA, FC, 128, 4 * KC * 128], BF16)

    wkv_out = dout("wkv_out", [NA, cfg.nstate, FC, 128, 64])
    shift_out = dout("shift_out", [NA, cfg.nstate, 128, KC])
    y_dbg = dout("y_dbg", [NT, 128, KC, TT]) if "mla" not in cfg.phases else None

    cf = P.alloc("cf", [128, 128 + 5 * TT])
    ident_f = cf[:, 0:128]
    m_lt = cf[:, 128:128 + TT].re("p (c t) -> p c t", c=NCH)
    m_gt = cf[:, 128 + TT:128 + 2 * TT].re("p (c t) -> p c t", c=NCH)
    m_le = cf[:, 128 + 2 * TT:128 + 3 * TT].re("p (c t) -> p c t", c=NCH)
    scanmask = cf[:, 128 + 3 * TT:128 + 4 * TT]
    colmask = cf[:, 128 + 4 * TT:128 + 5 * TT]
    ident_b = P.alloc("ident_b", [128, 128], BF16)
    ones_b = P.alloc("ones_b", [128, 128], BF16)
    bones_b = P.alloc("bones_b", [128, 128], BF16)
    eye_st = P.alloc("eye_st", [128, NCH, 64], BF16)
    bd_eye = P.alloc("bd_eye", [128, NCH, 128], BF16)
    P.dma("sp", cf, consts_f)
    P.copy("dve", ident_b, ident_f)
    P.memset("dve", ones_b, 1.0)
    P.memset("dve", bones_b, 0.0)
    P.memset("dve", bones_b[0:64, 0:64], 1.0)
    P.memset("dve", bones_b[64:128, 64:128], 1.0)
    P.memset("dve", bd_eye, 0.0)
    P.memset("dve", eye_st, 0.0)
    for c in range(NCH):
        P.copy("dve", bd_eye[:, c, :], ident_f)
        P.copy("dve", eye_st[0:64, c, :], ident_f[0:64, 0:64])
        P.copy("dve", eye_st[64:128, c, :], ident_f[64:128, 64:128])

    P.mark("consts_done")
    base_off = P.off
    for l in range(NA):
        for fc in range(FC):
            P.dma("pool", w_in_b[l, fc].k("D_winb_%d_%d" % (l, fc)), w_in_f[l, fc], max_dma_last_dim=4096)

    P.mark("prologue_done")
    par = P.alloc("par", [128, NPAR_A])
    w1b = P.alloc("w1b", [128, KC, 64], BF16)
    a1b = P.alloc("a1b", [128, KC, 64], BF16)
    v1b = P.alloc("v1b", [128, KC, 32], BF16)
    w2b = P.alloc("w2b", [64, DI], BF16)
    a2b = P.alloc("a2b", [64, DI], BF16)
    v2b = P.alloc("v2b", [32, DI], BF16)
    woutb = P.alloc("woutb", [128, FC, D], BF16)
    wbuf = [P.alloc("wbuf%d" % i, [128, 4, KC, 128], BF16) for i in range(2)]
    xt = P.alloc("xt", [128, KC, TT])
    hn = P.alloc("hn", [128, KC, TT])
    dx = P.alloc("dx", [128, KC, TT], BF16)
    hprev = P.alloc("hprev", [128, KC])
    shst = P.alloc("shst", [128, KC, NCH])
    shcp = P.alloc("shcp", [128, NCH, KC])
    rstd = P.alloc("rstd", [128, TT])
    mx = [P.alloc("mx%d" % p, [128, KC, TT], BF16) for p in range(6)]
    G = P.alloc("G", [128, FC, TT], BF16)
    hwb = P.alloc("hwb", [64, TT], BF16)
    hab = P.alloc("hab", [64, TT], BF16)
    hvb = P.alloc("hvb", [32, TT], BF16)
    S_f = P.alloc("S_f", [128, FC, 64])
    S_b = P.alloc("S_b", [128, FC, 64], BF16)
    Sin = [P.alloc("Sin%d" % i, [128, NCH, 64]) for i in range(2)]
    Sinb = [P.alloc("Sinb%d" % i, [128, NCH, 64], BF16) for i in range(2)]
    Sout = [P.alloc("Sout%d" % i, [128, NCH, 64]) for i in range(2)]
    pn = ["iclr", "sg", "zs", "r_f", "k_f", "v_f", "vg", "vf", "kkr", "rn", "kkn", "tq", "kp", "bv",
          "cum", "e_inc", "e_inv", "e_exc", "e_cl", "tmp", "yc", "ysq", "yf", "bon"]
    pr = {n: P.alloc(n, [128, TT]) for n in pn}
    sqk = P.alloc("sqk", [128, TT], BF16)
    rkb = P.alloc("rkb", [128, TT], BF16)
    stat = P.alloc("stat", [128, 8, NCH])
    sn_ = ["StR", "StB", "StA", "StN", "StNT", "StT", "StV", "StN2", "StNT2"]
    St = {n: P.alloc(n, [128, NCH, 64], BF16) for n in sn_}
    StP = P.alloc("StP", [128, 64], BF16)
    StU = P.alloc("StU", [128, 64], BF16)
    bn_ = ["BdR", "BdK", "BdB", "BdA", "BdMak", "BdMrb", "BdMrk", "BdT", "BdN", "BdNT", "BdN2", "BdNT2",
           "BdKh", "BdBh", "BdV", "BdKhT", "BdBhT", "BdY"]
    Bd = {n: P.alloc(n, [128, NCH, 128], BF16) for n in bn_}
    for n in bn_:
        if n not in ("BdKhT", "BdBhT"):
            P.memset("pool", Bd[n], 0.0)
    print("sbuf words used (RWKV phase):", P.off)

    def c3(t):
        return t.re("p (c t) -> p c t", c=NCH)

    def bd_halves(dst, src, engs=("act", "pool")):
        P.copy(engs[0], dst[0:64, :, 0:64], src[0:64])
        P.copy(engs[1], dst[64:128, :, 64:128], src[64:128])

    def rwkv_tile(l, ti, src, dst):
        sample = ti >= cfg.ptiles
        if sample:
            sti = ti - cfg.ptiles
        else:
            seq_i, tis = divmod(ti, cfg.tps)
        pa = lambda col, j=0: par[:, col + j:col + j + 1]
        bA = P.bank(0, "b0")[:, 0:TT]
        P.mark("tile_%d_%d_start" % (l, ti))
        P.dma("sp", xt, src[ti].k(src.key + "_%d" % ti))
        sq = mx[0]
        P.act(sq, xt, AF.Square)
        ssb = P.bank(3, "b3")[:, TT:2 * TT]
        for kc in range(KC):
            P.mm(ssb, ones_b, sq[:, kc, :], start=kc == 0, stop=kc == KC - 1)
        P.act(rstd, ssb, AF.Sqrt, bias=NORM_EPS, scale=1.0 / D)
        P.recip(rstd, rstd)
        for kc in range(KC):
            P.stt(hn[:, kc, :], xt[:, kc, :], pa(PA_NW, kc), rstd, ALU.mult, ALU.mult)
        P.mark("t%d_%d_shift" % (l, ti))
        if not sample:
            if tis == 0:
                P.memset("dve", hprev, 0.0)
            P.tt("dve", dx[:, :, 1:TT], hn[:, :, 0:TT - 1], hn[:, :, 1:TT], ALU.subtract)
            P.tt("dve", dx[:, :, 0:1], hprev.re("p (k o) -> p k o", o=1), hn[:, :, 0:1], ALU.subtract)
            P.copy("dve", hprev.re("p (k o) -> p k o", o=1), hn[:, :, TT - 1:TT])
            if tis == cfg.tps - 1:
                P.dma("sp", shift_out[l, seq_i], hprev)
        else:
            hn0 = hn.re("p k (r c) -> p k r c", c=C)[:, :, :, 0]
            dx0 = dx.re("p k (r c) -> p k r c", c=C)[:, :, :, 0]
            P.dma("sp", shst, st_shift[l, sti])
            P.memset("pool", dx, 0.0)
            P.tt("dve", dx0, shst, hn0, ALU.subtract)
            P.copy("dve", shcp.re("p r k -> p k r"), hn0)
            P.dma("sp", shift_out[l, cfg.nseq + sti * NCH: cfg.nseq + (sti + 1) * NCH].re("r p k -> p r k"), shcp)
        P.mark("t%d_%d_mixed" % (l, ti))
        for p in (4, 5, 0, 1, 2, 3):
            for kc in range(KC):
                P.stt(mx[p][:, kc, :], dx[:, kc, :], pa(PA_MIX, p * KC + kc), hn[:, kc, :], ALU.mult, ALU.add)
        b2a = P.bank(2, "b2")[0:64, 0:TT]
        b2b = P.bank(2, "b2")[0:64, TT:2 * TT]
        for kc in range(KC):
            P.mm(b2a, w1b[:, kc, :], mx[4][:, kc, :], start=kc == 0, stop=kc == KC - 1)
        P.act(hwb, b2a, AF.Tanh)
        for kc in range(KC):
            P.mm(b2b, a1b[:, kc, :], mx[5][:, kc, :], start=kc == 0, stop=kc == KC - 1)
        P.copy("act", hab, b2b)
        if l > 0:
            b3a = P.bank(3, "b3")[0:32, 0:TT]
            for kc in range(KC):
                P.mm(b3a, v1b[:, kc, :], mx[2][:, kc, :], start=kc == 0, stop=kc == KC - 1)
            P.copy("act", hvb, b3a)
        for fc in range(FC):
            P.mark("t%d_%d_fc%d" % (l, ti, fc))
            w = wbuf[fc % 2]
            P.dma("sp", w.re("p a k f -> p (a k f)"), w_in_b[l, fc].k("D_winb_%d_%d" % (l, fc)))
            fs = slice(fc * 128, (fc + 1) * 128)
            pb = {}
            for p, (bi, half) in enumerate(((0, 0), (0, 1), (1, 0), (1, 1))):
                o = P.bank(bi, "b%d%s" % (bi, "ab"[half]))[:, half * TT:(half + 1) * TT]
                pb[p] = o
                for kc in range(KC):
                    P.mm(o, w[:, p, kc, :], mx[p][:, kc, :], start=kc == 0, stop=kc == KC - 1)
            wl = P.bank(2, "b2")[:, 0:TT]
            al = P.bank(2, "b2")[:, TT:2 * TT]
            P.mm(wl, w2b[:, fs], hwb)
            P.mm(al, a2b[:, fs], hab)
            if l > 0:
                vl = P.bank(3, "b3")[:, 0:TT]
                P.mm(vl, v2b[:, fs], hvb)
            P.mark("t%d_%d_fc%d_prims" % (l, ti, fc))
            P.act(pr["iclr"], al, AF.Sigmoid, bias=pa(PA_A0, fc))
            P.act(pr["sg"], wl, AF.Sigmoid, bias=pa(PA_W0, fc))
            if sample:
                P.tt("pool", pr["sg"], pr["sg"], colmask, ALU.mult)
            P.act(pr["zs"], pb[3], AF.Silu)
            P.copy("act", pr["r_f"], pb[0])
            P.copy("dve", pr["k_f"], pb[1])
            if l == 0:
                P.copy("act", pr["v_f"], pb[2])
                P.dma("sp", vfirst[ti, fc].k("D_vf_%d_%d" % (ti, fc)), pr["v_f"])
            else:
                P.act(pr["vg"], vl, AF.Sigmoid, bias=pa(PA_V0, fc))
                P.dma("sp", pr["vf"], vfirst[ti, fc].k("D_vf_%d_%d" % (ti, fc)))
                P.tt("dve", pr["tmp"], pr["vf"], pb[2], ALU.subtract)
                P.tt("dve", pr["tmp"], pr["tmp"], pr["vg"], ALU.mult)
                P.tt("dve", pr["v_f"], pr["tmp"], pb[2], ALU.add)
            P.ts("dve", pr["kkr"], pr["k_f"], pa(PA_KK, fc))
            P.tt("pool", sqk, pr["kkr"], pr["kkr"], ALU.mult)
            ssk = P.bank(3, "b3")[:, TT:2 * TT]
            P.mm(ssk, bones_b, sqk)
            P.act(pr["rn"], ssk, AF.Sqrt)
            P.ts("dve", pr["rn"], pr["rn"], 1e-12, op0=ALU.max)
            P.recip(pr["rn"], pr["rn"])
            P.tt("dve", pr["kkn"], pr["kkr"], pr["rn"], ALU.mult)
            P.ts("dve", pr["tq"], pr["iclr"], -1.0, pa(PA_KA, fc), op0=ALU.add, op1=ALU.mult)
            P.stt(pr["kp"], pr["tq"], 1.0, pr["k_f"], ALU.add, ALU.mult)
            P.tt("pool", pr["bv"], pr["kkn"], pr["iclr"], ALU.mult)
            P.scan(pr["cum"], scanmask, pr["sg"], 0.0, ALU.mult, ALU.add)
            P.act(pr["e_inc"], pr["cum"], AF.Exp, scale=-C0)
            P.act(pr["e_inv"], pr["cum"], AF.Exp, scale=C0)
            P.tt("pool", pr["tmp"], pr["cum"], pr["sg"], ALU.subtract)
            P.act(pr["e_exc"], pr["tmp"], AF.Exp, scale=-C0)
            cum3 = c3(pr["cum"])
            P.tt("dve", c3(pr["yc"]), cum3[:, :, C - 1:C].bc([128, NCH, C]), cum3, ALU.subtract)
            P.act(pr["e_cl"], pr["yc"], AF.Exp, scale=-C0)
            P.mark("t%d_%d_fc%d_operands" % (l, ti, fc))
            P.tt("dve", St["StR"], c3(pr["r_f"]), c3(pr["e_inc"]), ALU.mult)
            bd_halves(Bd["BdR"], St["StR"])
            P.tt("dve", St["StB"], c3(pr["bv"]), c3(pr["e_inv"]), ALU.mult)
            bd_halves(Bd["BdB"], St["StB"])
            P.stt(St["StA"], c3(pr["kkn"]), -1.0, c3(pr["e_exc"]), ALU.mult, ALU.mult)
            bd_halves(Bd["BdA"], St["StA"])
            for (dst_, a_, b_) in (("BdK", "kp", "e_inv"), ("BdKh", "kp", "e_cl"), ("BdBh", "bv", "e_cl")):
                P.tt("dve", Bd[dst_][0:64, :, 0:64], c3(pr[a_])[0:64], c3(pr[b_])[0:64], ALU.mult)
                P.tt("pool", Bd[dst_][64:128, :, 64:128], c3(pr[a_])[64:128], c3(pr[b_])[64:128], ALU.mult)
            bd_halves(Bd["BdV"], c3(pr["v_f"]))
            P.mark("t%d_%d_fc%d_gram" % (l, ti, fc))
            gk = {}
            for name, (bi, half), lh, rh in (("ak", (0, 0), "BdK", "StA"), ("ab", (0, 1), "BdB", "StA"),
                                             ("abT", (1, 0), "BdA", "StB"), ("rk", (1, 1), "BdK", "StR"),
                                             ("rb", (2, 0), "BdB", "StR")):
                o = c3(P.bank(bi, "b%d%s" % (bi, "ab"[half]))[:, half * TT:(half + 1) * TT])
                gk[name] = o
                for c in range(NCH):
                    P.mm(o[:, c, :], Bd[lh][:, c, :], St[rh][:, c, :])

            def bd_masked(dst, src, mask):
                P.tt("dve", dst[0:64, :, 0:64], src[0:64], mask[0:64], ALU.mult)
                P.tt("dve", dst[64:128, :, 64:128], src[64:128], mask[64:128], ALU.mult)
            bd_masked(Bd["BdMak"], gk["ak"], m_lt)
            bd_masked(Bd["BdMrk"], gk["rk"], m_le)
            bd_masked(Bd["BdMrb"], gk["rb"], m_le)
            if not sample:
                P.tt("dve", St["StN"], gk["ab"], m_lt, ALU.mult)
                P.tt("dve", St["StNT"], gk["abT"], m_gt, ALU.mult)
                bd_halves(Bd["BdN"], St["StN"])
                bd_halves(Bd["BdNT"], St["StNT"])
                P.tt("pool", St["StT"], St["StN"], eye_st, ALU.add)
                cur = ("StN", "StNT", "BdN", "BdNT")
                nxt = ("StN2", "StNT2", "BdN2", "BdNT2")
                for lev in range(1, 6):
                    sN, sNT, bN, bNT = cur
                    nN, nNT, nbN, nbNT = nxt
                    o2 = c3(P.bank(4, "b4")[:, TT:2 * TT])
                    for c in range(NCH):
                        P.mm(o2[:, c, :], Bd[bN][:, c, :], St[sNT][:, c, :])
                    if lev < 5:
                        o1 = c3(P.bank(4, "b4")[:, 0:TT])
                        for c in range(NCH):
                            P.mm(o1[:, c, :], Bd[bNT][:, c, :], St[sN][:, c, :])
                        if _os.environ.get("E285", "act") == "ident":
                            P.act(St[nN], o1, AF.Identity)
                        else:
                            P.copy(_os.environ.get("E285", "act"), St[nN], o1)
                        P.copy("dve", St[nNT], o2)
                        bd_halves(Bd[nbN], St[nN])
                        bd_halves(Bd[nbNT], St[nNT])
                    else:
                        P.copy("dve", Bd[nbNT][0:64, :, 0:64], o2[0:64])
                        P.copy("act", Bd[nbNT][64:128, :, 64:128], o2[64:128])
                    o3 = c3(P.bank(5, "b5")[:, 0:TT])
                    for c in range(NCH):
                        P.mm(o3[:, c, :], Bd[nbNT][:, c, :], St["StT"][:, c, :])
                    P.tt("dve", St["StT"], o3, St["StT"], ALU.add)
                    cur, nxt = nxt, cur
                bd_halves(Bd["BdT"], St["StT"])
                BdT = Bd["BdT"]
            else:
                BdT = bd_eye
            P.mark("t%d_%d_fc%d_transp" % (l, ti, fc))
            t4 = P.bank(4, "b4").bf()
            t4k = T(t4.ap, "b4")
            for c in range(NCH):
                P.tr(T(t4.ap[:, c * 128:(c + 1) * 128], "b4"), Bd["BdKh"][:, c, :], ident_b)
            for c in range(NCH):
                P.tr(T(t4.ap[:, 512 + c * 128:512 + (c + 1) * 128], "b4"), Bd["BdBh"][:, c, :], ident_b)
            P.copy("act", Bd["BdKhT"].re("p c t -> p (c t)"), T(t4.ap[:, 0:512], "b4"))
            P.copy("dve", Bd["BdBhT"].re("p c t -> p (c t)"), T(t4.ap[:, 512:1024], "b4"))
            t5 = P.bank(5, "b5").bf()
            for c in range(NCH):
                P.tr(T(t5.ap[:, c * 128:(c + 1) * 128], "b5"), Bd["BdV"][:, c, :], ident_b)
            t5v = T(t5.ap[:, 0:512].rearrange("p (c t) -> p c t", c=NCH), "b5")
            P.copy("act", St["StV"][0:64], t5v[0:64, :, 0:64])
            P.copy("dve", St["StV"][64:128], t5v[64:128, :, 64:128])
            P.mark("t%d_%d_fc%d_chain" % (l, ti, fc))
            if sample:
                sb_ = fc % 2
                P.dma("sp", Sin[sb_], st_wkv[l, sti, fc])
                P.copy("pool", Sinb[sb_], Sin[sb_])
            elif tis == 0:
                P.memset("dve", S_f[:, fc, :].k("S_f%d" % fc), 0.0)
                P.memset("pool", S_b[:, fc, :].k("S_b%d" % fc), 0.0)
            yy = c3(P.bank(7, "b7")[:, 0:TT])
            pp = P.bank(6, "b6")[:, 0:64]
            pu = P.bank(6, "b6")[:, 64:128]
            sn = P.bank(6, "b6")[:, 128:192]
            for c in range(NCH):
                if sample:
                    s0b, s0f, s1f = Sinb[sb_][:, c, :], Sin[sb_][:, c, :], Sout[sb_][:, c, :]
                else:
                    s0b, s0f, s1f = S_b[:, fc, :].k("S_b%d" % fc), S_f[:, fc, :].k("S_f%d" % fc), S_f[:, fc, :].k("S_f%d" % fc)
                P.mm(pp, Bd["BdA"][:, c, :], s0b, start=True, stop=False)
                P.mm(pp, Bd["BdMak"][:, c, :], St["StV"][:, c, :], start=False, stop=True)
                P.copy("act", StP, pp)
                P.mm(pu, BdT[:, c, :], StP)
                P.copy("dve", StU, pu)
                P.mm(yy[:, c, :], Bd["BdR"][:, c, :], s0b, start=True, stop=False)
                P.mm(yy[:, c, :], Bd["BdMrb"][:, c, :], StU, start=False, stop=False)
                P.mm(yy[:, c, :], Bd["BdMrk"][:, c, :], St["StV"][:, c, :], start=False, stop=True)
                P.mm(sn, Bd["BdBhT"][:, c, :], StU, start=True, stop=False)
                P.mm(sn, Bd["BdKhT"][:, c, :], St["StV"][:, c, :], start=False, stop=True)
                wc = c3(pr["e_inc"])[:, c, C - 1:C]
                P.stt(s1f, s0f, wc, sn, ALU.mult, ALU.add)
                if not sample:
                    P.copy("act", s0b, s1f)
            if sample:
                P.dma("sp", wkv_out[l, cfg.nseq + sti * NCH: cfg.nseq + (sti + 1) * NCH, fc].re("r p v -> p r v"), Sout[sb_])
            elif tis == cfg.tps - 1:
                P.dma("sp", wkv_out[l, seq_i, fc], S_f[:, fc, :].k("S_f%d" % fc))
            P.mark("t%d_%d_fc%d_gn" % (l, ti, fc))
            s1 = stat[:, 0, :]
            s2 = stat[:, 1, :]
            mean = stat[:, 2, :]
            var = stat[:, 3, :]
            rs = stat[:, 4, :]
            P.red(s1, yy)
            ysq3 = c3(pr["ysq"])
            P.act(ysq3, yy, AF.Square)
            P.red(s2, ysq3)
            P.ts("dve", mean, s1, 1.0 / 64)
            P.tt("dve", var, mean, mean, ALU.mult)
            P.stt(var, s2, 1.0 / 64, var, ALU.mult, ALU.subtract)
            P.ts("dve", var, var, 0.0, GN_EPS, op0=ALU.max, op1=ALU.add)
            P.act(rs, var, AF.Sqrt)
            P.recip(rs, rs)
            yc3 = c3(pr["yc"])
            P.tt("dve", yc3, yy, mean.re("p (c o) -> p c o", o=1).bc([128, NCH, C]), ALU.subtract)
            rs3 = rs.re("p (c o) -> p c o", o=1).bc([128, NCH, C])
            P.tt("dve", Bd["BdY"][0:64, :, 0:64], yc3[0:64], rs3[0:64], ALU.mult)
            P.tt("pool", Bd["BdY"][64:128, :, 64:128], yc3[64:128], rs3[64:128], ALU.mult)
            for c in range(NCH):
                P.tr(T(t5.ap[:, 512 + c * 128:512 + (c + 1) * 128], "b5"), Bd["BdY"][:, c, :], ident_b)
            t5y = T(t5.ap[:, 512:1024].rearrange("p (c t) -> p c t", c=NCH), "b5")
            yf3 = c3(pr["yf"])
            P.ts("dve", yf3[0:64], t5y[0:64, :, 0:64], pa(PA_LW, fc)[0:64], pa(PA_LB, fc)[0:64], op0=ALU.mult, op1=ALU.add)
            P.ts("dve", yf3[64:128], t5y[64:128, :, 64:128], pa(PA_LW, fc)[64:128], pa(PA_LB, fc)[64:128], op0=ALU.mult, op1=ALU.add)
            P.mark("t%d_%d_fc%d_bonus" % (l, ti, fc))
            P.stt(rkb, pr["r_f"], pa(PA_RK, fc), pr["kp"], ALU.mult, ALU.mult)
            bo = P.bank(3, "b3")[:, 0:TT]
            P.mm(bo, bones_b, rkb)
            P.tt("dve", pr["bon"], bo, pr["v_f"], ALU.mult)
            P.tt("pool", pr["bon"], pr["bon"], pr["yf"], ALU.add)
            P.tt("pool", G[:, fc, :], pr["bon"], pr["zs"], ALU.mult)
        P.mark("t%d_%d_outproj" % (l, ti))
        for oc in range(KC):
            o = P.bank(2 + (oc % 2), "b%da" % (2 + oc % 2))[:, 0:TT]
            for fc in range(FC):
                P.mm(o, woutb[:, fc, oc * 128:(oc + 1) * 128], G[:, fc, :], start=fc == 0, stop=fc == FC - 1)
            P.tt("dve", xt[:, oc, :], xt[:, oc, :], o, ALU.add)
        P.dma("sp", dst[ti].k(dst.key + "_%d" % ti), xt)

    if "rwkv" in cfg.phases:
        for l in range(NA):
            P.dma("sp", par, par_a[l])
            P.dma("pool", w1b.re("p k f -> p (k f)"), w1_f[l])
            P.dma("pool", a1b.re("p k f -> p (k f)"), a1_f[l])
            P.dma("pool", w2b, w2_f[l], max_dma_last_dim=4096)
            P.dma("pool", a2b, a2_f[l], max_dma_last_dim=4096)
            if l > 0:
                P.dma("pool", v1b.re("p k f -> p (k f)"), v1_f)
                P.dma("pool", v2b, v2_f, max_dma_last_dim=4096)
            for fcq in range(4):
                P.dma("pool", woutb[:, fcq * 4:(fcq + 1) * 4, :].re("p a f -> p (a f)"),
                      wout_f[l][:, fcq * 4 * D:(fcq + 1) * 4 * D], max_dma_last_dim=4096)
            src = x_in if l == 0 else xa
            dst = xa if l == 0 else xb
            for ti in range(NT):
                rwkv_tile(l, ti, src, dst)
        if "mla" not in cfg.phases:
            for ti in range(NT):
                P.dma("sp", xt, xb[ti].k("D_xb_%d" % ti))
                P.dma("sp", y_dbg[ti], xt)

    if "mla" in cfg.phases:
        mla_phase(P, nc, cfg, dict(din=din, dout=dout, dscr=dscr, xa=xa, xb=xb, ident_f=ident_f, ident_b=ident_b,
                                   ones_b=ones_b, base_off=base_off))

    P.marks_final = list(getattr(P, "marks", []))
    import os as _os
    if _os.environ.get("SHOWMARKS"):
        for n_, i_ in P.marks_final:
            if "fc" not in n_ or "fc0" in n_ and "_0_" in n_:
                print("MARK", n_, i_)
    P.emit()
    print("prog stats:", P.stats)
    return nc


def vec_fm(v, n):
    return np.ascontiguousarray(np.asarray(v, np.float32).reshape(n, 128).T)


def make_consts():
    cf = np.zeros((128, 128 + 5 * TT), np.float32)
    cf[:, 0:128] = np.eye(128, dtype=np.float32)
    j = np.arange(128) % 64
    t = np.arange(64)
    lt = (j[:, None] < t[None, :]).astype(np.float32)
    gt = (j[:, None] > t[None, :]).astype(np.float32)
    le = (j[:, None] <= t[None, :]).astype(np.float32)
    cf[:, 128:128 + TT] = np.tile(lt, (1, NCH))
    cf[:, 128 + TT:128 + 2 * TT] = np.tile(gt, (1, NCH))
    cf[:, 128 + 2 * TT:128 + 3 * TT] = np.tile(le, (1, NCH))
    sm = np.ones(TT, np.float32)
    sm[0::C] = 0.0
    cf[:, 128 + 3 * TT:128 + 4 * TT] = sm[None, :]
    cf[:, 128 + 4 * TT:128 + 5 * TT] = 1.0 - sm[None, :]
    return cf


def prep_shared(inp):
    sh = {}
    sh["consts_f"] = make_consts()
    par = np.zeros((NA, 128, NPAR_A), np.float32)
    for l in range(NA):
        par[l, :, PA_NW:PA_NW + 8] = vec_fm(inp["a_norm_w"][l], 8)
        mixl = np.asarray(inp["a_mix"][l])
        for p, src in enumerate((0, 1, 2, 3, 4, 5)):
            par[l, :, PA_MIX + p * 8:PA_MIX + (p + 1) * 8] = vec_fm(mixl[src], 8)
        par[l, :, PA_W0:PA_W0 + 16] = vec_fm(inp["a_w0"][l], 16)
        par[l, :, PA_A0:PA_A0 + 16] = vec_fm(inp["a_a0"][l], 16)
        if l > 0:
            par[l, :, PA_V0:PA_V0 + 16] = vec_fm(inp["a_v0"][l - 1], 16)
        par[l, :, PA_KK:PA_KK + 16] = vec_fm(inp["a_k_k"][l], 16)
        par[l, :, PA_KA:PA_KA + 16] = vec_fm(inp["a_k_a"][l], 16)
        par[l, :, PA_RK:PA_RK + 16] = vec_fm(np.asarray(inp["a_r_k"][l]).reshape(-1), 16)
        par[l, :, PA_LW:PA_LW + 16] = vec_fm(inp["a_lnx_w"][l], 16)
        par[l, :, PA_LB:PA_LB + 16] = vec_fm(inp["a_lnx_b"][l], 16)
    sh["par_a"] = par
    w = np.asarray(inp["a_w_in"], np.float32).reshape(NA, 4, KC, 128, FC, 128)
    sh["w_in_f"] = np.ascontiguousarray(w.transpose(0, 4, 3, 1, 2, 5)).reshape(NA, FC, 128, 4 * KC * 128)
    def kfm(a, n):
        a = np.asarray(a, np.float32)
        lead = a.shape[:-2]
        a = a.reshape(lead + (KC, 128, n))
        a = np.moveaxis(a, -3, -2)
        return np.ascontiguousarray(a).reshape(lead + (128, KC * n))
    sh["w1_f"] = kfm(inp["a_w1"], 64)
    sh["a1_f"] = kfm(inp["a_a1"], 64)
    sh["v1_f"] = kfm(inp["a_v1"][0], 32)
    sh["w2_f"] = np.ascontiguousarray(inp["a_w2"], np.float32)
    sh["a2_f"] = np.ascontiguousarray(inp["a_a2"], np.float32)
    sh["v2_f"] = np.ascontiguousarray(inp["a_v2"][0], np.float32)
    wo = np.asarray(inp["a_w_out"], np.float32).reshape(NA, FC, 128, D)
    sh["wout_f"] = np.ascontiguousarray(wo.transpose(0, 2, 1, 3)).reshape(NA, 128, FC * D)
    return sh


def x_tiles(xp, xs, cfg):
    nt = cfg.ntiles
    out = np.zeros((nt, 128, KC, TT), np.float32)
    a = xp.reshape(cfg.ptiles, TT, KC, 128)
    out[:cfg.ptiles] = a.transpose(0, 3, 2, 1)
    b = xs.reshape(cfg.stiles, NCH, KC, 128)
    o = out[cfg.ptiles:].reshape(cfg.stiles, 128, KC, NCH, C)
    o[:, :, :, :, 0] = b.transpose(0, 3, 2, 1)
    return out


def prep_core(inp, cfg, pi, si):
    m = {}
    xp = np.asarray(inp["x_prompt"], np.float32)[pi]
    xs = np.asarray(inp["x_sample"], np.float32)[si][:, 0, :]
    m["x_in"] = x_tiles(xp, xs, cfg)
    sw = np.asarray(inp["state_wkv"], np.float32)[:, si]
    sw = sw.reshape(NA, cfg.stiles, NCH, FC, 2, 64, 64)
    m["st_wkv"] = np.ascontiguousarray(sw.transpose(0, 1, 3, 4, 6, 2, 5)).reshape(NA, cfg.stiles, FC, 128, NCH, 64)
    ss = np.asarray(inp["state_shift"], np.float32)[:, si]
    ss = ss.reshape(NA, cfg.stiles, NCH, KC, 128)
    m["st_shift"] = np.ascontiguousarray(ss.transpose(0, 1, 4, 3, 2))
    return m


def untile(y, cfg):
    yp = y[:cfg.ptiles].transpose(0, 3, 2, 1).reshape(cfg.nseq, cfg.seq, D)
    ys = y[cfg.ptiles:].reshape(cfg.stiles, 128, KC, NCH, C)[:, :, :, :, 0].transpose(0, 3, 2, 1).reshape(cfg.ns, D)
    return yp, ys


def unstate(w, cfg):
    a = w.reshape(NA, cfg.nstate, FC, 2, 64, 64)
    return np.ascontiguousarray(a.transpose(0, 1, 2, 3, 5, 4)).reshape(NA, cfg.nstate, 32, 64, 64)


def unshift(s, cfg):
    return np.ascontiguousarray(s.transpose(0, 1, 3, 2)).reshape(NA, cfg.nstate, D)


def prep_shared_mla(inp, cfg):
    sh = {}
    pb = np.zeros((128, NPAR_B), np.float32)
    pb[:, PB_KVN:PB_KVN + 8] = vec_fm(inp["kv_in_norm_w"], 8)
    for l in range(NB):
        pb[:, PB_BN + l * 8:PB_BN + (l + 1) * 8] = vec_fm(inp["b_norm_w"][l], 8)
        pb[:, PB_QN + l * 3:PB_QN + (l + 1) * 3] = vec_fm(inp["b_q_norm_w"][l], 3)
    pb[:, PB_FN:PB_FN + 8] = vec_fm(inp["final_norm_w"], 8)
    sh["par_b"] = pb
    sh["kvnw_bc"] = np.ascontiguousarray(np.broadcast_to(np.asarray(inp["kv_norm_w"], np.float32)[None, :], (128, KV_LORA)))
    half = QK_ROPE // 2
    inv_freq = (np.float32(10000.0) ** (-np.arange(half, dtype=np.float32) / np.float32(half))).astype(np.float32)
    pos = np.arange(cfg.seq, dtype=np.float32)
    ang = (pos[:, None] * inv_freq[None, :]).astype(np.float32)
    cs, sn = np.cos(ang).astype(np.float32), np.sin(ang).astype(np.float32)
    sh["rope_tm"] = np.ascontiguousarray(np.concatenate([cs, sn], axis=1))
    cf2 = np.concatenate([cs, cs], axis=1).T
    sf2 = np.concatenate([sn, sn], axis=1).T
    sh["rope_fm"] = np.ascontiguousarray(np.stack([cf2, sf2]))
    past = np.float32(cfg.npages * PAGE)
    angs = (past * inv_freq).astype(np.float32)
    cs1, sn1 = np.cos(angs).astype(np.float32), np.sin(angs).astype(np.float32)
    sh["rope_s_tm"] = np.ascontiguousarray(np.broadcast_to(np.concatenate([cs1, sn1])[None, :], (16, 64)))
    sh["rope_s_fm"] = np.ascontiguousarray(np.stack([np.concatenate([cs1, cs1]), np.concatenate([sn1, sn1])], axis=1))
    s_ = np.arange(128)[:, None]
    q_ = np.arange(TT)[None, :]
    cmk = np.concatenate([(s_ <= q_), (128 + s_ <= q_)], axis=1).astype(np.float32)
    sh["cmask"] = np.ascontiguousarray(cmk)
    sh["iota_p"] = np.arange(128, dtype=np.float32).reshape(128, 1)

    def kfm(a, nk):
        a = np.asarray(a, np.float32)
        n = a.shape[-1]
        lead = a.shape[:-2]
        a = a.reshape(lead + (nk, 128, n))
        a = np.moveaxis(a, -3, -2)
        return np.ascontiguousarray(a).reshape(lead + (128, nk * n))
    sh["wdkv_f"] = kfm(inp["w_dkv"], KC)
    wuk = np.asarray(inp["w_uk"], np.float32)
    wuv = np.asarray(inp["w_uv"], np.float32)
    sh["wuk_f"] = np.ascontiguousarray(wuk.reshape(2, 128, BH, 128).transpose(1, 0, 2, 3)).reshape(128, -1)
    sh["wuv_f"] = np.ascontiguousarray(wuv.reshape(2, 128, BH, 128).transpose(1, 0, 2, 3)).reshape(128, -1)
    sh["wukT_f"] = np.ascontiguousarray(wuk.transpose(2, 1, 0)).reshape(128, -1)
    sh["bwin_f"] = kfm(inp["b_w_in"], KC)
    sh["bwuq_f"] = kfm(inp["b_w_uq"], 3)
    sh["bwout_f"] = kfm(inp["b_w_out"], BH)
    sh["cache_ckv"] = np.asarray(inp["cache_ckv"], np.float32).reshape(-1, KV_LORA)
    sh["cache_kpe"] = np.asarray(inp["cache_kpe"], np.float32).reshape(-1, QK_ROPE)
    return sh


def run_all(inp, cfg, ncores, trace=False):
    nc = build(cfg)
    sh = prep_shared(inp)
    sh.update(prep_shared_mla(inp, cfg))
    in_maps = []
    for c in range(ncores):
        pi = list(range(c * cfg.nseq, (c + 1) * cfg.nseq))
        si = list(range(c * cfg.ns, (c + 1) * cfg.ns))
        m = dict(sh)
        m.update(prep_core(inp, cfg, pi, si))
        m["page_table"] = np.ascontiguousarray(np.asarray(inp["page_table"], np.int32)[si])
        in_maps.append(m)
    res = run_bass_kernel_spmd(nc, in_maps, core_ids=list(range(ncores)))
    return assemble([r for r in res.results], cfg, ncores)


def assemble(results, cfg, ncores):
    yp, ys, wkp, wks, shp, shs, ckp, kpp, cks, kps = ([] for _ in range(10))
    for r in results:
        a = r["y_p"].transpose(0, 3, 2, 1).reshape(cfg.nseq, cfg.seq, D)
        yp.append(a)
        ys.append(r["y_s"].transpose(2, 1, 0).reshape(cfg.ns, 1, D))
        w = unstate(r["wkv_out"], cfg)
        wkp.append(w[:, :cfg.nseq])
        wks.append(w[:, cfg.nseq:])
        s = unshift(r["shift_out"], cfg)
        shp.append(s[:, :cfg.nseq])
        shs.append(s[:, cfg.nseq:])
        ckp.append(r["ckv_p"].reshape(cfg.nseq, cfg.seq, KV_LORA))
        kpp.append(r["kpe_p"].reshape(cfg.nseq, cfg.seq, QK_ROPE))
        cks.append(r["ckv_s"].reshape(cfg.ns, 1, KV_LORA))
        kps.append(r["kpe_s"].reshape(cfg.ns, 1, QK_ROPE))
    cat = lambda xs, ax: np.ascontiguousarray(np.concatenate(xs, axis=ax), dtype=np.float32)
    return (cat(yp, 0), cat(ys, 0), cat(wkp, 1), cat(shp, 1), cat(ckp, 0), cat(kpp, 0),
            cat(wks, 1), cat(shs, 1), cat(cks, 0), cat(kps, 0))


def kernel(**inputs):
    inp = {k: np.asarray(v) for k, v in inputs.items()}
    npool = inp["cache_ckv"].shape[0]
    npages = inp["page_table"].shape[1]
    cfg = Cfg(nseq=2, seq=inp["x_prompt"].shape[1], ns=16, npages=npages, npool=npool)
    return run_all(inp, cfg, 8)
```

```python
import contextlib
import numpy as np
import concourse.bass as bass
import concourse.mybir as mybir
from concourse.bass_utils import run_bass_kernel_spmd

F32 = mybir.dt.float32
BF16 = mybir.dt.bfloat16
I32 = mybir.dt.int32
ALU = mybir.AluOpType
AF = mybir.ActivationFunctionType
AX = mybir.AxisListType

D = 1024
KC = 8
DI = 2048
FC = 16
NA = 2
NB = 2
C = 64
TT = 256
NCH = TT // C
KV_LORA = 256
QK_ROPE = 64
QK_NOPE = 128
Q_LORA = 384
BH = 16
NORM_EPS = 1e-6
GN_EPS = 64 * 1e-5
C0 = float(np.exp(-0.5))
ATTN_SCALE = float((QK_NOPE + QK_ROPE) ** -0.5)
PAGE = 128

COMPUTE = ("pe", "dve", "act", "pool")
NHW = 32
NSW = 16
NDMASEM = NHW + NSW


class T:
    def __init__(self, ap, key):
        self.ap = ap
        self.key = key

    def __getitem__(self, idx):
        return T(self.ap[idx], self.key)

    def bc(self, shape):
        return T(self.ap.broadcast_to(list(shape)), self.key)

    def re(self, pat, **kw):
        return T(self.ap.rearrange(pat, **kw), self.key)

    def k(self, key):
        return T(self.ap, key)

    def bf(self):
        return T(self.ap.bitcast(BF16), self.key)


def _keys(*ts):
    out = []
    for t in ts:
        if isinstance(t, T):
            out.append(t.key)
    return out


class Prog:
    def __init__(self, nc):
        self.nc = nc
        self.ops = []
        self.stack = contextlib.ExitStack()
        self.sbuf = self.stack.enter_context(nc.sbuf_tensor("sbpool", [128, 50 * 1024], F32))
        self.psum = self.stack.enter_context(nc.psum_tensor("pspool", [128, 8, 512], F32))
        self.off = 0
        self.names = set()

    def alloc(self, name, shape, dt=F32):
        assert name not in self.names, name
        self.names.add(name)
        n = int(np.prod(shape[1:]))
        words = n if dt in (F32, I32) else (n + 1) // 2
        words = (words + 7) // 8 * 8
        assert self.off + words <= 50 * 1024, ("sbuf overflow", name, self.off, words)
        ap = self.sbuf[0:shape[0], self.off:self.off + words]
        self.off += words
        if dt != F32:
            ap = ap.bitcast(dt)
        ap = ap[:, 0:n]
        if len(shape) == 3:
            ap = ap.rearrange("p (a b) -> p a b", a=shape[1], b=shape[2])
        elif len(shape) == 4:
            ap = ap.rearrange("p (a b c) -> p a b c", a=shape[1], b=shape[2], c=shape[3])
        return T(ap, name)

    def bank(self, i, key=None):
        return T(self.psum[:, i, :], "b%d" % i)

    def op(self, eng, fn, reads=(), writes=(), dma=False):
        self.ops.append(dict(eng=eng, fn=fn, reads=tuple(reads), writes=tuple(writes), dma=dma))

    def mark(self, name):
        self.marks = getattr(self, "marks", [])
        self.marks.append((name, len(self.ops)))

    def barrier(self):
        self.ops.append(dict(eng=None, barrier=True, reads=(), writes=(), dma=False))

    def dma(self, q, out, in_, **kw):
        o = out.ap if isinstance(out, T) else out
        i = in_.ap if isinstance(in_, T) else in_
        self.op(q, lambda e: e.dma_start(out=o, in_=i, **kw), _keys(in_), _keys(out), dma=True)

    def mm(self, out, lhsT, rhs, start=True, stop=True):
        self.op("pe", lambda e: e.matmul(out.ap, lhsT.ap, rhs.ap, start=start, stop=stop),
                _keys(lhsT, rhs), _keys(out))

    def tr(self, out, in_, ident):
        self.op("pe", lambda e: e.transpose(out.ap, in_.ap, ident.ap), _keys(in_, ident), _keys(out))

    def tt(self, eng, out, in0, in1, op):
        self.op(eng, lambda e: e.tensor_tensor(out.ap, in0.ap, in1.ap, op), _keys(in0, in1), _keys(out))

    def ts(self, eng, out, in0, s1, s2=None, op0=ALU.mult, op1=None):
        a1 = s1.ap if isinstance(s1, T) else s1
        a2 = s2.ap if isinstance(s2, T) else s2
        if op1 is None:
            self.op(eng, lambda e: e.tensor_scalar(out.ap, in0.ap, a1, None, op0), _keys(in0, s1), _keys(out))
        else:
            self.op(eng, lambda e: e.tensor_scalar(out.ap, in0.ap, a1, a2, op0, op1),
                    _keys(in0, s1, s2), _keys(out))

    def stt(self, out, in0, scalar, in1, op0, op1):
        sc = scalar.ap if isinstance(scalar, T) else scalar
        self.op("dve", lambda e: e.scalar_tensor_tensor(out.ap, in0.ap, sc, in1.ap, op0, op1),
                _keys(in0, scalar, in1), _keys(out))

    def act(self, out, in_, func, bias=None, scale=None):
        kw = {}
        if bias is not None:
            kw["bias"] = bias.ap if isinstance(bias, T) else bias
        if scale is not None:
            kw["scale"] = scale.ap if isinstance(scale, T) else scale
        self.op("act", lambda e: e.activation(out=out.ap, in_=in_.ap, func=func, **kw),
                _keys(in_, bias, scale), _keys(out))

    def gather(self, out, in_, idx):
        self.op("pool", lambda e: e.indirect_dma_start(out=out.ap, out_offset=None, in_=in_.ap,
                in_offset=bass.IndirectOffsetOnAxis(ap=idx.ap, axis=0)), _keys(in_, idx), _keys(out), dma=True)

    def act_acc(self, out, in_, func, accum):
        self.op("act", lambda e: e.activation(out=out.ap, in_=in_.ap, func=func, accum_out=accum.ap),
                _keys(in_), _keys(out, accum))

    def copy(self, eng, out, in_):
        if eng == "act":
            self.op("act", lambda e: e.activation(out=out.ap, in_=in_.ap, func=AF.Copy), _keys(in_), _keys(out))
        else:
            self.op(eng, lambda e: e.tensor_copy(out.ap, in_.ap), _keys(in_), _keys(out))

    def memset(self, eng, out, val):
        self.op(eng, lambda e: e.memset(out.ap, val), (), _keys(out))

    def red(self, out, in_, op=ALU.add, axis=AX.X):
        self.op("dve", lambda e: e.tensor_reduce(out.ap, in_.ap, axis, op), _keys(in_), _keys(out))

    def recip(self, out, in_):
        self.op("dve", lambda e: e.reciprocal(out.ap, in_.ap), _keys(in_), _keys(out))

    def scan(self, out, d0, d1, init, op0, op1):
        self.op("dve", lambda e: e.tensor_tensor_scan(out.ap, d0.ap, d1.ap, init, op0, op1),
                _keys(d0, d1), _keys(out))

    def emit(self):
        nc = self.nc
        import os as _os
        mo = int(_os.environ.get("MAXOPS", "0"))
        if mo:
            self.ops = self.ops[:mo]
        ops = self.ops
        last_w = {}
        readers = {}
        last_on_eng = {}
        dma_since = []
        for i, o in enumerate(ops):
            if o.get("barrier"):
                o["deps"] = set(last_on_eng.values()) | set(dma_since)
                o["bar_deps"] = set(o["deps"])
                dma_since = []
                last_w.clear()
                readers.clear()
                last_on_eng = {}
                o["is_bar"] = True
                continue
            deps = set()
            E = o["eng"]
            for k in o["reads"]:
                w = last_w.get(k)
                if w is not None:
                    wo = ops[w]
                    if wo["dma"] or wo["eng"] != E or E != "pe":
                        deps.add(w)
                if len(k) == 2 and k[0] == "b" and k[1].isdigit():
                    for (re_, rdma), r in readers.get(k, {}).items():
                        if re_ != E:
                            deps.add(r)
            for k in o["writes"]:
                w = last_w.get(k)
                if w is not None:
                    wo = ops[w]
                    if wo["dma"] or wo["eng"] != E or E != "pe":
                        deps.add(w)
                for (re_, rdma), r in readers.get(k, {}).items():
                    if rdma or re_ != E or E != "pe":
                        deps.add(r)
            for k in o["reads"]:
                readers.setdefault(k, {})[(E, o["dma"])] = i
            for k in o["writes"]:
                last_w[k] = i
                readers[k] = {}
            deps.discard(i)
            o["deps"] = deps
            if o["dma"]:
                dma_since.append(i)
            else:
                last_on_eng[E] = i
        pending = {}
        for i, o in enumerate(ops):
            if o.get("barrier"):
                for e in ("pe", "dve", "act", "pool", "sp"):
                    pending.setdefault(e, set()).update(o["bar_deps"])
                continue
            E = o["eng"]
            if pending.get(E):
                o["deps"] |= pending[E]
                pending[E] = set()
        for o in ops:
            o["flag"] = False
        for o in ops:
            if o.get("barrier"):
                continue
            for d in o["deps"]:
                if not ops[d]["dma"]:
                    ops[d]["flag"] = True
        cnt = {e: 0 for e in COMPUTE}
        ndma = 0
        nq = {"hw": 0, "sw": 0}
        for o in ops:
            if o.get("barrier"):
                continue
            if o["dma"]:
                if o["eng"] == "pool":
                    o["sem"] = NHW + nq["sw"] % NSW
                    o["val"] = 16 * (nq["sw"] // NSW + 1)
                    nq["sw"] += 1
                else:
                    o["sem"] = nq["hw"] % NHW
                    o["val"] = 16 * (nq["hw"] // NHW + 1)
                    nq["hw"] += 1
                ndma += 1
            elif o["flag"]:
                cnt[o["eng"]] += 1
                o["val"] = cnt[o["eng"]]
        sems = {e: self.stack.enter_context(nc.semaphore("s_" + e)) for e in COMPUTE}
        dsems = [self.stack.enter_context(nc.semaphore("s_dma%d" % j)) for j in range(NDMASEM)]
        per_eng = {e: [] for e in ("pe", "dve", "act", "pool", "sp")}
        for i, o in enumerate(ops):
            if o.get("barrier"):
                continue
            per_eng[o["eng"]].append(i)
        final_dma = {}
        for o in ops:
            if o.get("barrier"):
                continue
            if o["dma"]:
                final_dma[o["sem"]] = max(final_dma.get(o["sem"], 0), o["val"])
        n_wait = [0]

        def emit_engine(ename, e):
            waited = {}
            for i in per_eng[ename]:
                o = ops[i]
                need = {}
                for d in o["deps"]:
                    do = ops[d]
                    key = ("d", do["sem"]) if do["dma"] else ("c", do["eng"])
                    need[key] = max(need.get(key, 0), do["val"])
                if o["dma"] and o["val"] > 16:
                    key = ("d", o["sem"])
                    need[key] = max(need.get(key, 0), o["val"] - 16)
                for key, v in need.items():
                    if waited.get(key, 0) >= v:
                        continue
                    waited[key] = v
                    s = dsems[key[1]] if key[0] == "d" else sems[key[1]]
                    e.wait_ge(s, v)
                    n_wait[0] += 1
                ins = o["fn"](e)
                if o["dma"]:
                    ins.then_inc(dsems[o["sem"]], 16)
                elif o["flag"]:
                    ins.then_inc(sems[ename], 1)
            if ename == "sp":
                for sidx, v in final_dma.items():
                    e.wait_ge(dsems[sidx], v)

        with nc.Block() as block:
            @block.sync
            def _(e):
                emit_engine("sp", e)

            @block.tensor
            def _(e):
                emit_engine("pe", e)

            @block.vector
            def _(e):
                emit_engine("dve", e)

            @block.scalar
            def _(e):
                emit_engine("act", e)

            @block.gpsimd
            def _(e):
                emit_engine("pool", e)
        self.stats = dict(n_ops=len(ops), n_wait=n_wait[0], cnt=cnt, ndma=ndma,
                          per_eng={k: len(v) for k, v in per_eng.items()})
        self.stack.close()


PA_NW = 0
PA_MIX = 8
PA_W0 = 56
PA_A0 = 72
PA_V0 = 88
PA_KK = 104
PA_KA = 120
PA_RK = 136
PA_LW = 152
PA_LB = 168
NPAR_A = 184


class Cfg:
    def __init__(self, nseq=2, seq=2048, ns=16, npages=64, npool=10240, phases=("rwkv", "mla")):
        self.nseq, self.seq, self.ns, self.npages, self.npool = nseq, seq, ns, npages, npool
        self.ptiles = nseq * seq // TT
        self.tps = seq // TT
        self.stiles = ns // NCH
        self.ntiles = self.ptiles + self.stiles
        self.nstate = nseq + ns
        self.phases = phases


PB_KVN = 0
PB_BN = 8
PB_FN = 24
PB_QN = 32
NPAR_B = 40
QW = BH * (QK_NOPE + QK_ROPE)
ZW = Q_LORA + DI


def mla_phase(P, nc, cfg, env):
    din, dout, dscr = env["din"], env["dout"], env["dscr"]
    xa, xb = env["xa"], env["xb"]
    ident_f, ident_b, ones_b = env["ident_f"], env["ident_b"], env["ones_b"]
    NS, SEQ, NSEQ = cfg.ns, cfg.seq, cfg.nseq
    NPG = cfg.npages
    par_b_d = din("par_b", [128, NPAR_B])
    kvnw_d = din("kvnw_bc", [128, KV_LORA])
    rope_tm_d = din("rope_tm", [SEQ, 64])
    rope_fm_d = din("rope_fm", [2, 64, SEQ])
    rope_s_tm_d = din("rope_s_tm", [16, 64])
    rope_s_fm_d = din("rope_s_fm", [64, 2])
    cm_d = din("cmask", [128, 2 * TT])
    iota_d = din("iota_p", [128, 1])
    wdkv_d = din("wdkv_f", [128, KC * 320])
    wuk_d = din("wuk_f", [128, 2 * BH * 128])
    wukT_d = din("wukT_f", [128, BH * KV_LORA])
    wuv_d = din("wuv_f", [128, 2 * BH * 128])
    win_d = din("bwin_f", [NB, 128, KC * ZW])
    wuq_d = din("bwuq_f", [NB, 128, 3 * QW])
    wout_d = din("bwout_f", [NB, 128, BH * D])
    pt_d = din("page_table", [NS, NPG], I32)
    cckv_d = din("cache_ckv", [cfg.npool * PAGE, KV_LORA])
    ckpe_d = din("cache_kpe", [cfg.npool * PAGE, QK_ROPE])
    y_p = dout("y_p", [cfg.ptiles, 128, KC, TT])
    y_s = dout("y_s", [128, KC, NS])
    ckv_p = dout("ckv_p", [NSEQ * SEQ, KV_LORA])
    kpe_p = dout("kpe_p", [NSEQ * SEQ, QK_ROPE])
    ckv_s = dout("ckv_s", [NS, KV_LORA])
    kpe_s = dout("kpe_s", [NS, QK_ROPE])
    xs_scr = dscr("xs_scr", [128, KC, NS])
    cs_scr = dscr("cs_scr", [NS, KV_LORA])

    P.barrier()
    P.off = env["base_off"]
    al = P.alloc
    parb = al("parb", [128, NPAR_B])
    kvnw = al("kvnw", [128, KV_LORA])
    iota_p = al("iota_pf", [128, 1])
    win_b = al("win_b", [128, KC, ZW], BF16)
    wuq_b = al("wuq_b", [128, 3, QW], BF16)
    wuqR = al("wuqR", [128, 3, BH, 64], BF16)
    wout_b = al("wout2_b", [128, BH, D], BF16)
    wdkv_b = al("wdkv_b", [128, KC, 320], BF16)
    wuv_b = al("wuv_b", [128, 2, BH, 128], BF16)
    xt = al("m_xt", [128, KC, TT])
    sqb = al("m_sqb", [128, KC, TT], BF16)
    hnb = al("m_hnb", [128, KC, TT], BF16)
    rstd = al("m_rstd", [128, TT])
    cq = al("m_cq", [128, 3, TT])
    cqb = al("m_cqb", [128, 3, TT], BF16)
    zs = al("m_zs", [128, BH, TT], BF16)
    Qn = al("m_Qn", [128, TT], BF16)
    Qr = al("m_Qr", [64, TT], BF16)
    qa = al("m_qa", [64, TT])
    qb = al("m_qb", [64, TT])
    rden = al("m_rden", [128, TT])
    tmpo = al("m_tmpo", [128, TT])
    small = al("m_small", [128, 8])
    P.dma("sp", parb, par_b_d)
    P.dma("sp", kvnw, kvnw_d)
    P.dma("sp", iota_p, iota_d)
    P.dma("pool", wdkv_b.re("p k f -> p (k f)"), wdkv_d, max_dma_last_dim=4096)
    P.dma("pool", wuv_b.re("p c h v -> p (c h v)"), wuv_d, max_dma_last_dim=4096)
    pbc = lambda col: parb[:, col:col + 1]
    mark0 = P.off

    def rmsnorm_fm(x3, n, wcol, out3):
        P.act(sqb[:, :, 0:n], x3, AF.Square)
        ss = P.bank(2)[:, 0:n]
        for kc in range(KC):
            P.mm(ss, ones_b, sqb[:, kc, 0:n], start=kc == 0, stop=kc == KC - 1)
        P.act(rstd[:, 0:n], ss, AF.Sqrt, bias=NORM_EPS, scale=1.0 / D)
        P.recip(rstd[:, 0:n], rstd[:, 0:n])
        for kc in range(KC):
            P.stt(out3[:, kc, :], x3[:, kc, :], pbc(wcol + kc), rstd[:, 0:n], ALU.mult, ALU.mult)

    def q_side(l, n):
        for j in range(3):
            o = P.bank(j % 2)[:, 0:n]
            for kc in range(KC):
                P.mm(o, win_b[:, kc, j * 128:(j + 1) * 128], hnb[:, kc, 0:n], start=kc == 0, stop=kc == KC - 1)
            P.copy("act", cq[:, j, 0:n], o)
        P.act(sqb[:, 0:3, 0:n], cq[:, :, 0:n], AF.Square)
        ss = P.bank(2)[:, 0:n]
        for j in range(3):
            P.mm(ss, ones_b, sqb[:, j, 0:n], start=j == 0, stop=j == 2)
        P.act(rstd[:, 0:n], ss, AF.Sqrt, bias=NORM_EPS, scale=1.0 / Q_LORA)
        P.recip(rstd[:, 0:n], rstd[:, 0:n])
        for j in range(3):
            P.stt(cqb[:, j, 0:n], cq[:, j, 0:n], pbc(PB_QN + l * 3 + j), rstd[:, 0:n], ALU.mult, ALU.mult)
        for fc in range(BH):
            o = P.bank(fc % 2)[:, 0:n]
            for kc in range(KC):
                P.mm(o, win_b[:, kc, Q_LORA + fc * 128:Q_LORA + (fc + 1) * 128], hnb[:, kc, 0:n],
                     start=kc == 0, stop=kc == KC - 1)
            P.act(zs[:, fc, 0:n], o, AF.Silu)

    def q_head(h, n, cos_t, sin_t):
        o = P.bank(0)[:, 0:n]
        for j in range(3):
            P.mm(o, wuq_b[:, j, h * 192:h * 192 + 128], cqb[:, j, 0:n], start=j == 0, stop=j == 2)
        P.ts("dve", Qn[:, 0:n], o, ATTN_SCALE)
        oa = P.bank(1)[0:64, 0:n]
        ob = P.bank(1)[0:64, TT:TT + n]
        for j in range(3):
            P.mm(oa, wuq_b[:, j, h * 192 + 128:h * 192 + 192], cqb[:, j, 0:n], start=j == 0, stop=j == 2)
        for j in range(3):
            P.mm(ob, wuqR[:, j, h, :], cqb[:, j, 0:n], start=j == 0, stop=j == 2)
        if isinstance(cos_t, tuple):
            P.ts("dve", qa[:, 0:n], oa, cos_t[0])
            P.stt(Qr[:, 0:n], ob, sin_t[0], qa[:, 0:n], ALU.mult, ALU.add)
        else:
            P.tt("dve", qa[:, 0:n], oa, cos_t, ALU.mult)
            P.tt("dve", qb[:, 0:n], ob, sin_t, ALU.mult)
            P.tt("pool", Qr[:, 0:n], qa[:, 0:n], qb[:, 0:n], ALU.add)

    def kv_tm(n, ps, rope_t, ckv_t, kpe_t):
        ssq = small[0:n, 0:1]
        rs = small[0:n, 1:2]
        P.act_acc(ckv_t, ps[:, 0:256], AF.Square, ssq)
        P.act(rs, ssq, AF.Sqrt, bias=NORM_EPS, scale=1.0 / KV_LORA)
        P.recip(rs, rs)
        P.stt(ckv_t, ps[:, 0:256], rs, kvnw[0:n, :], ALU.mult, ALU.mult)
        x1, x2 = ps[:, 256:288], ps[:, 288:320]
        cs, sn = rope_t[:, 0:32], rope_t[:, 32:64]
        t1 = tmpo[0:n, 0:32]
        t2 = tmpo[0:n, 32:64]
        P.tt("dve", t1, x1, cs, ALU.mult)
        P.tt("dve", t2, x2, sn, ALU.mult)
        P.tt("dve", kpe_t[:, 0:32], t1, t2, ALU.subtract)
        P.tt("dve", t1, x2, cs, ALU.mult)
        P.tt("dve", t2, x1, sn, ALU.mult)
        P.tt("dve", kpe_t[:, 32:64], t1, t2, ALU.add)

    for l in range(NB):
        src = xb if l == 0 else xa
        skey = "D_xb_%d" if l == 0 else "D_xa_%d"
        for kc in range(KC):
            P.dma("pool", win_b[:, kc, :], win_d[l][:, kc * ZW:(kc + 1) * ZW], max_dma_last_dim=4096)
        for j in range(3):
            P.dma("pool", wuq_b[:, j, :], wuq_d[l][:, j * QW:(j + 1) * QW], max_dma_last_dim=4096)
        for hq in range(4):
            P.dma("pool", wout_b[:, hq * 4:(hq + 1) * 4, :].re("p a f -> p (a f)"),
                  wout_d[l][:, hq * 4 * D:(hq + 1) * 4 * D], max_dma_last_dim=4096)
        wq4 = wuq_b.re("p j (h e) -> p j h e", e=192)
        P.ts("dve", wuqR[:, :, :, 0:32], wq4[:, :, :, 160:192], -1.0)
        P.copy("dve", wuqR[:, :, :, 32:64], wq4[:, :, :, 128:160])
        P.off = mark0
        sfx = "_%d" % l
        wuk_b = al("wuk_b" + sfx, [128, 2, BH, 128], BF16)
        ckvT = al("ckvT" + sfx, [128, 2, SEQ], BF16)
        kpeT = al("kpeT" + sfx, [64, SEQ], BF16)
        KhT = al("KhT" + sfx, [128, SEQ], BF16)
        Vh = al("Vh" + sfx, [128, SEQ // 128, 128], BF16)
        pTb = [al("pT%d" % i + sfx, [128, 2 * TT], BF16) for i in range(2)]
        cm = al("cm" + sfx, [128, 2, TT])
        cosf = al("cosf" + sfx, [64, TT])
        sinf = al("sinf" + sfx, [64, TT])
        ropet = al("ropet" + sfx, [128, 64])
        ckv_t = al("ckv_t" + sfx, [128, KV_LORA])
        kpe_t = al("kpe_t" + sfx, [128, 64])
        P.dma("pool", wuk_b.re("p c h n -> p (c h n)"), wuk_d, max_dma_last_dim=4096)
        P.dma("sp", cm.re("p j t -> p (j t)"), cm_d)
        for s_ in range(NSEQ):
            for tis in range(cfg.tps):
                ti = s_ * cfg.tps + tis
                P.dma("sp", xt, xb[ti].k("D_xb_%d" % ti))
                rmsnorm_fm(xt, TT, PB_KVN, hnb)
                for tc in range(TT // 128):
                    pos = tis * TT + tc * 128
                    ps = P.bank(0)[:, 0:320]
                    for kc in range(KC):
                        P.mm(ps, hnb[:, kc, tc * 128:(tc + 1) * 128], wdkv_b[:, kc, :], start=kc == 0, stop=kc == KC - 1)
                    P.dma("sp", ropet, rope_tm_d[pos:pos + 128])
                    kv_tm(128, ps, ropet, ckv_t, kpe_t)
                    if l == 0:
                        r0 = s_ * SEQ + pos
                        P.dma("sp", ckv_p[r0:r0 + 128].k("D_ckvp_%d" % r0), ckv_t)
                        P.dma("sp", kpe_p[r0:r0 + 128].k("D_kpep_%d" % r0), kpe_t)
                    pt_ = P.bank(1)
                    P.tr(pt_[:, 0:128], ckv_t[:, 0:128], ident_f)
                    P.tr(pt_[:, 128:256], ckv_t[:, 128:256], ident_f)
                    P.tr(pt_[0:64, 256:384], kpe_t, ident_f)
                    P.copy("act", ckvT[:, 0, pos:pos + 128], pt_[:, 0:128])
                    P.copy("act", ckvT[:, 1, pos:pos + 128], pt_[:, 128:256])
                    P.copy("act", kpeT[:, pos:pos + 128], pt_[0:64, 256:384])
            for tis in range(cfg.tps):
                ti = s_ * cfg.tps + tis
                P.dma("sp", xt, src[ti].k(skey % ti))
                rmsnorm_fm(xt, TT, PB_BN + l * 8, hnb)
                q_side(l, TT)
                P.dma("sp", cosf, rope_fm_d[0][:, tis * TT:(tis + 1) * TT])
                P.dma("sp", sinf, rope_fm_d[1][:, tis * TT:(tis + 1) * TT])
                P.ts("pool", cosf, cosf, ATTN_SCALE)
                P.ts("pool", sinf, sinf, ATTN_SCALE)
                kend = (tis + 1) * TT
                nsc = kend // 128
                for h in range(BH):
                    q_head(h, TT, cosf, sinf)
                    for kb in range((kend + 511) // 512):
                        w_ = min(512, kend - kb * 512)
                        o = P.bank(7)[:, 0:w_]
                        for cc in range(2):
                            P.mm(o, wuk_b[:, cc, h, :], ckvT[:, cc, kb * 512:kb * 512 + w_], start=cc == 0, stop=cc == 1)
                        P.copy("act", KhT[:, kb * 512:kb * 512 + w_], o)
                    for g0 in range(0, nsc, 4):
                        g1 = min(nsc, g0 + 4)
                        ob_ = P.bank(2)
                        for sc in range(g0, g1):
                            for cc in range(2):
                                P.mm(ob_[:, (sc - g0) * 128:(sc - g0 + 1) * 128], ckvT[:, cc, sc * 128:(sc + 1) * 128],
                                     wuv_b[:, cc, h, :], start=cc == 0, stop=cc == 1)
                        P.copy("dve", Vh[:, g0:g1, :].re("p a v -> p (a v)"), ob_[:, 0:(g1 - g0) * 128])
                    oT = P.bank(5)[:, 0:TT]
                    den = P.bank(6)[:, 0:TT]
                    npair = nsc // 2

                    def scores(pi_):
                        for e_ in range(2):
                            sc = 2 * pi_ + e_
                            sb_ = P.bank(3 + pi_ % 2)[:, e_ * TT:(e_ + 1) * TT]
                            P.mm(sb_, KhT[:, sc * 128:(sc + 1) * 128], Qn, start=True, stop=False)
                            P.mm(sb_, kpeT[:, sc * 128:(sc + 1) * 128], Qr, start=False, stop=True)
                    scores(0)
                    for pi_ in range(npair):
                        if pi_ + 1 < npair:
                            scores(pi_ + 1)
                        pT = pTb[pi_ % 2]
                        P.act(pT, P.bank(3 + pi_ % 2)[:, 0:2 * TT], AF.Exp)
                        if pi_ == npair - 1:
                            P.tt("pool", pT, pT, cm.re("p j t -> p (j t)"), ALU.mult)
                        for e_ in range(2):
                            sc = 2 * pi_ + e_
                            P.mm(oT, Vh[:, sc, :], pT[:, e_ * TT:(e_ + 1) * TT], start=sc == 0, stop=sc == nsc - 1)
                            P.mm(den, ones_b, pT[:, e_ * TT:(e_ + 1) * TT], start=sc == 0, stop=sc == nsc - 1)
                    P.recip(rden, den)
                    P.tt("dve", tmpo, oT, rden, ALU.mult)
                    P.tt("pool", zs[:, h, :], tmpo, zs[:, h, :], ALU.mult)
                for oc in range(KC):
                    o = P.bank(oc % 2)[:, 0:TT]
                    for h in range(BH):
                        P.mm(o, wout_b[:, h, oc * 128:(oc + 1) * 128], zs[:, h, :], start=h == 0, stop=h == BH - 1)
                    P.tt("dve", xt[:, oc, :], xt[:, oc, :], o, ALU.add)
                if l == NB - 1:
                    rmsnorm_fm_f32(P, xt, TT, PB_FN, pbc, sqb, rstd, ones_b)
                    P.dma("sp", y_p[ti], xt)
                else:
                    P.dma("sp", xa[ti].k("D_xa_%d" % ti), xt)
        P.barrier()
        P.off = mark0
        wukT_b = al("wukT_b" + sfx, [128, BH, KV_LORA], BF16)
        xs = al("xs" + sfx, [128, KC, NS])
        xkv = al("xkv" + sfx, [128, KC, NS])
        ropes = al("ropes" + sfx, [16, 64])
        ropesf = al("ropesf" + sfx, [64, 2])
        ckv_st = al("ckv_st" + sfx, [NS, KV_LORA])
        kpe_st = al("kpe_st" + sfx, [NS, 64])
        ckv_sT = al("ckv_sT" + sfx, [128, 2, NS], BF16)
        kpe_sT = al("kpe_sT" + sfx, [64, NS], BF16)
        cs1 = al("cs1" + sfx, [1, KV_LORA])
        cs1b = al("cs1b" + sfx, [1, KV_LORA + 8], BF16)
        qnb = al("qnb" + sfx, [128, NS], BF16)
        Qr_all = al("Qr_all" + sfx, [64, BH, NS], BF16)
        qlat = al("qlat" + sfx, [128, 2, BH, NS], BF16)
        olatT = al("olatT" + sfx, [128, 2, BH, NS], BF16)
        olat_n = al("olat_n" + sfx, [16, KV_LORA])
        ptb = al("ptb" + sfx, [128, NPG], I32)
        idx = al("idx" + sfx, [128, NPG], I32)
        pg_c = [al("pg_c%d" % i + sfx, [128, 4, KV_LORA]) for i in range(2)]
        pg_k = [al("pg_k%d" % i + sfx, [128, 4, 64]) for i in range(2)]
        cb = [al("cb%d" % i + sfx, [128, 4, KV_LORA + 1], BF16) for i in range(2)]
        kb_ = [al("kb%d" % i + sfx, [128, 4, 64], BF16) for i in range(2)]
        cT = al("cT" + sfx, [128, 4, 2, 128], BF16)
        kT = al("kT" + sfx, [64, 4, 128], BF16)
        pTs = al("pTs" + sfx, [128, 4, BH], BF16)
        pself = al("pself" + sfx, [1, BH], BF16)
        P.dma("pool", wukT_b.re("p h c -> p (h c)"), wukT_d, max_dma_last_dim=4096)
        P.dma("sp", ropes, rope_s_tm_d)
        P.dma("sp", ropesf, rope_s_fm_d)
        P.ts("dve", ropesf, ropesf, ATTN_SCALE)
        for i in range(2):
            P.memset("dve", cb[i][:, :, KV_LORA:KV_LORA + 1], 1.0)
        P.memset("dve", cs1b[:, KV_LORA:KV_LORA + 1], 1.0)
        for st in range(cfg.stiles):
            ti = cfg.ptiles + st
            P.dma("sp", xt, xb[ti].k("D_xb_%d" % ti))
            P.copy("dve", xkv[:, :, st * NCH:(st + 1) * NCH], xt.re("p k (r c) -> p k r c", c=C)[:, :, :, 0])
        if l == 0:
            P.copy("dve", xs, xkv)
        else:
            P.dma("sp", xs, xs_scr)
        rmsnorm_fm(xkv, NS, PB_KVN, hnb[:, :, 0:NS])
        ps = P.bank(0)[0:NS, 0:320]
        for kc in range(KC):
            P.mm(ps, hnb[:, kc, 0:NS], wdkv_b[:, kc, :], start=kc == 0, stop=kc == KC - 1)
        kv_tm(NS, ps, ropes[0:NS], ckv_st, kpe_st)
        if l == 0:
            P.dma("sp", ckv_s, ckv_st)
            P.dma("sp", kpe_s, kpe_st)
        P.dma("sp", cs_scr.k("D_cs_scr%d" % l), ckv_st)
        pt_ = P.bank(1)
        idn = ident_f[0:NS, 0:NS]
        P.tr(pt_[:, 0:NS], ckv_st[:, 0:128], idn)
        P.tr(pt_[:, NS:2 * NS], ckv_st[:, 128:256], idn)
        P.tr(pt_[0:64, 2 * NS:3 * NS], kpe_st, idn)
        P.copy("act", ckv_sT[:, 0, :], pt_[:, 0:NS])
        P.copy("act", ckv_sT[:, 1, :], pt_[:, NS:2 * NS])
        P.copy("act", kpe_sT, pt_[0:64, 2 * NS:3 * NS])
        rmsnorm_fm(xs, NS, PB_BN + l * 8, hnb[:, :, 0:NS])
        q_side(l, NS)
        for h in range(BH):
            q_head(h, NS, (ropesf[:, 0:1],), (ropesf[:, 1:2],))
            P.copy("act", Qr_all[:, h, :], Qr[:, 0:NS])
            for cc in range(2):
                o = P.bank(2)[:, cc * NS:(cc + 1) * NS]
                P.mm(o, wukT_b[:, h, cc * 128:(cc + 1) * 128], Qn[:, 0:NS])
            P.copy("dve", qlat[:, :, h, :], P.bank(2)[:, 0:2 * NS].re("p (c r) -> p c r", c=2))
        for r in range(NS):
            P.dma("sp", ptb, T(pt_d.ap[r].partition_broadcast(128), pt_d.key))
            P.ts("dve", idx, ptb, float(PAGE), iota_p, op0=ALU.mult, op1=ALU.add)
            ol = P.bank(5)[0:BH, 0:KV_LORA + 1]
            ng = NPG // 4
            for g in range(ng):
                b = g % 2
                for j in range(4):
                    pg = g * 4 + j
                    P.gather(pg_c[b][:, j, :], cckv_d, idx[:, pg:pg + 1])
                    P.gather(pg_k[b][:, j, :], ckpe_d, idx[:, pg:pg + 1])
                P.copy("dve", cb[b][:, :, 0:KV_LORA], pg_c[b])
                P.copy("act", kb_[b], pg_k[b])
                t3 = P.bank(3).bf()
                t4 = P.bank(4).bf()
                for j in range(4):
                    for cc in range(2):
                        P.tr(t3[:, (j * 2 + cc) * 128:(j * 2 + cc + 1) * 128], cb[b][:, j, cc * 128:(cc + 1) * 128], ident_b)
                    P.tr(t4[0:64, j * 128:(j + 1) * 128], kb_[b][:, j, :], ident_b)
                P.copy("act", cT.re("p j c t -> p (j c t)"), t3)
                P.copy("dve", kT.re("p j t -> p (j t)"), t4[0:64, 0:512])
                sc_ = P.bank(6)
                for j in range(4):
                    o = sc_[:, j * BH:(j + 1) * BH]
                    P.mm(o, cT[:, j, 0, :], qlat[:, 0, :, r], start=True, stop=False)
                    P.mm(o, cT[:, j, 1, :], qlat[:, 1, :, r], start=False, stop=False)
                    P.mm(o, kT[:, j, :], Qr_all[:, :, r], start=False, stop=True)
                P.act(pTs.re("p j h -> p (j h)"), sc_[:, 0:4 * BH], AF.Exp)
                for j in range(4):
                    P.mm(ol, pTs[:, j, :], cb[b][:, j, :], start=(g == 0 and j == 0), stop=False)
            so = P.bank(6)[0:1, 64:64 + BH]
            P.mm(so, ckv_sT[:, 0, r:r + 1], qlat[:, 0, :, r], start=True, stop=False)
            P.mm(so, ckv_sT[:, 1, r:r + 1], qlat[:, 1, :, r], start=False, stop=False)
            P.mm(so, kpe_sT[:, r:r + 1], Qr_all[:, :, r], start=False, stop=True)
            P.act(pself, so, AF.Exp)
            P.dma("sp", cs1, cs_scr.k("D_cs_scr%d" % l)[r:r + 1, :])
            P.copy("dve", cs1b[:, 0:KV_LORA], cs1)
            P.mm(ol, pself, cs1b[:, 0:KV_LORA + 1], start=False, stop=True)
            rd = small[0:BH, 2:3]
            P.recip(rd, ol[:, KV_LORA:KV_LORA + 1])
            P.ts("dve", olat_n, ol[:, 0:KV_LORA], rd)
            p7 = P.bank(7)
            idh = ident_f[0:BH, 0:BH]
            P.tr(p7[:, 0:BH], olat_n[:, 0:128], idh)
            P.tr(p7[:, BH:2 * BH], olat_n[:, 128:256], idh)
            P.copy("act", olatT[:, :, :, r], p7[:, 0:2 * BH].re("p (c h) -> p c h", c=2))
        for h in range(BH):
            o = P.bank(0)[:, 0:NS]
            for cc in range(2):
                P.mm(o, wuv_b[:, cc, h, :], olatT[:, cc, h, :], start=cc == 0, stop=cc == 1)
            P.tt("dve", zs[:, h, 0:NS], o, zs[:, h, 0:NS], ALU.mult)
        for oc in range(KC):
            o = P.bank(oc % 2)[:, 0:NS]
            for h in range(BH):
                P.mm(o, wout_b[:, h, oc * 128:(oc + 1) * 128], zs[:, h, 0:NS], start=h == 0, stop=h == BH - 1)
            P.tt("dve", xs[:, oc, :], xs[:, oc, :], o, ALU.add)
        if l == NB - 1:
            rmsnorm_fm_f32(P, xs, NS, PB_FN, pbc, sqb, rstd, ones_b)
            P.dma("sp", y_s, xs)
        else:
            P.dma("sp", xs_scr, xs)
        P.barrier()


def rmsnorm_fm_f32(P, x3, n, wcol, pbc, sqb, rstd, ones_b):
    P.act(sqb[:, :, 0:n], x3, AF.Square)
    ss = P.bank(2)[:, 0:n]
    for kc in range(KC):
        P.mm(ss, ones_b, sqb[:, kc, 0:n], start=kc == 0, stop=kc == KC - 1)
    P.act(rstd[:, 0:n], ss, AF.Sqrt, bias=NORM_EPS, scale=1.0 / D)
    P.recip(rstd[:, 0:n], rstd[:, 0:n])
    for kc in range(KC):
        P.stt(x3[:, kc, :], x3[:, kc, :], pbc(wcol + kc), rstd[:, 0:n], ALU.mult, ALU.mult)


def build(cfg):
    import os as _os
    nc = bass.Bass("TRN2", target_bir_lowering=False)
    P = Prog(nc)
    NT = cfg.ntiles

    def din(name, shape, dt=F32):
        return T(nc.dram_tensor(name, list(shape), dt, kind="ExternalInput").ap(), "D_" + name)

    def dout(name, shape, dt=F32):
        return T(nc.dram_tensor(name, list(shape), dt, kind="ExternalOutput").ap(), "D_" + name)

    def dscr(name, shape, dt=F32):
        return T(nc.dram_tensor(name, list(shape), dt, kind="Internal").ap(), "D_" + name)

    x_in = din("x_in", [NT, 128, KC, TT])
    consts_f = din("consts_f", [128, 128 + 3 * TT + 2 * TT])
    par_a = din("par_a", [NA, 128, NPAR_A])
    w_in_f = din("w_in_f", [NA, FC, 128, 4 * KC * 128])
    w1_f = din("w1_f", [NA, 128, KC * 64])
    a1_f = din("a1_f", [NA, 128, KC * 64])
    v1_f = din("v1_f", [128, KC * 32])
    w2_f = din("w2_f", [NA, 64, DI])
    a2_f = din("a2_f", [NA, 64, DI])
    v2_f = din("v2_f", [32, DI])
    wout_f = din("wout_f", [NA, 128, FC * D])
    st_wkv = din("st_wkv", [NA, cfg.stiles, FC, 128, NCH, 64])
    st_shift = din("st_shift", [NA, cfg.stiles, 128, KC, NCH])

    xa = dscr("xa", [NT, 128, KC, TT])
    xb = dscr("xb", [NT, 128, KC, TT])
    vfirst = dscr("vfirst", [NT, FC, 128, TT])
    w_in_b = dscr("w_in_b", [NA, FC, 128, 4 * KC * 128], BF16)

    wkv_out = dout("wkv_out", [NA, cfg.nstate, FC, 128, 64])
    shift_out = dout("shift_out", [NA, cfg.nstate, 128, KC])
    y_dbg = dout("y_dbg", [NT, 128, KC, TT]) if "mla" not in cfg.phases else None

    cf = P.alloc("cf", [128, 128 + 5 * TT])
    ident_f = cf[:, 0:128]
    m_lt = cf[:, 128:128 + TT].re("p (c t) -> p c t", c=NCH)
    m_gt = cf[:, 128 + TT:128 + 2 * TT].re("p (c t) -> p c t", c=NCH)
    m_le = cf[:, 128 + 2 * TT:128 + 3 * TT].re("p (c t) -> p c t", c=NCH)
    scanmask = cf[:, 128 + 3 * TT:128 + 4 * TT]
    colmask = cf[:, 128 + 4 * TT:128 + 5 * TT]
    ident_b = P.alloc("ident_b", [128, 128], BF16)
    ones_b = P.alloc("ones_b", [128, 128], BF16)
    bones_b = P.alloc("bones_b", [128, 128], BF16)
    eye_st = P.alloc("eye_st", [128, NCH, 64], BF16)
    bd_eye = P.alloc("bd_eye", [128, NCH, 128], BF16)
    P.dma("sp", cf, consts_f)
    P.copy("dve", ident_b, ident_f)
    P.memset("dve", ones_b, 1.0)
    P.memset("dve", bones_b, 0.0)
    P.memset("dve", bones_b[0:64, 0:64], 1.0)
    P.memset("dve", bones_b[64:128, 64:128], 1.0)
    P.memset("dve", bd_eye, 0.0)
    P.memset("dve", eye_st, 0.0)
    for c in range(NCH):
        P.copy("dve", bd_eye[:, c, :], ident_f)
        P.copy("dve", eye_st[0:64, c, :], ident_f[0:64, 0:64])
        P.copy("dve", eye_st[64:128, c, :], ident_f[64:128, 64:128])

    P.mark("consts_done")
    base_off = P.off
    for l in range(NA):
        for fc in range(FC):
            P.dma("pool", w_in_b[l, fc].k("D_winb_%d_%d" % (l, fc)), w_in_f[l, fc], max_dma_last_dim=4096)

    P.mark("prologue_done")
    par = P.alloc("par", [128, NPAR_A])
    w1b = P.alloc("w1b", [128, KC, 64], BF16)
    a1b = P.alloc("a1b", [128, KC, 64], BF16)
    v1b = P.alloc("v1b", [128, KC, 32], BF16)
    w2b = P.alloc("w2b", [64, DI], BF16)
    a2b = P.alloc("a2b", [64, DI], BF16)
    v2b = P.alloc("v2b", [32, DI], BF16)
    woutb = P.alloc("woutb", [128, FC, D], BF16)
    wbuf = [P.alloc("wbuf%d" % i, [128, 4, KC, 128], BF16) for i in range(2)]
    xt = P.alloc("xt", [128, KC, TT])
    hn = P.alloc("hn", [128, KC, TT])
    dx = P.alloc("dx", [128, KC, TT], BF16)
    hprev = P.alloc("hprev", [128, KC])
    shst = P.alloc("shst", [128, KC, NCH])
    shcp = P.alloc("shcp", [128, NCH, KC])
    rstd = P.alloc("rstd", [128, TT])
    mx = [P.alloc("mx%d" % p, [128, KC, TT], BF16) for p in range(6)]
    G = P.alloc("G", [128, FC, TT], BF16)
    hwb = P.alloc("hwb", [64, TT], BF16)
    hab = P.alloc("hab", [64, TT], BF16)
    hvb = P.alloc("hvb", [32, TT], BF16)
    S_f = P.alloc("S_f", [128, FC, 64])
    S_b = P.alloc("S_b", [128, FC, 64], BF16)
    Sin = [P.alloc("Sin%d" % i, [128, NCH, 64]) for i in range(2)]
    Sinb = [P.alloc("Sinb%d" % i, [128, NCH, 64], BF16) for i in range(2)]
    Sout = [P.alloc("Sout%d" % i, [128, NCH, 64]) for i in range(2)]
    pn = ["iclr", "sg", "zs", "r_f", "k_f", "v_f", "vg", "vf", "kkr", "rn", "kkn", "tq", "kp", "bv",
          "cum", "e_inc", "e_inv", "e_exc", "e_cl", "tmp", "yc", "ysq", "yf", "bon"]
    pr = {n: P.alloc(n, [128, TT]) for n in pn}
    sqk = P.alloc("sqk", [128, TT], BF16)
    rkb = P.alloc("rkb", [128, TT], BF16)
    stat = P.alloc("stat", [128, 8, NCH])
    sn_ = ["StR", "StB", "StA", "StN", "StNT", "StT", "StV", "StN2", "StNT2"]
    St = {n: P.alloc(n, [128, NCH, 64], BF16) for n in sn_}
    StP = P.alloc("StP", [128, 64], BF16)
    StU = P.alloc("StU", [128, 64], BF16)
    bn_ = ["BdR", "BdK", "BdB", "BdA", "BdMak", "BdMrb", "BdMrk", "BdT", "BdN", "BdNT", "BdN2", "BdNT2",
           "BdKh", "BdBh", "BdV", "BdKhT", "BdBhT", "BdY"]
    Bd = {n: P.alloc(n, [128, NCH, 128], BF16) for n in bn_}
    for n in bn_:
        if n not in ("BdKhT", "BdBhT"):
            P.memset("pool", Bd[n], 0.0)
    pr0, St0, Bd0 = pr, St, Bd
    BdQ = [dict(), dict()]
    for n in ("BdA", "BdR", "BdMak", "BdMrb", "BdMrk", "BdT", "BdKhT", "BdBhT"):
        BdQ[0][n] = Bd0[n]
        BdQ[1][n] = P.alloc(n + "_1", [128, NCH, 128], BF16)
        if n not in ("BdKhT", "BdBhT"):
            P.memset("pool", BdQ[1][n], 0.0)
    StVq = [St0["StV"], P.alloc("StV_1", [128, NCH, 64], BF16)]
    prq = [dict(), dict()]
    for n in ("e_inc", "r_f", "kp", "v_f", "zs"):
        prq[0][n] = pr0[n]
        prq[1][n] = P.alloc(n + "_1", [128, TT])
    print("sbuf words used (RWKV phase):", P.off)

    def capture(fn):
        saved = P.ops
        P.ops = []
        fn()
        out = P.ops
        P.ops = saved
        return out

    def merge_ops(a, b):
        if not b:
            return list(a)
        if not a:
            return list(b)
        out = []
        step = len(a) / float(len(b) + 1)
        nxt = step
        bi = 0
        for i, o in enumerate(a):
            out.append(o)
            while bi < len(b) and i + 1 >= nxt:
                out.append(b[bi])
                bi += 1
                nxt += step
        out.extend(b[bi:])
        return out

    def c3(t):
        return t.re("p (c t) -> p c t", c=NCH)

    def bd_halves(dst, src, engs=("act", "pool")):
        P.copy(engs[0], dst[0:64, :, 0:64], src[0:64])
        P.copy(engs[1], dst[64:128, :, 64:128], src[64:128])

    def rwkv_tile(l, ti, src, dst):
        sample = ti >= cfg.ptiles
        if sample:
            sti = ti - cfg.ptiles
        else:
            seq_i, tis = divmod(ti, cfg.tps)
        pa = lambda col, j=0: par[:, col + j:col + j + 1]
        bA = P.bank(0, "b0")[:, 0:TT]
        P.mark("tile_%d_%d_start" % (l, ti))
        P.dma("sp", xt, src[ti].k(src.key + "_%d" % ti))
        sq = mx[0]
        P.act(sq, xt, AF.Square)
        ssb = P.bank(3, "b3")[:, TT:2 * TT]
        for kc in range(KC):
            P.mm(ssb, ones_b, sq[:, kc, :], start=kc == 0, stop=kc == KC - 1)
        P.act(rstd, ssb, AF.Sqrt, bias=NORM_EPS, scale=1.0 / D)
        P.recip(rstd, rstd)
        for kc in range(KC):
            P.stt(hn[:, kc, :], xt[:, kc, :], pa(PA_NW, kc), rstd, ALU.mult, ALU.mult)
        P.mark("t%d_%d_shift" % (l, ti))
        if not sample:
            if tis == 0:
                P.memset("dve", hprev, 0.0)
            P.tt("dve", dx[:, :, 1:TT], hn[:, :, 0:TT - 1], hn[:, :, 1:TT], ALU.subtract)
            P.tt("dve", dx[:, :, 0:1], hprev.re("p (k o) -> p k o", o=1), hn[:, :, 0:1], ALU.subtract)
            P.copy("dve", hprev.re("p (k o) -> p k o", o=1), hn[:, :, TT - 1:TT])
            if tis == cfg.tps - 1:
                P.dma("sp", shift_out[l, seq_i], hprev)
        else:
            hn0 = hn.re("p k (r c) -> p k r c", c=C)[:, :, :, 0]
            dx0 = dx.re("p k (r c) -> p k r c", c=C)[:, :, :, 0]
            P.dma("sp", shst, st_shift[l, sti])
            P.memset("pool", dx, 0.0)
            P.tt("dve", dx0, shst, hn0, ALU.subtract)
            P.copy("dve", shcp.re("p r k -> p k r"), hn0)
            P.dma("sp", shift_out[l, cfg.nseq + sti * NCH: cfg.nseq + (sti + 1) * NCH].re("r p k -> p r k"), shcp)
        P.mark("t%d_%d_mixed" % (l, ti))
        for p in (4, 5, 0, 1, 2, 3):
            for kc in range(KC):
                P.stt(mx[p][:, kc, :], dx[:, kc, :], pa(PA_MIX, p * KC + kc), hn[:, kc, :], ALU.mult, ALU.add)
        b2a = P.bank(2, "b2")[0:64, 0:TT]
        b2b = P.bank(2, "b2")[0:64, TT:2 * TT]
        for kc in range(KC):
            P.mm(b2a, w1b[:, kc, :], mx[4][:, kc, :], start=kc == 0, stop=kc == KC - 1)
        P.act(hwb, b2a, AF.Tanh)
        for kc in range(KC):
            P.mm(b2b, a1b[:, kc, :], mx[5][:, kc, :], start=kc == 0, stop=kc == KC - 1)
        P.copy("act", hab, b2b)
        if l > 0:
            b3a = P.bank(3, "b3")[0:32, 0:TT]
            for kc in range(KC):
                P.mm(b3a, v1b[:, kc, :], mx[2][:, kc, :], start=kc == 0, stop=kc == KC - 1)
            P.copy("act", hvb, b3a)
        def prep(fc):
            q = fc % 2
            Bd = dict(Bd0)
            Bd.update(BdQ[q])
            St = dict(St0)
            St["StV"] = StVq[q]
            pr = dict(pr0)
            pr.update(prq[q])
            w = wbuf[fc % 2]
            P.dma("sp", w.re("p a k f -> p (a k f)"), w_in_b[l, fc].k("D_winb_%d_%d" % (l, fc)))
            fs = slice(fc * 128, (fc + 1) * 128)
            pb = {}
            for p, (bi, half) in enumerate(((0, 0), (0, 1), (1, 0), (1, 1))):
                o = P.bank(bi, "b%d%s" % (bi, "ab"[half]))[:, half * TT:(half + 1) * TT]
                pb[p] = o
                for kc in range(KC):
                    P.mm(o, w[:, p, kc, :], mx[p][:, kc, :], start=kc == 0, stop=kc == KC - 1)
            wl = P.bank(2, "b2")[:, 0:TT]
            al = P.bank(2, "b2")[:, TT:2 * TT]
            P.mm(wl, w2b[:, fs], hwb)
            P.mm(al, a2b[:, fs], hab)
            if l > 0:
                vl = P.bank(3, "b3")[:, 0:TT]
                P.mm(vl, v2b[:, fs], hvb)
            P.act(pr["iclr"], al, AF.Sigmoid, bias=pa(PA_A0, fc))
            P.act(pr["sg"], wl, AF.Sigmoid, bias=pa(PA_W0, fc))
            if sample:
                P.tt("pool", pr["sg"], pr["sg"], colmask, ALU.mult)
            P.act(pr["zs"], pb[3], AF.Silu)
            P.copy("act", pr["r_f"], pb[0])
            P.copy("dve", pr["k_f"], pb[1])
            if l == 0:
                P.copy("act", pr["v_f"], pb[2])
                P.dma("sp", vfirst[ti, fc].k("D_vf_%d_%d" % (ti, fc)), pr["v_f"])
            else:
                P.act(pr["vg"], vl, AF.Sigmoid, bias=pa(PA_V0, fc))
                P.dma("sp", pr["vf"], vfirst[ti, fc].k("D_vf_%d_%d" % (ti, fc)))
                P.tt("dve", pr["tmp"], pr["vf"], pb[2], ALU.subtract)
                P.tt("dve", pr["tmp"], pr["tmp"], pr["vg"], ALU.mult)
                P.tt("dve", pr["v_f"], pr["tmp"], pb[2], ALU.add)
            P.ts("dve", pr["kkr"], pr["k_f"], pa(PA_KK, fc))
            P.tt("pool", sqk, pr["kkr"], pr["kkr"], ALU.mult)
            ssk = P.bank(3, "b3")[:, TT:2 * TT]
            P.mm(ssk, bones_b, sqk)
            P.act(pr["rn"], ssk, AF.Sqrt)
            P.ts("dve", pr["rn"], pr["rn"], 1e-12, op0=ALU.max)
            P.recip(pr["rn"], pr["rn"])
            P.tt("dve", pr["kkn"], pr["kkr"], pr["rn"], ALU.mult)
            P.ts("dve", pr["tq"], pr["iclr"], -1.0, pa(PA_KA, fc), op0=ALU.add, op1=ALU.mult)
            P.stt(pr["kp"], pr["tq"], 1.0, pr["k_f"], ALU.add, ALU.mult)
            P.tt("pool", pr["bv"], pr["kkn"], pr["iclr"], ALU.mult)
            P.scan(pr["cum"], scanmask, pr["sg"], 0.0, ALU.mult, ALU.add)
            P.act(pr["e_inc"], pr["cum"], AF.Exp, scale=-C0)
            P.act(pr["e_inv"], pr["cum"], AF.Exp, scale=C0)
            P.tt("pool", pr["tmp"], pr["cum"], pr["sg"], ALU.subtract)
            P.act(pr["e_exc"], pr["tmp"], AF.Exp, scale=-C0)
            cum3 = c3(pr["cum"])
            P.tt("dve", c3(pr["e_cl"]), cum3[:, :, C - 1:C].bc([128, NCH, C]), cum3, ALU.subtract)
            P.act(pr["e_cl"], pr["e_cl"], AF.Exp, scale=-C0)
            P.tt("dve", St["StR"], c3(pr["r_f"]), c3(pr["e_inc"]), ALU.mult)
            bd_halves(Bd["BdR"], St["StR"])
            P.tt("dve", St["StB"], c3(pr["bv"]), c3(pr["e_inv"]), ALU.mult)
            bd_halves(Bd["BdB"], St["StB"])
            P.stt(St["StA"], c3(pr["kkn"]), -1.0, c3(pr["e_exc"]), ALU.mult, ALU.mult)
            bd_halves(Bd["BdA"], St["StA"])
            for (dst_, a_, b_) in (("BdK", "kp", "e_inv"), ("BdKh", "kp", "e_cl"), ("BdBh", "bv", "e_cl")):
                P.tt("dve", Bd[dst_][0:64, :, 0:64], c3(pr[a_])[0:64], c3(pr[b_])[0:64], ALU.mult)
                P.tt("pool", Bd[dst_][64:128, :, 64:128], c3(pr[a_])[64:128], c3(pr[b_])[64:128], ALU.mult)
            bd_halves(Bd["BdV"], c3(pr["v_f"]))
            gk = {}
            for name, (bi, half), lh, rh in (("ak", (0, 0), "BdK", "StA"), ("ab", (0, 1), "BdB", "StA"),
                                             ("abT", (1, 0), "BdA", "StB"), ("rk", (1, 1), "BdK", "StR"),
                                             ("rb", (2, 0), "BdB", "StR")):
                o = c3(P.bank(bi, "b%d%s" % (bi, "ab"[half]))[:, half * TT:(half + 1) * TT])
                gk[name] = o
                for c in range(NCH):
                    P.mm(o[:, c, :], Bd[lh][:, c, :], St[rh][:, c, :])

            def bd_masked(dst, src, mask):
                P.tt("dve", dst[0:64, :, 0:64], src[0:64], mask[0:64], ALU.mult)
                P.tt("dve", dst[64:128, :, 64:128], src[64:128], mask[64:128], ALU.mult)
            bd_masked(Bd["BdMak"], gk["ak"], m_lt)
            bd_masked(Bd["BdMrk"], gk["rk"], m_le)
            bd_masked(Bd["BdMrb"], gk["rb"], m_le)
            if not sample:
                P.tt("dve", St["StN"], gk["ab"], m_lt, ALU.mult)
                P.tt("dve", St["StNT"], gk["abT"], m_gt, ALU.mult)
                bd_halves(Bd["BdN"], St["StN"])
                bd_halves(Bd["BdNT"], St["StNT"])
                P.tt("pool", St["StT"], St["StN"], eye_st, ALU.add)
                cur = ("StN", "StNT", "BdN", "BdNT")
                nxt = ("StN2", "StNT2", "BdN2", "BdNT2")
                for lev in range(1, 6):
                    sN, sNT, bN, bNT = cur
                    nN, nNT, nbN, nbNT = nxt
                    o2 = c3(P.bank(4, "b4")[:, TT:2 * TT])
                    for c in range(NCH):
                        P.mm(o2[:, c, :], Bd[bN][:, c, :], St[sNT][:, c, :])
                    if lev < 5:
                        o1 = c3(P.bank(4, "b4")[:, 0:TT])
                        for c in range(NCH):
                            P.mm(o1[:, c, :], Bd[bNT][:, c, :], St[sN][:, c, :])
                        if _os.environ.get("E285", "act") == "ident":
                            P.act(St[nN], o1, AF.Identity)
                        else:
                            P.copy(_os.environ.get("E285", "act"), St[nN], o1)
                        P.copy("dve", St[nNT], o2)
                        bd_halves(Bd[nbN], St[nN])
                        bd_halves(Bd[nbNT], St[nNT])
                    else:
                        P.copy("dve", Bd[nbNT][0:64, :, 0:64], o2[0:64])
                        P.copy("act", Bd[nbNT][64:128, :, 64:128], o2[64:128])
                    o3 = c3(P.bank(5, "b5")[:, 0:TT])
                    for c in range(NCH):
                        P.mm(o3[:, c, :], Bd[nbNT][:, c, :], St["StT"][:, c, :])
                    P.tt("dve", St["StT"], o3, St["StT"], ALU.add)
                    cur, nxt = nxt, cur
                bd_halves(Bd["BdT"], St["StT"])
                BdT = Bd["BdT"]
            else:
                BdT = bd_eye
            t4 = P.bank(4, "b4").bf()
            t4k = T(t4.ap, "b4")
            for c in range(NCH):
                P.tr(T(t4.ap[:, c * 128:(c + 1) * 128], "b4"), Bd["BdKh"][:, c, :], ident_b)
            for c in range(NCH):
                P.tr(T(t4.ap[:, 512 + c * 128:512 + (c + 1) * 128], "b4"), Bd["BdBh"][:, c, :], ident_b)
            P.copy("act", Bd["BdKhT"].re("p c t -> p (c t)"), T(t4.ap[:, 0:512], "b4"))
            P.copy("dve", Bd["BdBhT"].re("p c t -> p (c t)"), T(t4.ap[:, 512:1024], "b4"))
            t5 = P.bank(5, "b5").bf()
            for c in range(NCH):
                P.tr(T(t5.ap[:, c * 128:(c + 1) * 128], "b5"), Bd["BdV"][:, c, :], ident_b)
            t5v = T(t5.ap[:, 0:512].rearrange("p (c t) -> p c t", c=NCH), "b5")
            P.copy("act", St["StV"][0:64], t5v[0:64, :, 0:64])
            P.copy("dve", St["StV"][64:128], t5v[64:128, :, 64:128])
            if sample:
                sb_ = fc % 2
                P.dma("sp", Sin[sb_], st_wkv[l, sti, fc])
                P.copy("pool", Sinb[sb_], Sin[sb_])
            elif tis == 0:
                P.memset("dve", S_f[:, fc, :].k("S_f%d" % fc), 0.0)
                P.memset("pool", S_b[:, fc, :].k("S_b%d" % fc), 0.0)
            return BdT

        def chainpost(fc, BdT):
            q = fc % 2
            Bd = dict(Bd0)
            Bd.update(BdQ[q])
            St = dict(St0)
            St["StV"] = StVq[q]
            pr = dict(pr0)
            pr.update(prq[q])
            sb_ = fc % 2
            t6 = P.bank(6).bf()
            yy = c3(P.bank(7, "b7")[:, 0:TT])
            pp = P.bank(6, "b6")[:, 0:64]
            pu = P.bank(6, "b6")[:, 64:128]
            sn = P.bank(6, "b6")[:, 128:192]
            for c in range(NCH):
                if sample:
                    s0b, s0f, s1f = Sinb[sb_][:, c, :], Sin[sb_][:, c, :], Sout[sb_][:, c, :]
                else:
                    s0b, s0f, s1f = S_b[:, fc, :].k("S_b%d" % fc), S_f[:, fc, :].k("S_f%d" % fc), S_f[:, fc, :].k("S_f%d" % fc)
                P.mm(pp, Bd["BdA"][:, c, :], s0b, start=True, stop=False)
                P.mm(pp, Bd["BdMak"][:, c, :], St["StV"][:, c, :], start=False, stop=True)
                P.copy("act", StP, pp)
                P.mm(pu, BdT[:, c, :], StP)
                P.copy("dve", StU, pu)
                P.mm(yy[:, c, :], Bd["BdR"][:, c, :], s0b, start=True, stop=False)
                P.mm(yy[:, c, :], Bd["BdMrb"][:, c, :], StU, start=False, stop=False)
                P.mm(yy[:, c, :], Bd["BdMrk"][:, c, :], St["StV"][:, c, :], start=False, stop=True)
                P.mm(sn, Bd["BdBhT"][:, c, :], StU, start=True, stop=False)
                P.mm(sn, Bd["BdKhT"][:, c, :], St["StV"][:, c, :], start=False, stop=True)
                wc = c3(pr["e_inc"])[:, c, C - 1:C]
                P.stt(s1f, s0f, wc, sn, ALU.mult, ALU.add)
                if not sample:
                    P.copy("act", s0b, s1f)
            if sample:
                P.dma("sp", wkv_out[l, cfg.nseq + sti * NCH: cfg.nseq + (sti + 1) * NCH, fc].re("r p v -> p r v"), Sout[sb_])
            elif tis == cfg.tps - 1:
                P.dma("sp", wkv_out[l, seq_i, fc], S_f[:, fc, :].k("S_f%d" % fc))
            s1 = stat[:, 0, :]
            s2 = stat[:, 1, :]
            mean = stat[:, 2, :]
            var = stat[:, 3, :]
            rs = stat[:, 4, :]
            P.red(s1, yy)
            ysq3 = c3(pr["ysq"])
            P.act(ysq3, yy, AF.Square)
            P.red(s2, ysq3)
            P.ts("dve", mean, s1, 1.0 / 64)
            P.tt("dve", var, mean, mean, ALU.mult)
            P.stt(var, s2, 1.0 / 64, var, ALU.mult, ALU.subtract)
            P.ts("dve", var, var, 0.0, GN_EPS, op0=ALU.max, op1=ALU.add)
            P.act(rs, var, AF.Sqrt)
            P.recip(rs, rs)
            yc3 = c3(pr["yc"])
            P.tt("dve", yc3, yy, mean.re("p (c o) -> p c o", o=1).bc([128, NCH, C]), ALU.subtract)
            rs3 = rs.re("p (c o) -> p c o", o=1).bc([128, NCH, C])
            P.tt("dve", Bd["BdY"][0:64, :, 0:64], yc3[0:64], rs3[0:64], ALU.mult)
            P.tt("pool", Bd["BdY"][64:128, :, 64:128], yc3[64:128], rs3[64:128], ALU.mult)
            for c in range(NCH):
                P.tr(T(t6.ap[:, 512 + c * 128:512 + (c + 1) * 128], "b6"), Bd["BdY"][:, c, :], ident_b)
            t5y = T(t6.ap[:, 512:1024].rearrange("p (c t) -> p c t", c=NCH), "b6")
            yf3 = c3(pr["yf"])
            P.ts("dve", yf3[0:64], t5y[0:64, :, 0:64], pa(PA_LW, fc)[0:64], pa(PA_LB, fc)[0:64], op0=ALU.mult, op1=ALU.add)
            P.ts("dve", yf3[64:128], t5y[64:128, :, 64:128], pa(PA_LW, fc)[64:128], pa(PA_LB, fc)[64:128], op0=ALU.mult, op1=ALU.add)
            P.stt(rkb, pr["r_f"], pa(PA_RK, fc), pr["kp"], ALU.mult, ALU.mult)
            bo = P.bank(7, "b7")[:, TT:2 * TT]
            P.mm(bo, bones_b, rkb)
            P.tt("dve", pr["bon"], bo, pr["v_f"], ALU.mult)
            P.tt("pool", pr["bon"], pr["bon"], pr["yf"], ALU.add)
            P.tt("pool", G[:, fc, :], pr["bon"], pr["zs"], ALU.mult)

        pending = []
        for fc in range(FC):
            holder = {}
            a_ops = capture(lambda: holder.__setitem__("BdT", prep(fc)))
            P.ops.extend(merge_ops(a_ops, pending))
            pending = capture(lambda: chainpost(fc, holder["BdT"]))
        P.ops.extend(pending)
        P.mark("t%d_%d_outproj" % (l, ti))
        for oc in range(KC):
            o = P.bank(2 + (oc % 2), "b%da" % (2 + oc % 2))[:, 0:TT]
            for fc in range(FC):
                P.mm(o, woutb[:, fc, oc * 128:(oc + 1) * 128], G[:, fc, :], start=fc == 0, stop=fc == FC - 1)
            P.tt("dve", xt[:, oc, :], xt[:, oc, :], o, ALU.add)
        P.dma("sp", dst[ti].k(dst.key + "_%d" % ti), xt)

    if "rwkv" in cfg.phases:
        for l in range(NA):
            P.dma("sp", par, par_a[l])
            P.dma("pool", w1b.re("p k f -> p (k f)"), w1_f[l])
            P.dma("pool", a1b.re("p k f -> p (k f)"), a1_f[l])
            P.dma("pool", w2b, w2_f[l], max_dma_last_dim=4096)
            P.dma("pool", a2b, a2_f[l], max_dma_last_dim=4096)
            if l > 0:
                P.dma("pool", v1b.re("p k f -> p (k f)"), v1_f)
                P.dma("pool", v2b, v2_f, max_dma_last_dim=4096)
            for fcq in range(4):
                P.dma("pool", woutb[:, fcq * 4:(fcq + 1) * 4, :].re("p a f -> p (a f)"),
                      wout_f[l][:, fcq * 4 * D:(fcq + 1) * 4 * D], max_dma_last_dim=4096)
            src = x_in if l == 0 else xa
            dst = xa if l == 0 else xb
            for ti in range(NT):
                rwkv_tile(l, ti, src, dst)
        if "mla" not in cfg.phases:
            for ti in range(NT):
                P.dma("sp", xt, xb[ti].k("D_xb_%d" % ti))
                P.dma("sp", y_dbg[ti], xt)

    if "mla" in cfg.phases:
        mla_phase(P, nc, cfg, dict(din=din, dout=dout, dscr=dscr, xa=xa, xb=xb, ident_f=ident_f, ident_b=ident_b,
                                   ones_b=ones_b, base_off=base_off))

    P.marks_final = list(getattr(P, "marks", []))
    import os as _os
    if _os.environ.get("SHOWMARKS"):
        for n_, i_ in P.marks_final:
            if "fc" not in n_ or "fc0" in n_ and "_0_" in n_:
                print("MARK", n_, i_)
    P.emit()
    print("prog stats:", P.stats)
    return nc


def vec_fm(v, n):
    return np.ascontiguousarray(np.asarray(v, np.float32).reshape(n, 128).T)


def make_consts():
    cf = np.zeros((128, 128 + 5 * TT), np.float32)
    cf[:, 0:128] = np.eye(128, dtype=np.float32)
    j = np.arange(128) % 64
    t = np.arange(64)
    lt = (j[:, None] < t[None, :]).astype(np.float32)
    gt = (j[:, None] > t[None, :]).astype(np.float32)
    le = (j[:, None] <= t[None, :]).astype(np.float32)
    cf[:, 128:128 + TT] = np.tile(lt, (1, NCH))
    cf[:, 128 + TT:128 + 2 * TT] = np.tile(gt, (1, NCH))
    cf[:, 128 + 2 * TT:128 + 3 * TT] = np.tile(le, (1, NCH))
    sm = np.ones(TT, np.float32)
    sm[0::C] = 0.0
    cf[:, 128 + 3 * TT:128 + 4 * TT] = sm[None, :]
    cf[:, 128 + 4 * TT:128 + 5 * TT] = 1.0 - sm[None, :]
    return cf


def prep_shared(inp):
    sh = {}
    sh["consts_f"] = make_consts()
    par = np.zeros((NA, 128, NPAR_A), np.float32)
    for l in range(NA):
        par[l, :, PA_NW:PA_NW + 8] = vec_fm(inp["a_norm_w"][l], 8)
        mixl = np.asarray(inp["a_mix"][l])
        for p, src in enumerate((0, 1, 2, 3, 4, 5)):
            par[l, :, PA_MIX + p * 8:PA_MIX + (p + 1) * 8] = vec_fm(mixl[src], 8)
        par[l, :, PA_W0:PA_W0 + 16] = vec_fm(inp["a_w0"][l], 16)
        par[l, :, PA_A0:PA_A0 + 16] = vec_fm(inp["a_a0"][l], 16)
        if l > 0:
            par[l, :, PA_V0:PA_V0 + 16] = vec_fm(inp["a_v0"][l - 1], 16)
        par[l, :, PA_KK:PA_KK + 16] = vec_fm(inp["a_k_k"][l], 16)
        par[l, :, PA_KA:PA_KA + 16] = vec_fm(inp["a_k_a"][l], 16)
        par[l, :, PA_RK:PA_RK + 16] = vec_fm(np.asarray(inp["a_r_k"][l]).reshape(-1), 16)
        par[l, :, PA_LW:PA_LW + 16] = vec_fm(inp["a_lnx_w"][l], 16)
        par[l, :, PA_LB:PA_LB + 16] = vec_fm(inp["a_lnx_b"][l], 16)
    sh["par_a"] = par
    w = np.asarray(inp["a_w_in"], np.float32).reshape(NA, 4, KC, 128, FC, 128)
    sh["w_in_f"] = np.ascontiguousarray(w.transpose(0, 4, 3, 1, 2, 5)).reshape(NA, FC, 128, 4 * KC * 128)
    def kfm(a, n):
        a = np.asarray(a, np.float32)
        lead = a.shape[:-2]
        a = a.reshape(lead + (KC, 128, n))
        a = np.moveaxis(a, -3, -2)
        return np.ascontiguousarray(a).reshape(lead + (128, KC * n))
    sh["w1_f"] = kfm(inp["a_w1"], 64)
    sh["a1_f"] = kfm(inp["a_a1"], 64)
    sh["v1_f"] = kfm(inp["a_v1"][0], 32)
    sh["w2_f"] = np.ascontiguousarray(inp["a_w2"], np.float32)
    sh["a2_f"] = np.ascontiguousarray(inp["a_a2"], np.float32)
    sh["v2_f"] = np.ascontiguousarray(inp["a_v2"][0], np.float32)
    wo = np.asarray(inp["a_w_out"], np.float32).reshape(NA, FC, 128, D)
    sh["wout_f"] = np.ascontiguousarray(wo.transpose(0, 2, 1, 3)).reshape(NA, 128, FC * D)
    return sh


def x_tiles(xp, xs, cfg):
    nt = cfg.ntiles
    out = np.zeros((nt, 128, KC, TT), np.float32)
    a = xp.reshape(cfg.ptiles, TT, KC, 128)
    out[:cfg.ptiles] = a.transpose(0, 3, 2, 1)
    b = xs.reshape(cfg.stiles, NCH, KC, 128)
    o = out[cfg.ptiles:].reshape(cfg.stiles, 128, KC, NCH, C)
    o[:, :, :, :, 0] = b.transpose(0, 3, 2, 1)
    return out


def prep_core(inp, cfg, pi, si):
    m = {}
    xp = np.asarray(inp["x_prompt"], np.float32)[pi]
    xs = np.asarray(inp["x_sample"], np.float32)[si][:, 0, :]
    m["x_in"] = x_tiles(xp, xs, cfg)
    sw = np.asarray(inp["state_wkv"], np.float32)[:, si]
    sw = sw.reshape(NA, cfg.stiles, NCH, FC, 2, 64, 64)
    m["st_wkv"] = np.ascontiguousarray(sw.transpose(0, 1, 3, 4, 6, 2, 5)).reshape(NA, cfg.stiles, FC, 128, NCH, 64)
    ss = np.asarray(inp["state_shift"], np.float32)[:, si]
    ss = ss.reshape(NA, cfg.stiles, NCH, KC, 128)
    m["st_shift"] = np.ascontiguousarray(ss.transpose(0, 1, 4, 3, 2))
    return m


def untile(y, cfg):
    yp = y[:cfg.ptiles].transpose(0, 3, 2, 1).reshape(cfg.nseq, cfg.seq, D)
    ys = y[cfg.ptiles:].reshape(cfg.stiles, 128, KC, NCH, C)[:, :, :, :, 0].transpose(0, 3, 2, 1).reshape(cfg.ns, D)
    return yp, ys


def unstate(w, cfg):
    a = w.reshape(NA, cfg.nstate, FC, 2, 64, 64)
    return np.ascontiguousarray(a.transpose(0, 1, 2, 3, 5, 4)).reshape(NA, cfg.nstate, 32, 64, 64)


def unshift(s, cfg):
    return np.ascontiguousarray(s.transpose(0, 1, 3, 2)).reshape(NA, cfg.nstate, D)


def prep_shared_mla(inp, cfg):
    sh = {}
    pb = np.zeros((128, NPAR_B), np.float32)
    pb[:, PB_KVN:PB_KVN + 8] = vec_fm(inp["kv_in_norm_w"], 8)
    for l in range(NB):
        pb[:, PB_BN + l * 8:PB_BN + (l + 1) * 8] = vec_fm(inp["b_norm_w"][l], 8)
        pb[:, PB_QN + l * 3:PB_QN + (l + 1) * 3] = vec_fm(inp["b_q_norm_w"][l], 3)
    pb[:, PB_FN:PB_FN + 8] = vec_fm(inp["final_norm_w"], 8)
    sh["par_b"] = pb
    sh["kvnw_bc"] = np.ascontiguousarray(np.broadcast_to(np.asarray(inp["kv_norm_w"], np.float32)[None, :], (128, KV_LORA)))
    half = QK_ROPE // 2
    inv_freq = (np.float32(10000.0) ** (-np.arange(half, dtype=np.float32) / np.float32(half))).astype(np.float32)
    pos = np.arange(cfg.seq, dtype=np.float32)
    ang = (pos[:, None] * inv_freq[None, :]).astype(np.float32)
    cs, sn = np.cos(ang).astype(np.float32), np.sin(ang).astype(np.float32)
    sh["rope_tm"] = np.ascontiguousarray(np.concatenate([cs, sn], axis=1))
    cf2 = np.concatenate([cs, cs], axis=1).T
    sf2 = np.concatenate([sn, sn], axis=1).T
    sh["rope_fm"] = np.ascontiguousarray(np.stack([cf2, sf2]))
    past = np.float32(cfg.npages * PAGE)
    angs = (past * inv_freq).astype(np.float32)
    cs1, sn1 = np.cos(angs).astype(np.float32), np.sin(angs).astype(np.float32)
    sh["rope_s_tm"] = np.ascontiguousarray(np.broadcast_to(np.concatenate([cs1, sn1])[None, :], (16, 64)))
    sh["rope_s_fm"] = np.ascontiguousarray(np.stack([np.concatenate([cs1, cs1]), np.concatenate([sn1, sn1])], axis=1))
    s_ = np.arange(128)[:, None]
    q_ = np.arange(TT)[None, :]
    cmk = np.concatenate([(s_ <= q_), (128 + s_ <= q_)], axis=1).astype(np.float32)
    sh["cmask"] = np.ascontiguousarray(cmk)
    sh["iota_p"] = np.arange(128, dtype=np.float32).reshape(128, 1)

    def kfm(a, nk):
        a = np.asarray(a, np.float32)
        n = a.shape[-1]
        lead = a.shape[:-2]
        a = a.reshape(lead + (nk, 128, n))
        a = np.moveaxis(a, -3, -2)
        return np.ascontiguousarray(a).reshape(lead + (128, nk * n))
    sh["wdkv_f"] = kfm(inp["w_dkv"], KC)
    wuk = np.asarray(inp["w_uk"], np.float32)
    wuv = np.asarray(inp["w_uv"], np.float32)
    sh["wuk_f"] = np.ascontiguousarray(wuk.reshape(2, 128, BH, 128).transpose(1, 0, 2, 3)).reshape(128, -1)
    sh["wuv_f"] = np.ascontiguousarray(wuv.reshape(2, 128, BH, 128).transpose(1, 0, 2, 3)).reshape(128, -1)
    sh["wukT_f"] = np.ascontiguousarray(wuk.transpose(2, 1, 0)).reshape(128, -1)
    sh["bwin_f"] = kfm(inp["b_w_in"], KC)
    sh["bwuq_f"] = kfm(inp["b_w_uq"], 3)
    sh["bwout_f"] = kfm(inp["b_w_out"], BH)
    sh["cache_ckv"] = np.asarray(inp["cache_ckv"], np.float32).reshape(-1, KV_LORA)
    sh["cache_kpe"] = np.asarray(inp["cache_kpe"], np.float32).reshape(-1, QK_ROPE)
    return sh


def run_all(inp, cfg, ncores, trace=False):
    nc = build(cfg)
    sh = prep_shared(inp)
    sh.update(prep_shared_mla(inp, cfg))
    in_maps = []
    for c in range(ncores):
        pi = list(range(c * cfg.nseq, (c + 1) * cfg.nseq))
        si = list(range(c * cfg.ns, (c + 1) * cfg.ns))
        m = dict(sh)
        m.update(prep_core(inp, cfg, pi, si))
        m["page_table"] = np.ascontiguousarray(np.asarray(inp["page_table"], np.int32)[si])
        in_maps.append(m)
    res = run_bass_kernel_spmd(nc, in_maps, core_ids=list(range(ncores)))
    return assemble([r for r in res.results], cfg, ncores)


def assemble(results, cfg, ncores):
    yp, ys, wkp, wks, shp, shs, ckp, kpp, cks, kps = ([] for _ in range(10))
    for r in results:
        a = r["y_p"].transpose(0, 3, 2, 1).reshape(cfg.nseq, cfg.seq, D)
        yp.append(a)
        ys.append(r["y_s"].transpose(2, 1, 0).reshape(cfg.ns, 1, D))
        w = unstate(r["wkv_out"], cfg)
        wkp.append(w[:, :cfg.nseq])
        wks.append(w[:, cfg.nseq:])
        s = unshift(r["shift_out"], cfg)
        shp.append(s[:, :cfg.nseq])
        shs.append(s[:, cfg.nseq:])
        ckp.append(r["ckv_p"].reshape(cfg.nseq, cfg.seq, KV_LORA))
        kpp.append(r["kpe_p"].reshape(cfg.nseq, cfg.seq, QK_ROPE))
        cks.append(r["ckv_s"].reshape(cfg.ns, 1, KV_LORA))
        kps.append(r["kpe_s"].reshape(cfg.ns, 1, QK_ROPE))
    cat = lambda xs, ax: np.ascontiguousarray(np.concatenate(xs, axis=ax), dtype=np.float32)
    return (cat(yp, 0), cat(ys, 0), cat(wkp, 1), cat(shp, 1), cat(ckp, 0), cat(kpp, 0),
            cat(wks, 1), cat(shs, 1), cat(cks, 0), cat(kps, 0))


def kernel(**inputs):
    inp = {k: np.asarray(v) for k, v in inputs.items()}
    npool = inp["cache_ckv"].shape[0]
    npages = inp["page_table"].shape[1]
    cfg = Cfg(nseq=2, seq=inp["x_prompt"].shape[1], ns=16, npages=npages, npool=npool)
    return run_all(inp, cfg, 8)
```

```python
import contextlib
import numpy as np
import concourse.bass as bass
import concourse.mybir as mybir
from concourse.bass_utils import run_bass_kernel_spmd

F32 = mybir.dt.float32
BF16 = mybir.dt.bfloat16
I32 = mybir.dt.int32
ALU = mybir.AluOpType
AF = mybir.ActivationFunctionType
AX = mybir.AxisListType

D = 1024
KC = 8
DI = 2048
FC = 16
NA = 2
NB = 2
C = 64
TT = 256
NCH = TT // C
KV_LORA = 256
QK_ROPE = 64
QK_NOPE = 128
Q_LORA = 384
BH = 16
NORM_EPS = 1e-6
GN_EPS = 64 * 1e-5
C0 = float(np.exp(-0.5))
ATTN_SCALE = float((QK_NOPE + QK_ROPE) ** -0.5)
PAGE = 128

COMPUTE = ("pe", "dve", "act", "pool")
NHW = 32
NSW = 16
NDMASEM = NHW + NSW


class T:
    def __init__(self, ap, key):
        self.ap = ap
        self.key = key

    def __getitem__(self, idx):
        return T(self.ap[idx], self.key)

    def bc(self, shape):
        return T(self.ap.broadcast_to(list(shape)), self.key)

    def re(self, pat, **kw):
        return T(self.ap.rearrange(pat, **kw), self.key)

    def k(self, key):
        return T(self.ap, key)

    def bf(self):
        return T(self.ap.bitcast(BF16), self.key)


def _keys(*ts):
    out = []
    for t in ts:
        if isinstance(t, T):
            out.append(t.key)
    return out


class Prog:
    def __init__(self, nc):
        self.nc = nc
        self.ops = []
        self.stack = contextlib.ExitStack()
        self.sbuf = self.stack.enter_context(nc.sbuf_tensor("sbpool", [128, 50 * 1024], F32))
        self.psum = self.stack.enter_context(nc.psum_tensor("pspool", [128, 8, 512], F32))
        self.off = 0
        self.names = set()

    def alloc(self, name, shape, dt=F32):
        assert name not in self.names, name
        self.names.add(name)
        n = int(np.prod(shape[1:]))
        words = n if dt in (F32, I32) else (n + 1) // 2
        words = (words + 7) // 8 * 8
        assert self.off + words <= 50 * 1024, ("sbuf overflow", name, self.off, words)
        ap = self.sbuf[0:shape[0], self.off:self.off + words]
        self.off += words
        if dt != F32:
            ap = ap.bitcast(dt)
        ap = ap[:, 0:n]
        if len(shape) == 3:
            ap = ap.rearrange("p (a b) -> p a b", a=shape[1], b=shape[2])
        elif len(shape) == 4:
            ap = ap.rearrange("p (a b c) -> p a b c", a=shape[1], b=shape[2], c=shape[3])
        return T(ap, name)

    def bank(self, i, key=None):
        return T(self.psum[:, i, :], "b%d" % i)

    def op(self, eng, fn, reads=(), writes=(), dma=False):
        self.ops.append(dict(eng=eng, fn=fn, reads=tuple(reads), writes=tuple(writes), dma=dma))

    def mark(self, name):
        self.marks = getattr(self, "marks", [])
        self.marks.append((name, len(self.ops)))

    def barrier(self):
        self.ops.append(dict(eng=None, barrier=True, reads=(), writes=(), dma=False))

    def dma(self, q, out, in_, **kw):
        o = out.ap if isinstance(out, T) else out
        i = in_.ap if isinstance(in_, T) else in_
        self.op(q, lambda e: e.dma_start(out=o, in_=i, **kw), _keys(in_), _keys(out), dma=True)

    def mm(self, out, lhsT, rhs, start=True, stop=True):
        self.op("pe", lambda e: e.matmul(out.ap, lhsT.ap, rhs.ap, start=start, stop=stop),
                _keys(lhsT, rhs), _keys(out))

    def tr(self, out, in_, ident):
        self.op("pe", lambda e: e.transpose(out.ap, in_.ap, ident.ap), _keys(in_, ident), _keys(out))

    def tt(self, eng, out, in0, in1, op):
        self.op(eng, lambda e: e.tensor_tensor(out.ap, in0.ap, in1.ap, op), _keys(in0, in1), _keys(out))

    def ts(self, eng, out, in0, s1, s2=None, op0=ALU.mult, op1=None):
        a1 = s1.ap if isinstance(s1, T) else s1
        a2 = s2.ap if isinstance(s2, T) else s2
        if op1 is None:
            self.op(eng, lambda e: e.tensor_scalar(out.ap, in0.ap, a1, None, op0), _keys(in0, s1), _keys(out))
        else:
            self.op(eng, lambda e: e.tensor_scalar(out.ap, in0.ap, a1, a2, op0, op1),
                    _keys(in0, s1, s2), _keys(out))

    def stt(self, out, in0, scalar, in1, op0, op1):
        sc = scalar.ap if isinstance(scalar, T) else scalar
        self.op("dve", lambda e: e.scalar_tensor_tensor(out.ap, in0.ap, sc, in1.ap, op0, op1),
                _keys(in0, scalar, in1), _keys(out))

    def act(self, out, in_, func, bias=None, scale=None):
        kw = {}
        if bias is not None:
            kw["bias"] = bias.ap if isinstance(bias, T) else bias
        if scale is not None:
            kw["scale"] = scale.ap if isinstance(scale, T) else scale
        self.op("act", lambda e: e.activation(out=out.ap, in_=in_.ap, func=func, **kw),
                _keys(in_, bias, scale), _keys(out))

    def gather(self, out, in_, idx):
        self.op("pool", lambda e: e.indirect_dma_start(out=out.ap, out_offset=None, in_=in_.ap,
                in_offset=bass.IndirectOffsetOnAxis(ap=idx.ap, axis=0)), _keys(in_, idx), _keys(out), dma=True)

    def act_acc(self, out, in_, func, accum):
        self.op("act", lambda e: e.activation(out=out.ap, in_=in_.ap, func=func, accum_out=accum.ap),
                _keys(in_), _keys(out, accum))

    def copy(self, eng, out, in_):
        if eng == "act":
            self.op("act", lambda e: e.activation(out=out.ap, in_=in_.ap, func=AF.Copy), _keys(in_), _keys(out))
        else:
            self.op(eng, lambda e: e.tensor_copy(out.ap, in_.ap), _keys(in_), _keys(out))

    def memset(self, eng, out, val):
        self.op(eng, lambda e: e.memset(out.ap, val), (), _keys(out))

    def red(self, out, in_, op=ALU.add, axis=AX.X):
        self.op("dve", lambda e: e.tensor_reduce(out.ap, in_.ap, axis, op), _keys(in_), _keys(out))

    def recip(self, out, in_):
        self.op("dve", lambda e: e.reciprocal(out.ap, in_.ap), _keys(in_), _keys(out))

    def scan(self, out, d0, d1, init, op0, op1):
        self.op("dve", lambda e: e.tensor_tensor_scan(out.ap, d0.ap, d1.ap, init, op0, op1),
                _keys(d0, d1), _keys(out))

    def emit(self):
        nc = self.nc
        import os as _os
        mo = int(_os.environ.get("MAXOPS", "0"))
        if mo:
            self.ops = self.ops[:mo]
        ops = self.ops
        last_w = {}
        readers = {}
        last_on_eng = {}
        dma_since = []
        for i, o in enumerate(ops):
            if o.get("barrier"):
                o["deps"] = set(last_on_eng.values()) | set(dma_since)
                o["bar_deps"] = set(o["deps"])
                dma_since = []
                last_w.clear()
                readers.clear()
                last_on_eng = {}
                o["is_bar"] = True
                continue
            deps = set()
            E = o["eng"]
            for k in o["reads"]:
                w = last_w.get(k)
                if w is not None:
                    wo = ops[w]
                    if wo["dma"] or wo["eng"] != E or E != "pe":
                        deps.add(w)
                if len(k) == 2 and k[0] == "b" and k[1].isdigit():
                    for (re_, rdma), r in readers.get(k, {}).items():
                        if re_ != E:
                            deps.add(r)
            for k in o["writes"]:
                w = last_w.get(k)
                if w is not None:
                    wo = ops[w]
                    if wo["dma"] or wo["eng"] != E or E != "pe":
                        deps.add(w)
                for (re_, rdma), r in readers.get(k, {}).items():
                    if rdma or re_ != E or E != "pe":
                        deps.add(r)
            for k in o["reads"]:
                readers.setdefault(k, {})[(E, o["dma"])] = i
            for k in o["writes"]:
                last_w[k] = i
                readers[k] = {}
            deps.discard(i)
            o["deps"] = deps
            if o["dma"]:
                dma_since.append(i)
            else:
                last_on_eng[E] = i
        pending = {}
        for i, o in enumerate(ops):
            if o.get("barrier"):
                for e in ("pe", "dve", "act", "pool", "sp"):
                    pending.setdefault(e, set()).update(o["bar_deps"])
                continue
            E = o["eng"]
            if pending.get(E):
                o["deps"] |= pending[E]
                pending[E] = set()
        for o in ops:
            o["flag"] = False
        for o in ops:
            if o.get("barrier"):
                continue
            for d in o["deps"]:
                if not ops[d]["dma"]:
                    ops[d]["flag"] = True
        cnt = {e: 0 for e in COMPUTE}
        ndma = 0
        nq = {"hw": 0, "sw": 0}
        for o in ops:
            if o.get("barrier"):
                continue
            if o["dma"]:
                if o["eng"] == "pool":
                    o["sem"] = NHW + nq["sw"] % NSW
                    o["val"] = 16 * (nq["sw"] // NSW + 1)
                    nq["sw"] += 1
                else:
                    o["sem"] = nq["hw"] % NHW
                    o["val"] = 16 * (nq["hw"] // NHW + 1)
                    nq["hw"] += 1
                ndma += 1
            elif o["flag"]:
                cnt[o["eng"]] += 1
                o["val"] = cnt[o["eng"]]
        sems = {e: self.stack.enter_context(nc.semaphore("s_" + e)) for e in COMPUTE}
        dsems = [self.stack.enter_context(nc.semaphore("s_dma%d" % j)) for j in range(NDMASEM)]
        per_eng = {e: [] for e in ("pe", "dve", "act", "pool", "sp")}
        for i, o in enumerate(ops):
            if o.get("barrier"):
                continue
            per_eng[o["eng"]].append(i)
        final_dma = {}
        for o in ops:
            if o.get("barrier"):
                continue
            if o["dma"]:
                final_dma[o["sem"]] = max(final_dma.get(o["sem"], 0), o["val"])
        n_wait = [0]

        def emit_engine(ename, e):
            waited = {}
            for i in per_eng[ename]:
                o = ops[i]
                need = {}
                for d in o["deps"]:
                    do = ops[d]
                    key = ("d", do["sem"]) if do["dma"] else ("c", do["eng"])
                    need[key] = max(need.get(key, 0), do["val"])
                if o["dma"] and o["val"] > 16:
                    key = ("d", o["sem"])
                    need[key] = max(need.get(key, 0), o["val"] - 16)
                for key, v in need.items():
                    if waited.get(key, 0) >= v:
                        continue
                    waited[key] = v
                    s = dsems[key[1]] if key[0] == "d" else sems[key[1]]
                    e.wait_ge(s, v)
                    n_wait[0] += 1
                ins = o["fn"](e)
                if o["dma"]:
                    ins.then_inc(dsems[o["sem"]], 16)
                elif o["flag"]:
                    ins.then_inc(sems[ename], 1)
            if ename == "sp":
                for sidx, v in final_dma.items():
                    e.wait_ge(dsems[sidx], v)

        with nc.Block() as block:
            @block.sync
            def _(e):
                emit_engine("sp", e)

            @block.tensor
            def _(e):
                emit_engine("pe", e)

            @block.vector
            def _(e):
                emit_engine("dve", e)

            @block.scalar
            def _(e):
                emit_engine("act", e)

            @block.gpsimd
            def _(e):
                emit_engine("pool", e)
        self.stats = dict(n_ops=len(ops), n_wait=n_wait[0], cnt=cnt, ndma=ndma,
                          per_eng={k: len(v) for k, v in per_eng.items()})
        self.stack.close()


PA_NW = 0
PA_MIX = 8
PA_W0 = 56
PA_A0 = 72
PA_V0 = 88
PA_KK = 104
PA_KA = 120
PA_RK = 136
PA_LW = 152
PA_LB = 168
NPAR_A = 184


class Cfg:
    def __init__(self, nseq=2, seq=2048, ns=16, npages=64, npool=10240, phases=("rwkv", "mla")):
        self.nseq, self.seq, self.ns, self.npages, self.npool = nseq, seq, ns, npages, npool
        self.ptiles = nseq * seq // TT
        self.tps = seq // TT
        self.stiles = ns // NCH
        self.ntiles = self.ptiles + self.stiles
        self.nstate = nseq + ns
        self.phases = phases


PB_KVN = 0
PB_BN = 8
PB_FN = 24
PB_QN = 32
NPAR_B = 40
QW = BH * (QK_NOPE + QK_ROPE)
ZW = Q_LORA + DI


def mla_phase(P, nc, cfg, env):
    din, dout, dscr = env["din"], env["dout"], env["dscr"]
    xa, xb = env["xa"], env["xb"]
    ident_f, ident_b, ones_b = env["ident_f"], env["ident_b"], env["ones_b"]
    NS, SEQ, NSEQ = cfg.ns, cfg.seq, cfg.nseq
    NPG = cfg.npages
    par_b_d = din("par_b", [128, NPAR_B])
    kvnw_d = din("kvnw_bc", [128, KV_LORA])
    rope_tm_d = din("rope_tm", [SEQ, 64])
    rope_fm_d = din("rope_fm", [2, 64, SEQ])
    rope_s_tm_d = din("rope_s_tm", [16, 64])
    rope_s_fm_d = din("rope_s_fm", [64, 2])
    cm_d = din("cmask", [128, 2 * TT])
    iota_d = din("iota_p", [128, 1])
    wdkv_d = din("wdkv_f", [128, KC * 320])
    wuk_d = din("wuk_f", [128, 2 * BH * 128])
    wukT_d = din("wukT_f", [128, BH * KV_LORA])
    wuv_d = din("wuv_f", [128, 2 * BH * 128])
    win_d = din("bwin_f", [NB, 128, KC * ZW])
    wuq_d = din("bwuq_f", [NB, 128, 3 * QW])
    wout_d = din("bwout_f", [NB, 128, BH * D])
    pt_d = din("page_table", [NS, NPG], I32)
    cckv_d = din("cache_ckv", [cfg.npool * PAGE, KV_LORA])
    ckpe_d = din("cache_kpe", [cfg.npool * PAGE, QK_ROPE])
    y_p = dout("y_p", [cfg.ptiles, 128, KC, TT])
    y_s = dout("y_s", [128, KC, NS])
    ckv_p = dout("ckv_p", [NSEQ * SEQ, KV_LORA])
    kpe_p = dout("kpe_p", [NSEQ * SEQ, QK_ROPE])
    ckv_s = dout("ckv_s", [NS, KV_LORA])
    kpe_s = dout("kpe_s", [NS, QK_ROPE])
    xs_scr = dscr("xs_scr", [128, KC, NS])
    cs_scr = dscr("cs_scr", [NS, KV_LORA])

    P.barrier()
    P.off = env["base_off"]
    al = P.alloc

    def capture(fn):
        saved = P.ops
        P.ops = []
        fn()
        out = P.ops
        P.ops = saved
        return out

    def merge_ops(a, b):
        if not b:
            return list(a)
        if not a:
            return list(b)
        out = []
        step = len(a) / float(len(b) + 1)
        nxt = step
        bi = 0
        for i, o in enumerate(a):
            out.append(o)
            while bi < len(b) and i + 1 >= nxt:
                out.append(b[bi])
                bi += 1
                nxt += step
        out.extend(b[bi:])
        return out
    parb = al("parb", [128, NPAR_B])
    kvnw = al("kvnw", [128, KV_LORA])
    iota_p = al("iota_pf", [128, 1])
    win_b = al("win_b", [128, KC, ZW], BF16)
    wuq_b = al("wuq_b", [128, 3, QW], BF16)
    wuqR = al("wuqR", [128, 3, BH, 64], BF16)
    wout_b = al("wout2_b", [128, BH, D], BF16)
    wdkv_b = al("wdkv_b", [128, KC, 320], BF16)
    wuv_b = al("wuv_b", [128, 2, BH, 128], BF16)
    xt = al("m_xt", [128, KC, TT])
    sqb = al("m_sqb", [128, KC, TT], BF16)
    hnb = al("m_hnb", [128, KC, TT], BF16)
    rstd = al("m_rstd", [128, TT])
    cq = al("m_cq", [128, 3, TT])
    cqb = al("m_cqb", [128, 3, TT], BF16)
    zs = al("m_zs", [128, BH, TT], BF16)
    Qn = al("m_Qn", [128, TT], BF16)
    Qr = al("m_Qr", [64, TT], BF16)
    qa = al("m_qa", [64, TT])
    qb = al("m_qb", [64, TT])
    rden = al("m_rden", [128, TT])
    tmpo = al("m_tmpo", [128, TT])
    small = al("m_small", [128, 8])
    P.dma("sp", parb, par_b_d)
    P.dma("sp", kvnw, kvnw_d)
    P.dma("sp", iota_p, iota_d)
    P.dma("pool", wdkv_b.re("p k f -> p (k f)"), wdkv_d, max_dma_last_dim=4096)
    P.dma("pool", wuv_b.re("p c h v -> p (c h v)"), wuv_d, max_dma_last_dim=4096)
    pbc = lambda col: parb[:, col:col + 1]
    mark0 = P.off

    def rmsnorm_fm(x3, n, wcol, out3):
        P.act(sqb[:, :, 0:n], x3, AF.Square)
        ss = P.bank(2)[:, 0:n]
        for kc in range(KC):
            P.mm(ss, ones_b, sqb[:, kc, 0:n], start=kc == 0, stop=kc == KC - 1)
        P.act(rstd[:, 0:n], ss, AF.Sqrt, bias=NORM_EPS, scale=1.0 / D)
        P.recip(rstd[:, 0:n], rstd[:, 0:n])
        for kc in range(KC):
            P.stt(out3[:, kc, :], x3[:, kc, :], pbc(wcol + kc), rstd[:, 0:n], ALU.mult, ALU.mult)

    def q_side(l, n):
        for j in range(3):
            o = P.bank(j % 2)[:, 0:n]
            for kc in range(KC):
                P.mm(o, win_b[:, kc, j * 128:(j + 1) * 128], hnb[:, kc, 0:n], start=kc == 0, stop=kc == KC - 1)
            P.copy("act", cq[:, j, 0:n], o)
        P.act(sqb[:, 0:3, 0:n], cq[:, :, 0:n], AF.Square)
        ss = P.bank(2)[:, 0:n]
        for j in range(3):
            P.mm(ss, ones_b, sqb[:, j, 0:n], start=j == 0, stop=j == 2)
        P.act(rstd[:, 0:n], ss, AF.Sqrt, bias=NORM_EPS, scale=1.0 / Q_LORA)
        P.recip(rstd[:, 0:n], rstd[:, 0:n])
        for j in range(3):
            P.stt(cqb[:, j, 0:n], cq[:, j, 0:n], pbc(PB_QN + l * 3 + j), rstd[:, 0:n], ALU.mult, ALU.mult)
        for fc in range(BH):
            o = P.bank(fc % 2)[:, 0:n]
            for kc in range(KC):
                P.mm(o, win_b[:, kc, Q_LORA + fc * 128:Q_LORA + (fc + 1) * 128], hnb[:, kc, 0:n],
                     start=kc == 0, stop=kc == KC - 1)
            P.act(zs[:, fc, 0:n], o, AF.Silu)

    def q_head(h, n, cos_t, sin_t, Qn=Qn, Qr=Qr):
        o = P.bank(0)[:, 0:n]
        for j in range(3):
            P.mm(o, wuq_b[:, j, h * 192:h * 192 + 128], cqb[:, j, 0:n], start=j == 0, stop=j == 2)
        P.ts("dve", Qn[:, 0:n], o, ATTN_SCALE)
        oa = P.bank(1)[0:64, 0:n]
        ob = P.bank(1)[0:64, TT:TT + n]
        for j in range(3):
            P.mm(oa, wuq_b[:, j, h * 192 + 128:h * 192 + 192], cqb[:, j, 0:n], start=j == 0, stop=j == 2)
        for j in range(3):
            P.mm(ob, wuqR[:, j, h, :], cqb[:, j, 0:n], start=j == 0, stop=j == 2)
        if isinstance(cos_t, tuple):
            P.ts("dve", qa[:, 0:n], oa, cos_t[0])
            P.stt(Qr[:, 0:n], ob, sin_t[0], qa[:, 0:n], ALU.mult, ALU.add)
        else:
            P.tt("dve", qa[:, 0:n], oa, cos_t, ALU.mult)
            P.tt("dve", qb[:, 0:n], ob, sin_t, ALU.mult)
            P.tt("pool", Qr[:, 0:n], qa[:, 0:n], qb[:, 0:n], ALU.add)

    def kv_tm(n, ps, rope_t, ckv_t, kpe_t):
        ssq = small[0:n, 0:1]
        rs = small[0:n, 1:2]
        P.act_acc(ckv_t, ps[:, 0:256], AF.Square, ssq)
        P.act(rs, ssq, AF.Sqrt, bias=NORM_EPS, scale=1.0 / KV_LORA)
        P.recip(rs, rs)
        P.stt(ckv_t, ps[:, 0:256], rs, kvnw[0:n, :], ALU.mult, ALU.mult)
        x1, x2 = ps[:, 256:288], ps[:, 288:320]
        cs, sn = rope_t[:, 0:32], rope_t[:, 32:64]
        t1 = tmpo[0:n, 0:32]
        t2 = tmpo[0:n, 32:64]
        P.tt("dve", t1, x1, cs, ALU.mult)
        P.tt("dve", t2, x2, sn, ALU.mult)
        P.tt("dve", kpe_t[:, 0:32], t1, t2, ALU.subtract)
        P.tt("dve", t1, x2, cs, ALU.mult)
        P.tt("dve", t2, x1, sn, ALU.mult)
        P.tt("dve", kpe_t[:, 32:64], t1, t2, ALU.add)

    for l in range(NB):
        src = xb if l == 0 else xa
        skey = "D_xb_%d" if l == 0 else "D_xa_%d"
        for kc in range(KC):
            P.dma("pool", win_b[:, kc, :], win_d[l][:, kc * ZW:(kc + 1) * ZW], max_dma_last_dim=4096)
        for j in range(3):
            P.dma("pool", wuq_b[:, j, :], wuq_d[l][:, j * QW:(j + 1) * QW], max_dma_last_dim=4096)
        for hq in range(4):
            P.dma("pool", wout_b[:, hq * 4:(hq + 1) * 4, :].re("p a f -> p (a f)"),
                  wout_d[l][:, hq * 4 * D:(hq + 1) * 4 * D], max_dma_last_dim=4096)
        wq4 = wuq_b.re("p j (h e) -> p j h e", e=192)
        P.ts("dve", wuqR[:, :, :, 0:32], wq4[:, :, :, 160:192], -1.0)
        P.copy("dve", wuqR[:, :, :, 32:64], wq4[:, :, :, 128:160])
        P.off = mark0
        sfx = "_%d" % l
        wuk_b = al("wuk_b" + sfx, [128, 2, BH, 128], BF16)
        ckvT = al("ckvT" + sfx, [128, 2, SEQ], BF16)
        kpeT = al("kpeT" + sfx, [64, SEQ], BF16)
        KhTP = [al("KhT%d" % i + sfx, [128, SEQ], BF16) for i in range(2)]
        VhP = [al("Vh%d" % i + sfx, [128, SEQ // 128, 128], BF16) for i in range(2)]
        QnP = [Qn, al("Qn1" + sfx, [128, TT], BF16)]
        QrP = [Qr, al("Qr1" + sfx, [64, TT], BF16)]
        pTb = [al("pT%d" % i + sfx, [128, 2 * TT], BF16) for i in range(2)]
        cm = al("cm" + sfx, [128, 2, TT])
        cosf = al("cosf" + sfx, [64, TT])
        sinf = al("sinf" + sfx, [64, TT])
        ropet = al("ropet" + sfx, [128, 64])
        ckv_t = al("ckv_t" + sfx, [128, KV_LORA])
        kpe_t = al("kpe_t" + sfx, [128, 64])
        P.dma("pool", wuk_b.re("p c h n -> p (c h n)"), wuk_d, max_dma_last_dim=4096)
        P.dma("sp", cm.re("p j t -> p (j t)"), cm_d)
        for s_ in range(NSEQ):
            for tis in range(cfg.tps):
                ti = s_ * cfg.tps + tis
                P.dma("sp", xt, xb[ti].k("D_xb_%d" % ti))
                rmsnorm_fm(xt, TT, PB_KVN, hnb)
                for tc in range(TT // 128):
                    pos = tis * TT + tc * 128
                    ps = P.bank(0)[:, 0:320]
                    for kc in range(KC):
                        P.mm(ps, hnb[:, kc, tc * 128:(tc + 1) * 128], wdkv_b[:, kc, :], start=kc == 0, stop=kc == KC - 1)
                    P.dma("sp", ropet, rope_tm_d[pos:pos + 128])
                    kv_tm(128, ps, ropet, ckv_t, kpe_t)
                    if l == 0:
                        r0 = s_ * SEQ + pos
                        P.dma("sp", ckv_p[r0:r0 + 128].k("D_ckvp_%d" % r0), ckv_t)
                        P.dma("sp", kpe_p[r0:r0 + 128].k("D_kpep_%d" % r0), kpe_t)
                    pt_ = P.bank(1)
                    P.tr(pt_[:, 0:128], ckv_t[:, 0:128], ident_f)
                    P.tr(pt_[:, 128:256], ckv_t[:, 128:256], ident_f)
                    P.tr(pt_[0:64, 256:384], kpe_t, ident_f)
                    P.copy("act", ckvT[:, 0, pos:pos + 128], pt_[:, 0:128])
                    P.copy("act", ckvT[:, 1, pos:pos + 128], pt_[:, 128:256])
                    P.copy("act", kpeT[:, pos:pos + 128], pt_[0:64, 256:384])
            for tis in range(cfg.tps):
                ti = s_ * cfg.tps + tis
                P.dma("sp", xt, src[ti].k(skey % ti))
                rmsnorm_fm(xt, TT, PB_BN + l * 8, hnb)
                q_side(l, TT)
                P.dma("sp", cosf, rope_fm_d[0][:, tis * TT:(tis + 1) * TT])
                P.dma("sp", sinf, rope_fm_d[1][:, tis * TT:(tis + 1) * TT])
                P.ts("pool", cosf, cosf, ATTN_SCALE)
                P.ts("pool", sinf, sinf, ATTN_SCALE)
                kend = (tis + 1) * TT
                nsc = kend // 128
                def gen(h):
                    pq = h % 2
                    q_head(h, TT, cosf, sinf, Qn=QnP[pq], Qr=QrP[pq])
                    for kb in range((kend + 511) // 512):
                        w_ = min(512, kend - kb * 512)
                        o = P.bank(7)[:, 0:w_]
                        for cc in range(2):
                            P.mm(o, wuk_b[:, cc, h, :], ckvT[:, cc, kb * 512:kb * 512 + w_], start=cc == 0, stop=cc == 1)
                        P.copy("dve", KhTP[pq][:, kb * 512:kb * 512 + w_], o)
                    for g0 in range(0, nsc, 4):
                        g1 = min(nsc, g0 + 4)
                        ob_ = P.bank(2)
                        for sc in range(g0, g1):
                            for cc in range(2):
                                P.mm(ob_[:, (sc - g0) * 128:(sc - g0 + 1) * 128], ckvT[:, cc, sc * 128:(sc + 1) * 128],
                                     wuv_b[:, cc, h, :], start=cc == 0, stop=cc == 1)
                        P.copy("dve", VhP[pq][:, g0:g1, :].re("p a v -> p (a v)"), ob_[:, 0:(g1 - g0) * 128])

                def attn(h):
                    pq = h % 2
                    Qn_, Qr_, KhT, Vh = QnP[pq], QrP[pq], KhTP[pq], VhP[pq]
                    oT = P.bank(5)[:, 0:TT]
                    den = P.bank(6)[:, 0:TT]
                    npair = nsc // 2

                    def scores(pi_):
                        for e_ in range(2):
                            sc = 2 * pi_ + e_
                            sb_ = P.bank(3 + pi_ % 2)[:, e_ * TT:(e_ + 1) * TT]
                            P.mm(sb_, KhT[:, sc * 128:(sc + 1) * 128], Qn_, start=True, stop=False)
                            P.mm(sb_, kpeT[:, sc * 128:(sc + 1) * 128], Qr_, start=False, stop=True)
                    scores(0)
                    for pi_ in range(npair):
                        if pi_ + 1 < npair:
                            scores(pi_ + 1)
                        pT = pTb[pi_ % 2]
                        P.act(pT, P.bank(3 + pi_ % 2)[:, 0:2 * TT], AF.Exp)
                        if pi_ == npair - 1:
                            P.tt("pool", pT, pT, cm.re("p j t -> p (j t)"), ALU.mult)
                        for e_ in range(2):
                            sc = 2 * pi_ + e_
                            P.mm(oT, Vh[:, sc, :], pT[:, e_ * TT:(e_ + 1) * TT], start=sc == 0, stop=sc == nsc - 1)
                            P.mm(den, ones_b, pT[:, e_ * TT:(e_ + 1) * TT], start=sc == 0, stop=sc == nsc - 1)
                    P.recip(rden, den)
                    P.tt("dve", tmpo, oT, rden, ALU.mult)
                    P.tt("pool", zs[:, h, :], tmpo, zs[:, h, :], ALU.mult)

                gen(0)
                for h in range(BH):
                    a_ops = capture(lambda: attn(h))
                    b_ops = capture(lambda: gen(h + 1)) if h + 1 < BH else []
                    P.ops.extend(merge_ops(a_ops, b_ops))
                for oc in range(KC):
                    o = P.bank(oc % 2)[:, 0:TT]
                    for h in range(BH):
                        P.mm(o, wout_b[:, h, oc * 128:(oc + 1) * 128], zs[:, h, :], start=h == 0, stop=h == BH - 1)
                    P.tt("dve", xt[:, oc, :], xt[:, oc, :], o, ALU.add)
                if l == NB - 1:
                    rmsnorm_fm_f32(P, xt, TT, PB_FN, pbc, sqb, rstd, ones_b)
                    P.dma("sp", y_p[ti], xt)
                else:
                    P.dma("sp", xa[ti].k("D_xa_%d" % ti), xt)
        P.barrier()
        P.off = mark0
        wukT_b = al("wukT_b" + sfx, [128, BH, KV_LORA], BF16)
        xs = al("xs" + sfx, [128, KC, NS])
        xkv = al("xkv" + sfx, [128, KC, NS])
        ropes = al("ropes" + sfx, [16, 64])
        ropesf = al("ropesf" + sfx, [64, 2])
        ckv_st = al("ckv_st" + sfx, [NS, KV_LORA])
        kpe_st = al("kpe_st" + sfx, [NS, 64])
        ckv_sT = al("ckv_sT" + sfx, [128, 2, NS], BF16)
        kpe_sT = al("kpe_sT" + sfx, [64, NS], BF16)
        cs1 = al("cs1" + sfx, [1, KV_LORA])
        cs1b = al("cs1b" + sfx, [1, KV_LORA + 8], BF16)
        qnb = al("qnb" + sfx, [128, NS], BF16)
        Qr_all = al("Qr_all" + sfx, [64, BH, NS], BF16)
        qlat = al("qlat" + sfx, [128, 2, BH, NS], BF16)
        olatT = al("olatT" + sfx, [128, 2, BH, NS], BF16)
        olat_n = al("olat_n" + sfx, [16, KV_LORA])
        ptb = al("ptb" + sfx, [128, NPG], I32)
        idx = al("idx" + sfx, [128, NPG], I32)
        pg_c = [al("pg_c%d" % i + sfx, [128, 4, KV_LORA]) for i in range(2)]
        pg_k = [al("pg_k%d" % i + sfx, [128, 4, 64]) for i in range(2)]
        cb = [al("cb%d" % i + sfx, [128, 4, KV_LORA + 1], BF16) for i in range(2)]
        kb_ = [al("kb%d" % i + sfx, [128, 4, 64], BF16) for i in range(2)]
        cTP = [al("cT%d" % i + sfx, [128, 4, 2, 128], BF16) for i in range(2)]
        kTP = [al("kT%d" % i + sfx, [64, 4, 128], BF16) for i in range(2)]
        pTsP = [al("pTs%d" % i + sfx, [128, 4, BH], BF16) for i in range(2)]
        pself = al("pself" + sfx, [1, BH], BF16)
        P.dma("pool", wukT_b.re("p h c -> p (h c)"), wukT_d, max_dma_last_dim=4096)
        P.dma("sp", ropes, rope_s_tm_d)
        P.dma("sp", ropesf, rope_s_fm_d)
        P.ts("dve", ropesf, ropesf, ATTN_SCALE)
        for i in range(2):
            P.memset("dve", cb[i][:, :, KV_LORA:KV_LORA + 1], 1.0)
        P.memset("dve", cs1b[:, KV_LORA:KV_LORA + 1], 1.0)
        for st in range(cfg.stiles):
            ti = cfg.ptiles + st
            P.dma("sp", xt, xb[ti].k("D_xb_%d" % ti))
            P.copy("dve", xkv[:, :, st * NCH:(st + 1) * NCH], xt.re("p k (r c) -> p k r c", c=C)[:, :, :, 0])
        if l == 0:
            P.copy("dve", xs, xkv)
        else:
            P.dma("sp", xs, xs_scr)
        rmsnorm_fm(xkv, NS, PB_KVN, hnb[:, :, 0:NS])
        ps = P.bank(0)[0:NS, 0:320]
        for kc in range(KC):
            P.mm(ps, hnb[:, kc, 0:NS], wdkv_b[:, kc, :], start=kc == 0, stop=kc == KC - 1)
        kv_tm(NS, ps, ropes[0:NS], ckv_st, kpe_st)
        if l == 0:
            P.dma("sp", ckv_s, ckv_st)
            P.dma("sp", kpe_s, kpe_st)
        P.dma("sp", cs_scr.k("D_cs_scr%d" % l), ckv_st)
        pt_ = P.bank(1)
        idn = ident_f[0:NS, 0:NS]
        P.tr(pt_[:, 0:NS], ckv_st[:, 0:128], idn)
        P.tr(pt_[:, NS:2 * NS], ckv_st[:, 128:256], idn)
        P.tr(pt_[0:64, 2 * NS:3 * NS], kpe_st, idn)
        P.copy("act", ckv_sT[:, 0, :], pt_[:, 0:NS])
        P.copy("act", ckv_sT[:, 1, :], pt_[:, NS:2 * NS])
        P.copy("act", kpe_sT, pt_[0:64, 2 * NS:3 * NS])
        rmsnorm_fm(xs, NS, PB_BN + l * 8, hnb[:, :, 0:NS])
        q_side(l, NS)
        for h in range(BH):
            q_head(h, NS, (ropesf[:, 0:1],), (ropesf[:, 1:2],))
            P.copy("act", Qr_all[:, h, :], Qr[:, 0:NS])
            for cc in range(2):
                o = P.bank(2)[:, cc * NS:(cc + 1) * NS]
                P.mm(o, wukT_b[:, h, cc * 128:(cc + 1) * 128], Qn[:, 0:NS])
            P.copy("dve", qlat[:, :, h, :], P.bank(2)[:, 0:2 * NS].re("p (c r) -> p c r", c=2))
        for r in range(NS):
            P.dma("sp", ptb, T(pt_d.ap[r].partition_broadcast(128), pt_d.key))
            P.ts("dve", idx, ptb, float(PAGE), iota_p, op0=ALU.mult, op1=ALU.add)
            ol = P.bank(5)[0:BH, 0:KV_LORA + 1]
            ng = NPG // 4
            def stage1(g):
                b = g % 2
                for j in range(4):
                    pg = g * 4 + j
                    P.gather(pg_c[b][:, j, :], cckv_d, idx[:, pg:pg + 1])
                    P.gather(pg_k[b][:, j, :], ckpe_d, idx[:, pg:pg + 1])
                P.copy("dve", cb[b][:, :, 0:KV_LORA], pg_c[b])
                P.copy("act", kb_[b], pg_k[b])
                t3 = P.bank(3 if b == 0 else 0).bf()
                t4 = P.bank(4 if b == 0 else 1).bf()
                for j in range(4):
                    for cc in range(2):
                        P.tr(t3[:, (j * 2 + cc) * 128:(j * 2 + cc + 1) * 128], cb[b][:, j, cc * 128:(cc + 1) * 128], ident_b)
                    P.tr(t4[0:64, j * 128:(j + 1) * 128], kb_[b][:, j, :], ident_b)
                P.copy("act", cTP[b].re("p j c t -> p (j c t)"), t3)
                P.copy("dve", kTP[b].re("p j t -> p (j t)"), t4[0:64, 0:512])

            def stage2(g):
                b = g % 2
                sc_ = P.bank(6 if b == 0 else 2)
                for j in range(4):
                    o = sc_[:, j * BH:(j + 1) * BH]
                    P.mm(o, cTP[b][:, j, 0, :], qlat[:, 0, :, r], start=True, stop=False)
                    P.mm(o, cTP[b][:, j, 1, :], qlat[:, 1, :, r], start=False, stop=False)
                    P.mm(o, kTP[b][:, j, :], Qr_all[:, :, r], start=False, stop=True)
                P.act(pTsP[b].re("p j h -> p (j h)"), sc_[:, 0:4 * BH], AF.Exp)
                for j in range(4):
                    P.mm(ol, pTsP[b][:, j, :], cb[b][:, j, :], start=(g == 0 and j == 0), stop=False)

            stage1(0)
            for g in range(ng):
                a_ops = capture(lambda: stage2(g))
                b_ops = capture(lambda: stage1(g + 1)) if g + 1 < ng else []
                P.ops.extend(merge_ops(a_ops, b_ops))
            so = P.bank(6)[0:1, 64:64 + BH]
            P.mm(so, ckv_sT[:, 0, r:r + 1], qlat[:, 0, :, r], start=True, stop=False)
            P.mm(so, ckv_sT[:, 1, r:r + 1], qlat[:, 1, :, r], start=False, stop=False)
            P.mm(so, kpe_sT[:, r:r + 1], Qr_all[:, :, r], start=False, stop=True)
            P.act(pself, so, AF.Exp)
            P.dma("sp", cs1, cs_scr.k("D_cs_scr%d" % l)[r:r + 1, :])
            P.copy("dve", cs1b[:, 0:KV_LORA], cs1)
            P.mm(ol, pself, cs1b[:, 0:KV_LORA + 1], start=False, stop=True)
            rd = small[0:BH, 2:3]
            P.recip(rd, ol[:, KV_LORA:KV_LORA + 1])
            P.ts("dve", olat_n, ol[:, 0:KV_LORA], rd)
            p7 = P.bank(7)
            idh = ident_f[0:BH, 0:BH]
            P.tr(p7[:, 0:BH], olat_n[:, 0:128], idh)
            P.tr(p7[:, BH:2 * BH], olat_n[:, 128:256], idh)
            P.copy("act", olatT[:, :, :, r], p7[:, 0:2 * BH].re("p (c h) -> p c h", c=2))
        for h in range(BH):
            o = P.bank(0)[:, 0:NS]
            for cc in range(2):
                P.mm(o, wuv_b[:, cc, h, :], olatT[:, cc, h, :], start=cc == 0, stop=cc == 1)
            P.tt("dve", zs[:, h, 0:NS], o, zs[:, h, 0:NS], ALU.mult)
        for oc in range(KC):
            o = P.bank(oc % 2)[:, 0:NS]
            for h in range(BH):
                P.mm(o, wout_b[:, h, oc * 128:(oc + 1) * 128], zs[:, h, 0:NS], start=h == 0, stop=h == BH - 1)
            P.tt("dve", xs[:, oc, :], xs[:, oc, :], o, ALU.add)
        if l == NB - 1:
            rmsnorm_fm_f32(P, xs, NS, PB_FN, pbc, sqb, rstd, ones_b)
            P.dma("sp", y_s, xs)
        else:
            P.dma("sp", xs_scr, xs)
        P.barrier()


def rmsnorm_fm_f32(P, x3, n, wcol, pbc, sqb, rstd, ones_b):
    P.act(sqb[:, :, 0:n], x3, AF.Square)
    ss = P.bank(2)[:, 0:n]
    for kc in range(KC):
        P.mm(ss, ones_b, sqb[:, kc, 0:n], start=kc == 0, stop=kc == KC - 1)
    P.act(rstd[:, 0:n], ss, AF.Sqrt, bias=NORM_EPS, scale=1.0 / D)
    P.recip(rstd[:, 0:n], rstd[:, 0:n])
    for kc in range(KC):
        P.stt(x3[:, kc, :], x3[:, kc, :], pbc(wcol + kc), rstd[:, 0:n], ALU.mult, ALU.mult)


def build(cfg):
    import os as _os
    nc = bass.Bass("TRN2", target_bir_lowering=False)
    P = Prog(nc)
    NT = cfg.ntiles

    def din(name, shape, dt=F32):
        return T(nc.dram_tensor(name, list(shape), dt, kind="ExternalInput").ap(), "D_" + name)

    def dout(name, shape, dt=F32):
        return T(nc.dram_tensor(name, list(shape), dt, kind="ExternalOutput").ap(), "D_" + name)

    def dscr(name, shape, dt=F32):
        return T(nc.dram_tensor(name, list(shape), dt, kind="Internal").ap(), "D_" + name)

    x_in = din("x_in", [NT, 128, KC, TT])
    consts_f = din("consts_f", [128, 128 + 3 * TT + 2 * TT])
    par_a = din("par_a", [NA, 128, NPAR_A])
    w_in_f = din("w_in_f", [NA, FC, 128, 4 * KC * 128])
    w1_f = din("w1_f", [NA, 128, KC * 64])
    a1_f = din("a1_f", [NA, 128, KC * 64])
    v1_f = din("v1_f", [128, KC * 32])
    w2_f = din("w2_f", [NA, 64, DI])
    a2_f = din("a2_f", [NA, 64, DI])
    v2_f = din("v2_f", [32, DI])
    wout_f = din("wout_f", [NA, 128, FC * D])
    st_wkv = din("st_wkv", [NA, cfg.stiles, FC, 128, NCH, 64])
    st_shift = din("st_shift", [NA, cfg.stiles, 128, KC, NCH])

    xa = dscr("xa", [NT, 128, KC, TT])
    xb = dscr("xb", [NT, 128, KC, TT])
    vfirst = dscr("vfirst", [NT, FC, 128, TT])
    w_in_b = dscr("w_in_b", [NA, FC, 128, 4 * KC * 128], BF16)

    wkv_out = dout("wkv_out", [NA, cfg.nstate, FC, 128, 64])
    shift_out = dout("shift_out", [NA, cfg.nstate, 128, KC])
    y_dbg = dout("y_dbg", [NT, 128, KC, TT]) if "mla" not in cfg.phases else None

    cf = P.alloc("cf", [128, 128 + 5 * TT])
    ident_f = cf[:, 0:128]
    m_lt = cf[:, 128:128 + TT].re("p (c t) -> p c t", c=NCH)
    m_gt = cf[:, 128 + TT:128 + 2 * TT].re("p (c t) -> p c t", c=NCH)
    m_le = cf[:, 128 + 2 * TT:128 + 3 * TT].re("p (c t) -> p c t", c=NCH)
    scanmask = cf[:, 128 + 3 * TT:128 + 4 * TT]
    colmask = cf[:, 128 + 4 * TT:128 + 5 * TT]
    ident_b = P.alloc("ident_b", [128, 128], BF16)
    ones_b = P.alloc("ones_b", [128, 128], BF16)
    bones_b = P.alloc("bones_b", [128, 128], BF16)
    eye_st = P.alloc("eye_st", [128, NCH, 64], BF16)
    bd_eye = P.alloc("bd_eye", [128, NCH, 128], BF16)
    P.dma("sp", cf, consts_f)
    P.copy("dve", ident_b, ident_f)
    P.memset("dve", ones_b, 1.0)
    P.memset("dve", bones_b, 0.0)
    P.memset("dve", bones_b[0:64, 0:64], 1.0)
    P.memset("dve", bones_b[64:128, 64:128], 1.0)
    P.memset("dve", bd_eye, 0.0)
    P.memset("dve", eye_st, 0.0)
    for c in range(NCH):
        P.copy("dve", bd_eye[:, c, :], ident_f)
        P.copy("dve", eye_st[0:64, c, :], ident_f[0:64, 0:64])
        P.copy("dve", eye_st[64:128, c, :], ident_f[64:128, 64:128])

    P.mark("consts_done")
    base_off = P.off
    for l in range(NA):
        for fc in range(FC):
            P.dma("pool", w_in_b[l, fc].k("D_winb_%d_%d" % (l, fc)), w_in_f[l, fc], max_dma_last_dim=4096)

    P.mark("prologue_done")
    par = P.alloc("par", [128, NPAR_A])
    w1b = P.alloc("w1b", [128, KC, 64], BF16)
    a1b = P.alloc("a1b", [128, KC, 64], BF16)
    v1b = P.alloc("v1b", [128, KC, 32], BF16)
    w2b = P.alloc("w2b", [64, DI], BF16)
    a2b = P.alloc("a2b", [64, DI], BF16)
    v2b = P.alloc("v2b", [32, DI], BF16)
    woutb = P.alloc("woutb", [128, FC, D], BF16)
    wbuf = [P.alloc("wbuf%d" % i, [128, 4, KC, 128], BF16) for i in range(2)]
    xt = P.alloc("xt", [128, KC, TT])
    hn = P.alloc("hn", [128, KC, TT])
    dx = P.alloc("dx", [128, KC, TT], BF16)
    hprev = P.alloc("hprev", [128, KC])
    shst = P.alloc("shst", [128, KC, NCH])
    shcp = P.alloc("shcp", [128, NCH, KC])
    rstd = P.alloc("rstd", [128, TT])
    mx = [P.alloc("mx%d" % p, [128, KC, TT], BF16) for p in range(6)]
    G = P.alloc("G", [128, FC, TT], BF16)
    hwb = P.alloc("hwb", [64, TT], BF16)
    hab = P.alloc("hab", [64, TT], BF16)
    hvb = P.alloc("hvb", [32, TT], BF16)
    S_f = P.alloc("S_f", [128, FC, 64])
    S_b = P.alloc("S_b", [128, FC, 64], BF16)
    Sin = [P.alloc("Sin%d" % i, [128, NCH, 64]) for i in range(2)]
    Sinb = [P.alloc("Sinb%d" % i, [128, NCH, 64], BF16) for i in range(2)]
    Sout = [P.alloc("Sout%d" % i, [128, NCH, 64]) for i in range(2)]
    pn = ["iclr", "sg", "zs", "r_f", "k_f", "v_f", "vg", "vf", "kkr", "rn", "kkn", "tq", "kp", "bv",
          "cum", "e_inc", "e_inv", "e_exc", "e_cl", "tmp", "yc", "ysq", "yf", "bon"]
    pr = {n: P.alloc(n, [128, TT]) for n in pn}
    sqk = P.alloc("sqk", [128, TT], BF16)
    rkb = P.alloc("rkb", [128, TT], BF16)
    stat = P.alloc("stat", [128, 8, NCH])
    sn_ = ["StR", "StB", "StA", "StN", "StNT", "StT", "StV", "StN2", "StNT2"]
    St = {n: P.alloc(n, [128, NCH, 64], BF16) for n in sn_}
    StP = P.alloc("StP", [128, 64], BF16)
    StU = P.alloc("StU", [128, 64], BF16)
    bn_ = ["BdR", "BdK", "BdB", "BdA", "BdMak", "BdMrb", "BdMrk", "BdT", "BdN", "BdNT", "BdN2", "BdNT2",
           "BdKh", "BdBh", "BdV", "BdKhT", "BdBhT", "BdY"]
    Bd = {n: P.alloc(n, [128, NCH, 128], BF16) for n in bn_}
    for n in bn_:
        if n not in ("BdKhT", "BdBhT"):
            P.memset("pool", Bd[n], 0.0)
    pr0, St0, Bd0 = pr, St, Bd
    BdQ = [dict(), dict()]
    for n in ("BdA", "BdR", "BdMak", "BdMrb", "BdMrk", "BdT", "BdKhT", "BdBhT"):
        BdQ[0][n] = Bd0[n]
        BdQ[1][n] = P.alloc(n + "_1", [128, NCH, 128], BF16)
        if n not in ("BdKhT", "BdBhT"):
            P.memset("pool", BdQ[1][n], 0.0)
    StVq = [St0["StV"], P.alloc("StV_1", [128, NCH, 64], BF16)]
    prq = [dict(), dict()]
    for n in ("e_inc", "r_f", "kp", "v_f", "zs"):
        prq[0][n] = pr0[n]
        prq[1][n] = P.alloc(n + "_1", [128, TT])
    print("sbuf words used (RWKV phase):", P.off)

    def capture(fn):
        saved = P.ops
        P.ops = []
        fn()
        out = P.ops
        P.ops = saved
        return out

    def merge_ops(a, b):
        if not b:
            return list(a)
        if not a:
            return list(b)
        out = []
        step = len(a) / float(len(b) + 1)
        nxt = step
        bi = 0
        for i, o in enumerate(a):
            out.append(o)
            while bi < len(b) and i + 1 >= nxt:
                out.append(b[bi])
                bi += 1
                nxt += step
        out.extend(b[bi:])
        return out

    def c3(t):
        return t.re("p (c t) -> p c t", c=NCH)

    def bd_halves(dst, src, engs=("act", "pool")):
        P.copy(engs[0], dst[0:64, :, 0:64], src[0:64])
        P.copy(engs[1], dst[64:128, :, 64:128], src[64:128])

    def rwkv_tile(l, ti, src, dst):
        sample = ti >= cfg.ptiles
        if sample:
            sti = ti - cfg.ptiles
        else:
            seq_i, tis = divmod(ti, cfg.tps)
        pa = lambda col, j=0: par[:, col + j:col + j + 1]
        bA = P.bank(0, "b0")[:, 0:TT]
        P.mark("tile_%d_%d_start" % (l, ti))
        P.dma("sp", xt, src[ti].k(src.key + "_%d" % ti))
        sq = mx[0]
        P.act(sq, xt, AF.Square)
        ssb = P.bank(3, "b3")[:, TT:2 * TT]
        for kc in range(KC):
            P.mm(ssb, ones_b, sq[:, kc, :], start=kc == 0, stop=kc == KC - 1)
        P.act(rstd, ssb, AF.Sqrt, bias=NORM_EPS, scale=1.0 / D)
        P.recip(rstd, rstd)
        for kc in range(KC):
            P.stt(hn[:, kc, :], xt[:, kc, :], pa(PA_NW, kc), rstd, ALU.mult, ALU.mult)
        P.mark("t%d_%d_shift" % (l, ti))
        if not sample:
            if tis == 0:
                P.memset("dve", hprev, 0.0)
            P.tt("dve", dx[:, :, 1:TT], hn[:, :, 0:TT - 1], hn[:, :, 1:TT], ALU.subtract)
            P.tt("dve", dx[:, :, 0:1], hprev.re("p (k o) -> p k o", o=1), hn[:, :, 0:1], ALU.subtract)
            P.copy("dve", hprev.re("p (k o) -> p k o", o=1), hn[:, :, TT - 1:TT])
            if tis == cfg.tps - 1:
                P.dma("sp", shift_out[l, seq_i], hprev)
        else:
            hn0 = hn.re("p k (r c) -> p k r c", c=C)[:, :, :, 0]
            dx0 = dx.re("p k (r c) -> p k r c", c=C)[:, :, :, 0]
            P.dma("sp", shst, st_shift[l, sti])
            P.memset("pool", dx, 0.0)
            P.tt("dve", dx0, shst, hn0, ALU.subtract)
            P.copy("dve", shcp.re("p r k -> p k r"), hn0)
            P.dma("sp", shift_out[l, cfg.nseq + sti * NCH: cfg.nseq + (sti + 1) * NCH].re("r p k -> p r k"), shcp)
        P.mark("t%d_%d_mixed" % (l, ti))
        for p in (4, 5, 0, 1, 2, 3):
            for kc in range(KC):
                P.stt(mx[p][:, kc, :], dx[:, kc, :], pa(PA_MIX, p * KC + kc), hn[:, kc, :], ALU.mult, ALU.add)
        b2a = P.bank(2, "b2")[0:64, 0:TT]
        b2b = P.bank(2, "b2")[0:64, TT:2 * TT]
        for kc in range(KC):
            P.mm(b2a, w1b[:, kc, :], mx[4][:, kc, :], start=kc == 0, stop=kc == KC - 1)
        P.act(hwb, b2a, AF.Tanh)
        for kc in range(KC):
            P.mm(b2b, a1b[:, kc, :], mx[5][:, kc, :], start=kc == 0, stop=kc == KC - 1)
        P.copy("act", hab, b2b)
        if l > 0:
            b3a = P.bank(3, "b3")[0:32, 0:TT]
            for kc in range(KC):
                P.mm(b3a, v1b[:, kc, :], mx[2][:, kc, :], start=kc == 0, stop=kc == KC - 1)
            P.copy("act", hvb, b3a)
        def prep(fc):
            q = fc % 2
            Bd = dict(Bd0)
            Bd.update(BdQ[q])
            St = dict(St0)
            St["StV"] = StVq[q]
            pr = dict(pr0)
            pr.update(prq[q])
            w = wbuf[fc % 2]
            P.dma("sp", w.re("p a k f -> p (a k f)"), w_in_b[l, fc].k("D_winb_%d_%d" % (l, fc)))
            fs = slice(fc * 128, (fc + 1) * 128)
            pb = {}
            for p, (bi, half) in enumerate(((0, 0), (0, 1), (1, 0), (1, 1))):
                o = P.bank(bi, "b%d%s" % (bi, "ab"[half]))[:, half * TT:(half + 1) * TT]
                pb[p] = o
                for kc in range(KC):
                    P.mm(o, w[:, p, kc, :], mx[p][:, kc, :], start=kc == 0, stop=kc == KC - 1)
            wl = P.bank(2, "b2")[:, 0:TT]
            al = P.bank(2, "b2")[:, TT:2 * TT]
            P.mm(wl, w2b[:, fs], hwb)
            P.mm(al, a2b[:, fs], hab)
            if l > 0:
                vl = P.bank(3, "b3")[:, 0:TT]
                P.mm(vl, v2b[:, fs], hvb)
            P.act(pr["iclr"], al, AF.Sigmoid, bias=pa(PA_A0, fc))
            P.act(pr["sg"], wl, AF.Sigmoid, bias=pa(PA_W0, fc))
            if sample:
                P.tt("pool", pr["sg"], pr["sg"], colmask, ALU.mult)
            P.act(pr["zs"], pb[3], AF.Silu)
            P.copy("act", pr["r_f"], pb[0])
            P.copy("dve", pr["k_f"], pb[1])
            if l == 0:
                P.copy("act", pr["v_f"], pb[2])
                P.dma("sp", vfirst[ti, fc].k("D_vf_%d_%d" % (ti, fc)), pr["v_f"])
            else:
                P.act(pr["vg"], vl, AF.Sigmoid, bias=pa(PA_V0, fc))
                P.dma("sp", pr["vf"], vfirst[ti, fc].k("D_vf_%d_%d" % (ti, fc)))
                P.tt("dve", pr["tmp"], pr["vf"], pb[2], ALU.subtract)
                P.tt("dve", pr["tmp"], pr["tmp"], pr["vg"], ALU.mult)
                P.tt("dve", pr["v_f"], pr["tmp"], pb[2], ALU.add)
            P.ts("dve", pr["kkr"], pr["k_f"], pa(PA_KK, fc))
            P.tt("pool", sqk, pr["kkr"], pr["kkr"], ALU.mult)
            ssk = P.bank(3, "b3")[:, TT:2 * TT]
            P.mm(ssk, bones_b, sqk)
            P.act(pr["rn"], ssk, AF.Sqrt)
            P.ts("dve", pr["rn"], pr["rn"], 1e-12, op0=ALU.max)
            P.recip(pr["rn"], pr["rn"])
            P.tt("dve", pr["kkn"], pr["kkr"], pr["rn"], ALU.mult)
            P.ts("dve", pr["tq"], pr["iclr"], -1.0, pa(PA_KA, fc), op0=ALU.add, op1=ALU.mult)
            P.stt(pr["kp"], pr["tq"], 1.0, pr["k_f"], ALU.add, ALU.mult)
            P.tt("pool", pr["bv"], pr["kkn"], pr["iclr"], ALU.mult)
            P.scan(pr["cum"], scanmask, pr["sg"], 0.0, ALU.mult, ALU.add)
            P.act(pr["e_inc"], pr["cum"], AF.Exp, scale=-C0)
            P.act(pr["e_inv"], pr["cum"], AF.Exp, scale=C0)
            P.tt("pool", pr["tmp"], pr["cum"], pr["sg"], ALU.subtract)
            P.act(pr["e_exc"], pr["tmp"], AF.Exp, scale=-C0)
            cum3 = c3(pr["cum"])
            P.tt("dve", c3(pr["e_cl"]), cum3[:, :, C - 1:C].bc([128, NCH, C]), cum3, ALU.subtract)
            P.act(pr["e_cl"], pr["e_cl"], AF.Exp, scale=-C0)
            P.tt("dve", St["StR"], c3(pr["r_f"]), c3(pr["e_inc"]), ALU.mult)
            bd_halves(Bd["BdR"], St["StR"])
            P.tt("dve", St["StB"], c3(pr["bv"]), c3(pr["e_inv"]), ALU.mult)
            bd_halves(Bd["BdB"], St["StB"])
            P.stt(St["StA"], c3(pr["kkn"]), -1.0, c3(pr["e_exc"]), ALU.mult, ALU.mult)
            bd_halves(Bd["BdA"], St["StA"])
            for (dst_, a_, b_) in (("BdK", "kp", "e_inv"), ("BdKh", "kp", "e_cl"), ("BdBh", "bv", "e_cl")):
                P.tt("dve", Bd[dst_][0:64, :, 0:64], c3(pr[a_])[0:64], c3(pr[b_])[0:64], ALU.mult)
                P.tt("pool", Bd[dst_][64:128, :, 64:128], c3(pr[a_])[64:128], c3(pr[b_])[64:128], ALU.mult)
            bd_halves(Bd["BdV"], c3(pr["v_f"]))
            gk = {}
            for name, (bi, half), lh, rh in (("ak", (0, 0), "BdK", "StA"), ("ab", (0, 1), "BdB", "StA"),
                                             ("abT", (1, 0), "BdA", "StB"), ("rk", (1, 1), "BdK", "StR"),
                                             ("rb", (2, 0), "BdB", "StR")):
                o = c3(P.bank(bi, "b%d%s" % (bi, "ab"[half]))[:, half * TT:(half + 1) * TT])
                gk[name] = o
                for c in range(NCH):
                    P.mm(o[:, c, :], Bd[lh][:, c, :], St[rh][:, c, :])

            def bd_masked(dst, src, mask):
                P.tt("dve", dst[0:64, :, 0:64], src[0:64], mask[0:64], ALU.mult)
                P.tt("dve", dst[64:128, :, 64:128], src[64:128], mask[64:128], ALU.mult)
            bd_masked(Bd["BdMak"], gk["ak"], m_lt)
            bd_masked(Bd["BdMrk"], gk["rk"], m_le)
            bd_masked(Bd["BdMrb"], gk["rb"], m_le)
            if not sample:
                P.tt("dve", St["StN"], gk["ab"], m_lt, ALU.mult)
                P.tt("dve", St["StNT"], gk["abT"], m_gt, ALU.mult)
                bd_halves(Bd["BdN"], St["StN"])
                bd_halves(Bd["BdNT"], St["StNT"])
                P.tt("pool", St["StT"], St["StN"], eye_st, ALU.add)
                cur = ("StN", "StNT", "BdN", "BdNT")
                nxt = ("StN2", "StNT2", "BdN2", "BdNT2")
                for lev in range(1, 6):
                    sN, sNT, bN, bNT = cur
                    nN, nNT, nbN, nbNT = nxt
                    o2 = c3(P.bank(4, "b4")[:, TT:2 * TT])
                    for c in range(NCH):
                        P.mm(o2[:, c, :], Bd[bN][:, c, :], St[sNT][:, c, :])
                    if lev < 5:
                        o1 = c3(P.bank(4, "b4")[:, 0:TT])
                        for c in range(NCH):
                            P.mm(o1[:, c, :], Bd[bNT][:, c, :], St[sN][:, c, :])
                        if _os.environ.get("E285", "act") == "ident":
                            P.act(St[nN], o1, AF.Identity)
                        else:
                            P.copy(_os.environ.get("E285", "act"), St[nN], o1)
                        P.copy("dve", St[nNT], o2)
                        bd_halves(Bd[nbN], St[nN])
                        bd_halves(Bd[nbNT], St[nNT])
                    else:
                        P.copy("dve", Bd[nbNT][0:64, :, 0:64], o2[0:64])
                        P.copy("act", Bd[nbNT][64:128, :, 64:128], o2[64:128])
                    o3 = c3(P.bank(5, "b5")[:, 0:TT])
                    for c in range(NCH):
                        P.mm(o3[:, c, :], Bd[nbNT][:, c, :], St["StT"][:, c, :])
                    P.tt("dve", St["StT"], o3, St["StT"], ALU.add)
                    cur, nxt = nxt, cur
                bd_halves(Bd["BdT"], St["StT"])
                BdT = Bd["BdT"]
            else:
                BdT = bd_eye
            t4 = P.bank(4, "b4").bf()
            t4k = T(t4.ap, "b4")
            for c in range(NCH):
                P.tr(T(t4.ap[:, c * 128:(c + 1) * 128], "b4"), Bd["BdKh"][:, c, :], ident_b)
            for c in range(NCH):
                P.tr(T(t4.ap[:, 512 + c * 128:512 + (c + 1) * 128], "b4"), Bd["BdBh"][:, c, :], ident_b)
            P.copy("act", Bd["BdKhT"].re("p c t -> p (c t)"), T(t4.ap[:, 0:512], "b4"))
            P.copy("dve", Bd["BdBhT"].re("p c t -> p (c t)"), T(t4.ap[:, 512:1024], "b4"))
            t5 = P.bank(5, "b5").bf()
            for c in range(NCH):
                P.tr(T(t5.ap[:, c * 128:(c + 1) * 128], "b5"), Bd["BdV"][:, c, :], ident_b)
            t5v = T(t5.ap[:, 0:512].rearrange("p (c t) -> p c t", c=NCH), "b5")
            P.copy("act", St["StV"][0:64], t5v[0:64, :, 0:64])
            P.copy("dve", St["StV"][64:128], t5v[64:128, :, 64:128])
            if sample:
                sb_ = fc % 2
                P.dma("sp", Sin[sb_], st_wkv[l, sti, fc])
                P.copy("pool", Sinb[sb_], Sin[sb_])
            elif tis == 0:
                P.memset("dve", S_f[:, fc, :].k("S_f%d" % fc), 0.0)
                P.memset("pool", S_b[:, fc, :].k("S_b%d" % fc), 0.0)
            return BdT

        def chainpost(fc, BdT):
            q = fc % 2
            Bd = dict(Bd0)
            Bd.update(BdQ[q])
            St = dict(St0)
            St["StV"] = StVq[q]
            pr = dict(pr0)
            pr.update(prq[q])
            sb_ = fc % 2
            t6 = P.bank(6).bf()
            yy = c3(P.bank(7, "b7")[:, 0:TT])
            pp = P.bank(6, "b6")[:, 0:64]
            pu = P.bank(6, "b6")[:, 64:128]
            sn = P.bank(6, "b6")[:, 128:192]
            for c in range(NCH):
                if sample:
                    s0b, s0f, s1f = Sinb[sb_][:, c, :], Sin[sb_][:, c, :], Sout[sb_][:, c, :]
                else:
                    s0b, s0f, s1f = S_b[:, fc, :].k("S_b%d" % fc), S_f[:, fc, :].k("S_f%d" % fc), S_f[:, fc, :].k("S_f%d" % fc)
                P.mm(pp, Bd["BdA"][:, c, :], s0b, start=True, stop=False)
                P.mm(pp, Bd["BdMak"][:, c, :], St["StV"][:, c, :], start=False, stop=True)
                P.copy("act", StP, pp)
                P.mm(pu, BdT[:, c, :], StP)
                P.copy("dve", StU, pu)
                P.mm(yy[:, c, :], Bd["BdR"][:, c, :], s0b, start=True, stop=False)
                P.mm(yy[:, c, :], Bd["BdMrb"][:, c, :], StU, start=False, stop=False)
                P.mm(yy[:, c, :], Bd["BdMrk"][:, c, :], St["StV"][:, c, :], start=False, stop=True)
                P.mm(sn, Bd["BdBhT"][:, c, :], StU, start=True, stop=False)
                P.mm(sn, Bd["BdKhT"][:, c, :], St["StV"][:, c, :], start=False, stop=True)
                wc = c3(pr["e_inc"])[:, c, C - 1:C]
                P.stt(s1f, s0f, wc, sn, ALU.mult, ALU.add)
                if not sample:
                    P.copy("act", s0b, s1f)
            if sample:
                P.dma("sp", wkv_out[l, cfg.nseq + sti * NCH: cfg.nseq + (sti + 1) * NCH, fc].re("r p v -> p r v"), Sout[sb_])
            elif tis == cfg.tps - 1:
                P.dma("sp", wkv_out[l, seq_i, fc], S_f[:, fc, :].k("S_f%d" % fc))
            s1 = stat[:, 0, :]
            s2 = stat[:, 1, :]
            mean = stat[:, 2, :]
            var = stat[:, 3, :]
            rs = stat[:, 4, :]
            P.red(s1, yy)
            ysq3 = c3(pr["ysq"])
            P.act(ysq3, yy, AF.Square)
            P.red(s2, ysq3)
            P.ts("dve", mean, s1, 1.0 / 64)
            P.tt("dve", var, mean, mean, ALU.mult)
            P.stt(var, s2, 1.0 / 64, var, ALU.mult, ALU.subtract)
            P.ts("dve", var, var, 0.0, GN_EPS, op0=ALU.max, op1=ALU.add)
            P.act(rs, var, AF.Sqrt)
            P.recip(rs, rs)
            yc3 = c3(pr["yc"])
            P.tt("dve", yc3, yy, mean.re("p (c o) -> p c o", o=1).bc([128, NCH, C]), ALU.subtract)
            rs3 = rs.re("p (c o) -> p c o", o=1).bc([128, NCH, C])
            P.tt("dve", Bd["BdY"][0:64, :, 0:64], yc3[0:64], rs3[0:64], ALU.mult)
            P.tt("pool", Bd["BdY"][64:128, :, 64:128], yc3[64:128], rs3[64:128], ALU.mult)
            for c in range(NCH):
                P.tr(T(t6.ap[:, 512 + c * 128:512 + (c + 1) * 128], "b6"), Bd["BdY"][:, c, :], ident_b)
            t5y = T(t6.ap[:, 512:1024].rearrange("p (c t) -> p c t", c=NCH), "b6")
            yf3 = c3(pr["yf"])
            P.ts("dve", yf3[0:64], t5y[0:64, :, 0:64], pa(PA_LW, fc)[0:64], pa(PA_LB, fc)[0:64], op0=ALU.mult, op1=ALU.add)
            P.ts("dve", yf3[64:128], t5y[64:128, :, 64:128], pa(PA_LW, fc)[64:128], pa(PA_LB, fc)[64:128], op0=ALU.mult, op1=ALU.add)
            P.stt(rkb, pr["r_f"], pa(PA_RK, fc), pr["kp"], ALU.mult, ALU.mult)
            bo = P.bank(7, "b7")[:, TT:2 * TT]
            P.mm(bo, bones_b, rkb)
            P.tt("dve", pr["bon"], bo, pr["v_f"], ALU.mult)
            P.tt("pool", pr["bon"], pr["bon"], pr["yf"], ALU.add)
            P.tt("pool", G[:, fc, :], pr["bon"], pr["zs"], ALU.mult)

        pending = []
        for fc in range(FC):
            holder = {}
            a_ops = capture(lambda: holder.__setitem__("BdT", prep(fc)))
            P.ops.extend(merge_ops(a_ops, pending))
            pending = capture(lambda: chainpost(fc, holder["BdT"]))
        P.ops.extend(pending)
        P.mark("t%d_%d_outproj" % (l, ti))
        for oc in range(KC):
            o = P.bank(2 + (oc % 2), "b%da" % (2 + oc % 2))[:, 0:TT]
            for fc in range(FC):
                P.mm(o, woutb[:, fc, oc * 128:(oc + 1) * 128], G[:, fc, :], start=fc == 0, stop=fc == FC - 1)
            P.tt("dve", xt[:, oc, :], xt[:, oc, :], o, ALU.add)
        P.dma("sp", dst[ti].k(dst.key + "_%d" % ti), xt)

    if "rwkv" in cfg.phases:
        for l in range(NA):
            P.dma("sp", par, par_a[l])
            P.dma("pool", w1b.re("p k f -> p (k f)"), w1_f[l])
            P.dma("pool", a1b.re("p k f -> p (k f)"), a1_f[l])
            P.dma("pool", w2b, w2_f[l], max_dma_last_dim=4096)
            P.dma("pool", a2b, a2_f[l], max_dma_last_dim=4096)
            if l > 0:
                P.dma("pool", v1b.re("p k f -> p (k f)"), v1_f)
                P.dma("pool", v2b, v2_f, max_dma_last_dim=4096)
            for fcq in range(4):
                P.dma("pool", woutb[:, fcq * 4:(fcq + 1) * 4, :].re("p a f -> p (a f)"),
                      wout_f[l][:, fcq * 4 * D:(fcq + 1) * 4 * D], max_dma_last_dim=4096)
            src = x_in if l == 0 else xa
            dst = xa if l == 0 else xb
            for ti in range(NT):
                rwkv_tile(l, ti, src, dst)
        if "mla" not in cfg.phases:
            for ti in range(NT):
                P.dma("sp", xt, xb[ti].k("D_xb_%d" % ti))
                P.dma("sp", y_dbg[ti], xt)

    if "mla" in cfg.phases:
        mla_phase(P, nc, cfg, dict(din=din, dout=dout, dscr=dscr, xa=xa, xb=xb, ident_f=ident_f, ident_b=ident_b,
                                   ones_b=ones_b, base_off=base_off))

    P.marks_final = list(getattr(P, "marks", []))
    import os as _os
    if _os.environ.get("SHOWMARKS"):
        for n_, i_ in P.marks_final:
            if "fc" not in n_ or "fc0" in n_ and "_0_" in n_:
                print("MARK", n_, i_)
    P.emit()
    print("prog stats:", P.stats)
    return nc


def vec_fm(v, n):
    return np.ascontiguousarray(np.asarray(v, np.float32).reshape(n, 128).T)


def make_consts():
    cf = np.zeros((128, 128 + 5 * TT), np.float32)
    cf[:, 0:128] = np.eye(128, dtype=np.float32)
    j = np.arange(128) % 64
    t = np.arange(64)
    lt = (j[:, None] < t[None, :]).astype(np.float32)
    gt = (j[:, None] > t[None, :]).astype(np.float32)
    le = (j[:, None] <= t[None, :]).astype(np.float32)
    cf[:, 128:128 + TT] = np.tile(lt, (1, NCH))
    cf[:, 128 + TT:128 + 2 * TT] = np.tile(gt, (1, NCH))
    cf[:, 128 + 2 * TT:128 + 3 * TT] = np.tile(le, (1, NCH))
    sm = np.ones(TT, np.float32)
    sm[0::C] = 0.0
    cf[:, 128 + 3 * TT:128 + 4 * TT] = sm[None, :]
    cf[:, 128 + 4 * TT:128 + 5 * TT] = 1.0 - sm[None, :]
    return cf


def prep_shared(inp):
    sh = {}
    sh["consts_f"] = make_consts()
    par = np.zeros((NA, 128, NPAR_A), np.float32)
    for l in range(NA):
        par[l, :, PA_NW:PA_NW + 8] = vec_fm(inp["a_norm_w"][l], 8)
        mixl = np.asarray(inp["a_mix"][l])
        for p, src in enumerate((0, 1, 2, 3, 4, 5)):
            par[l, :, PA_MIX + p * 8:PA_MIX + (p + 1) * 8] = vec_fm(mixl[src], 8)
        par[l, :, PA_W0:PA_W0 + 16] = vec_fm(inp["a_w0"][l], 16)
        par[l, :, PA_A0:PA_A0 + 16] = vec_fm(inp["a_a0"][l], 16)
        if l > 0:
            par[l, :, PA_V0:PA_V0 + 16] = vec_fm(inp["a_v0"][l - 1], 16)
        par[l, :, PA_KK:PA_KK + 16] = vec_fm(inp["a_k_k"][l], 16)
        par[l, :, PA_KA:PA_KA + 16] = vec_fm(inp["a_k_a"][l], 16)
        par[l, :, PA_RK:PA_RK + 16] = vec_fm(np.asarray(inp["a_r_k"][l]).reshape(-1), 16)
        par[l, :, PA_LW:PA_LW + 16] = vec_fm(inp["a_lnx_w"][l], 16)
        par[l, :, PA_LB:PA_LB + 16] = vec_fm(inp["a_lnx_b"][l], 16)
    sh["par_a"] = par
    w = np.asarray(inp["a_w_in"], np.float32).reshape(NA, 4, KC, 128, FC, 128)
    sh["w_in_f"] = np.ascontiguousarray(w.transpose(0, 4, 3, 1, 2, 5)).reshape(NA, FC, 128, 4 * KC * 128)
    def kfm(a, n):
        a = np.asarray(a, np.float32)
        lead = a.shape[:-2]
        a = a.reshape(lead + (KC, 128, n))
        a = np.moveaxis(a, -3, -2)
        return np.ascontiguousarray(a).reshape(lead + (128, KC * n))
    sh["w1_f"] = kfm(inp["a_w1"], 64)
    sh["a1_f"] = kfm(inp["a_a1"], 64)
    sh["v1_f"] = kfm(inp["a_v1"][0], 32)
    sh["w2_f"] = np.ascontiguousarray(inp["a_w2"], np.float32)
    sh["a2_f"] = np.ascontiguousarray(inp["a_a2"], np.float32)
    sh["v2_f"] = np.ascontiguousarray(inp["a_v2"][0], np.float32)
    wo = np.asarray(inp["a_w_out"], np.float32).reshape(NA, FC, 128, D)
    sh["wout_f"] = np.ascontiguousarray(wo.transpose(0, 2, 1, 3)).reshape(NA, 128, FC * D)
    return sh


def x_tiles(xp, xs, cfg):
    nt = cfg.ntiles
    out = np.zeros((nt, 128, KC, TT), np.float32)
    a = xp.reshape(cfg.ptiles, TT, KC, 128)
    out[:cfg.ptiles] = a.transpose(0, 3, 2, 1)
    b = xs.reshape(cfg.stiles, NCH, KC, 128)
    o = out[cfg.ptiles:].reshape(cfg.stiles, 128, KC, NCH, C)
    o[:, :, :, :, 0] = b.transpose(0, 3, 2, 1)
    return out


def prep_core(inp, cfg, pi, si):
    m = {}
    xp = np.asarray(inp["x_prompt"], np.float32)[pi]
    xs = np.asarray(inp["x_sample"], np.float32)[si][:, 0, :]
    m["x_in"] = x_tiles(xp, xs, cfg)
    sw = np.asarray(inp["state_wkv"], np.float32)[:, si]
    sw = sw.reshape(NA, cfg.stiles, NCH, FC, 2, 64, 64)
    m["st_wkv"] = np.ascontiguousarray(sw.transpose(0, 1, 3, 4, 6, 2, 5)).reshape(NA, cfg.stiles, FC, 128, NCH, 64)
    ss = np.asarray(inp["state_shift"], np.float32)[:, si]
    ss = ss.reshape(NA, cfg.stiles, NCH, KC, 128)
    m["st_shift"] = np.ascontiguousarray(ss.transpose(0, 1, 4, 3, 2))
    return m


def untile(y, cfg):
    yp = y[:cfg.ptiles].transpose(0, 3, 2, 1).reshape(cfg.nseq, cfg.seq, D)
    ys = y[cfg.ptiles:].reshape(cfg.stiles, 128, KC, NCH, C)[:, :, :, :, 0].transpose(0, 3, 2, 1).reshape(cfg.ns, D)
    return yp, ys


def unstate(w, cfg):
    a = w.reshape(NA, cfg.nstate, FC, 2, 64, 64)
    return np.ascontiguousarray(a.transpose(0, 1, 2, 3, 5, 4)).reshape(NA, cfg.nstate, 32, 64, 64)


def unshift(s, cfg):
    return np.ascontiguousarray(s.transpose(0, 1, 3, 2)).reshape(NA, cfg.nstate, D)


def prep_shared_mla(inp, cfg):
    sh = {}
    pb = np.zeros((128, NPAR_B), np.float32)
    pb[:, PB_KVN:PB_KVN + 8] = vec_fm(inp["kv_in_norm_w"], 8)
    for l in range(NB):
        pb[:, PB_BN + l * 8:PB_BN + (l + 1) * 8] = vec_fm(inp["b_norm_w"][l], 8)
        pb[:, PB_QN + l * 3:PB_QN + (l + 1) * 3] = vec_fm(inp["b_q_norm_w"][l], 3)
    pb[:, PB_FN:PB_FN + 8] = vec_fm(inp["final_norm_w"], 8)
    sh["par_b"] = pb
    sh["kvnw_bc"] = np.ascontiguousarray(np.broadcast_to(np.asarray(inp["kv_norm_w"], np.float32)[None, :], (128, KV_LORA)))
    half = QK_ROPE // 2
    inv_freq = (np.float32(10000.0) ** (-np.arange(half, dtype=np.float32) / np.float32(half))).astype(np.float32)
    pos = np.arange(cfg.seq, dtype=np.float32)
    ang = (pos[:, None] * inv_freq[None, :]).astype(np.float32)
    cs, sn = np.cos(ang).astype(np.float32), np.sin(ang).astype(np.float32)
    sh["rope_tm"] = np.ascontiguousarray(np.concatenate([cs, sn], axis=1))
    cf2 = np.concatenate([cs, cs], axis=1).T
    sf2 = np.concatenate([sn, sn], axis=1).T
    sh["rope_fm"] = np.ascontiguousarray(np.stack([cf2, sf2]))
    past = np.float32(cfg.npages * PAGE)
    angs = (past * inv_freq).astype(np.float32)
    cs1, sn1 = np.cos(angs).astype(np.float32), np.sin(angs).astype(np.float32)
    sh["rope_s_tm"] = np.ascontiguousarray(np.broadcast_to(np.concatenate([cs1, sn1])[None, :], (16, 64)))
    sh["rope_s_fm"] = np.ascontiguousarray(np.stack([np.concatenate([cs1, cs1]), np.concatenate([sn1, sn1])], axis=1))
    s_ = np.arange(128)[:, None]
    q_ = np.arange(TT)[None, :]
    cmk = np.concatenate([(s_ <= q_), (128 + s_ <= q_)], axis=1).astype(np.float32)
    sh["cmask"] = np.ascontiguousarray(cmk)
    sh["iota_p"] = np.arange(128, dtype=np.float32).reshape(128, 1)

    def kfm(a, nk):
        a = np.asarray(a, np.float32)
        n = a.shape[-1]
        lead = a.shape[:-2]
        a = a.reshape(lead + (nk, 128, n))
        a = np.moveaxis(a, -3, -2)
        return np.ascontiguousarray(a).reshape(lead + (128, nk * n))
    sh["wdkv_f"] = kfm(inp["w_dkv"], KC)
    wuk = np.asarray(inp["w_uk"], np.float32)
    wuv = np.asarray(inp["w_uv"], np.float32)
    sh["wuk_f"] = np.ascontiguousarray(wuk.reshape(2, 128, BH, 128).transpose(1, 0, 2, 3)).reshape(128, -1)
    sh["wuv_f"] = np.ascontiguousarray(wuv.reshape(2, 128, BH, 128).transpose(1, 0, 2, 3)).reshape(128, -1)
    sh["wukT_f"] = np.ascontiguousarray(wuk.transpose(2, 1, 0)).reshape(128, -1)
    sh["bwin_f"] = kfm(inp["b_w_in"], KC)
    sh["bwuq_f"] = kfm(inp["b_w_uq"], 3)
    sh["bwout_f"] = kfm(inp["b_w_out"], BH)
    sh["cache_ckv"] = np.asarray(inp["cache_ckv"], np.float32).reshape(-1, KV_LORA)
    sh["cache_kpe"] = np.asarray(inp["cache_kpe"], np.float32).reshape(-1, QK_ROPE)
    return sh


def run_all(inp, cfg, ncores, trace=False):
    nc = build(cfg)
    sh = prep_shared(inp)
    sh.update(prep_shared_mla(inp, cfg))
    in_maps = []
    for c in range(ncores):
        pi = list(range(c * cfg.nseq, (c + 1) * cfg.nseq))
        si = list(range(c * cfg.ns, (c + 1) * cfg.ns))
        m = dict(sh)
        m.update(prep_core(inp, cfg, pi, si))
        m["page_table"] = np.ascontiguousarray(np.asarray(inp["page_table"], np.int32)[si])
        in_maps.append(m)
    res = run_bass_kernel_spmd(nc, in_maps, core_ids=list(range(ncores)))
    return assemble([r for r in res.results], cfg, ncores)


def assemble(results, cfg, ncores):
    yp, ys, wkp, wks, shp, shs, ckp, kpp, cks, kps = ([] for _ in range(10))
    for r in results:
        a = r["y_p"].transpose(0, 3, 2, 1).reshape(cfg.nseq, cfg.seq, D)
        yp.append(a)
        ys.append(r["y_s"].transpose(2, 1, 0).reshape(cfg.ns, 1, D))
        w = unstate(r["wkv_out"], cfg)
        wkp.append(w[:, :cfg.nseq])
        wks.append(w[:, cfg.nseq:])
        s = unshift(r["shift_out"], cfg)
        shp.append(s[:, :cfg.nseq])
        shs.append(s[:, cfg.nseq:])
        ckp.append(r["ckv_p"].reshape(cfg.nseq, cfg.seq, KV_LORA))
        kpp.append(r["kpe_p"].reshape(cfg.nseq, cfg.seq, QK_ROPE))
        cks.append(r["ckv_s"].reshape(cfg.ns, 1, KV_LORA))
        kps.append(r["kpe_s"].reshape(cfg.ns, 1, QK_ROPE))
    cat = lambda xs, ax: np.ascontiguousarray(np.concatenate(xs, axis=ax), dtype=np.float32)
    return (cat(yp, 0), cat(ys, 0), cat(wkp, 1), cat(shp, 1), cat(ckp, 0), cat(kpp, 0),
            cat(wks, 1), cat(shs, 1), cat(cks, 0), cat(kps, 0))


def kernel(**inputs):
    inp = {k: np.asarray(v) for k, v in inputs.items()}
    npool = inp["cache_ckv"].shape[0]
    npages = inp["page_table"].shape[1]
    cfg = Cfg(nseq=2, seq=inp["x_prompt"].shape[1], ns=16, npages=npages, npool=npool)
    return run_all(inp, cfg, 8)
```
